# Optimizing a Trainium2 kernel written in Bass

```python
import jax, jax.numpy as jnp
from jax import lax
import numpy as np

D_MODEL = 2048
BATCH = 4
SEQ = 8192
DEPTH = 1
DEC_BATCH = 16
DEC_SEQ = 16
PAST_LEN = 4096

CHUNK = 64
Q_BLOCK = 128
D_A = D_MODEL // 2
D_B = D_MODEL - D_A
HEAD_DIM_A = 128
N_HEADS_A = D_A // HEAD_DIM_A
N_HEADS_B = 4
HEAD_DIM_B = D_B // N_HEADS_B
CONV_W = 4
D_FF = 4 * D_MODEL
D_IN = 3 * D_A + 4 * D_B + 2 * N_HEADS_B
EPS = 1e-6

kernel_name = 'hymba_stickbreak_mlstm_stream_step'


def _rmsnorm(x, g):
    xf = x.astype(jnp.float32)
    y = xf * lax.rsqrt(jnp.mean(xf * xf, axis=-1, keepdims=True) + EPS)
    return (y * g.astype(jnp.float32)).astype(x.dtype)


def _split_cols(u):
    sizes = (D_A, D_A, D_A, D_B, D_B, D_B, D_B, N_HEADS_B, N_HEADS_B)
    return jnp.split(u, np.cumsum(sizes)[:-1].tolist(), axis=-1)


def _sb_block(q, k, v, q_pos, k_pos):
    z = jnp.einsum('bhtd,bhsd->bhts', q, k).astype(jnp.float32) * (HEAD_DIM_A ** -0.5)
    mask = k_pos[None, :] < q_pos[:, None]
    log_beta = jax.nn.log_sigmoid(z)
    log_1m = jnp.where(mask, log_beta - z, 0.0)
    tail = lax.cumsum(log_1m, axis=3, reverse=True) - log_1m
    a = jnp.where(mask, jnp.exp(log_beta + tail), 0.0)
    return jnp.einsum('bhts,bhsd->bhtd', a.astype(v.dtype), v)


def _sb_attention(q, k, v, q_pos, k_pos):
    B, H, T, d = q.shape
    blk = min(Q_BLOCK, T)
    nb = T // blk
    qb = jnp.moveaxis(q.reshape(B, H, nb, blk, d), 2, 0)
    out = lax.map(lambda a: _sb_block(a[0], k, v, a[1], k_pos), (qb, q_pos.reshape(nb, blk)))
    return jnp.moveaxis(out, 0, 2).reshape(B, H, T, d)


def _mlstm_chunk(state, q, k, v, ig, lf):
    C, n, m = state
    L = q.shape[2]
    b = jnp.cumsum(lf, axis=-1)
    causal = jnp.tril(jnp.ones((L, L), dtype=bool))
    dmat = jnp.where(causal, b[..., :, None] - b[..., None, :] + ig[..., None, :], -jnp.inf)
    inter = b + m[..., None]
    m_t = jnp.maximum(inter, jnp.max(dmat, axis=-1))
    w_inter = jnp.exp(inter - m_t)
    s = jnp.exp(dmat - m_t[..., None]) * jnp.einsum('bhtd,bhsd->bhts', q, k)
    num = w_inter[..., None] * jnp.einsum('bhtk,bhkv->bhtv', q, C) + jnp.einsum('bhts,bhsv->bhtv', s, v)
    den = w_inter * jnp.einsum('bhtk,bhk->bht', q, n) + jnp.sum(s, axis=-1)
    h = num / jnp.maximum(jnp.abs(den), jnp.exp(-m_t))[..., None]
    m_new = m_t[..., -1]
    w_state = jnp.exp(b[..., -1:] - b + ig - m_new[..., None])
    decay = jnp.exp(b[..., -1] + m - m_new)
    C_new = decay[..., None, None] * C + jnp.einsum('bhs,bhsk,bhsv->bhkv', w_state, k, v)
    n_new = decay[..., None] * n + jnp.einsum('bhs,bhsk->bhk', w_state, k)
    return h, (C_new, n_new, m_new)


def _mlstm(q, k, v, ig, lf, state):
    B, H, T, _ = q.shape
    L = min(CHUNK, T)
    nc = T // L

    def chunks(t):
        return jnp.moveaxis(t.reshape((B, H, nc, L) + t.shape[3:]), 2, 0)

    def step(carry, xs):
        h, carry = _mlstm_chunk(carry, *xs)
        return carry, h

    state, hs = lax.scan(step, state, (chunks(q), chunks(k), chunks(v), chunks(ig), chunks(lf)))
    return jnp.moveaxis(hs, 0, 2).reshape(B, H, T, -1), state


def _causal_conv(u, prev, w, b):
    T = u.shape[1]
    full = jnp.concatenate([prev.astype(u.dtype), u], axis=1)
    y = sum(full[:, j:j + T] * w[j] for j in range(CONV_W)) + b
    return jax.nn.silu(y), full[:, -(CONV_W - 1):]


def _layer(x, c, past, w_ada, b_ada, g_norm1, w_in, g_q, g_k, w_conv, b_conv, b_i, b_f, g_h,
           w_out, g_norm2, w_ff1, w_ff2):
    B, T, _ = x.shape
    mod = jnp.dot(jax.nn.silu(c), w_ada) + b_ada
    sh1, sc1, gt1, sh2, sc2, gt2 = jnp.split(mod[:, None, :], 6, axis=-1)
    h = _rmsnorm(x, g_norm1) * (1 + sc1) + sh1
    qa, ka, va, qb_raw, kb_raw, vb, ob, ib, fb = _split_cols(jnp.dot(h, w_in))

    qa = _rmsnorm(qa.reshape(B, T, N_HEADS_A, HEAD_DIM_A), g_q).transpose(0, 2, 1, 3)
    ka = _rmsnorm(ka.reshape(B, T, N_HEADS_A, HEAD_DIM_A), g_k).transpose(0, 2, 1, 3)
    va = va.reshape(B, T, N_HEADS_A, HEAD_DIM_A).transpose(0, 2, 1, 3)
    if past is None:
        P = 0
        k_all, v_all = ka, va
        conv_prev = jnp.zeros((B, CONV_W - 1, 2 * D_B), x.dtype)
        state0 = (jnp.zeros((B, N_HEADS_B, HEAD_DIM_B, HEAD_DIM_B), jnp.float32),
                  jnp.zeros((B, N_HEADS_B, HEAD_DIM_B), jnp.float32),
                  jnp.zeros((B, N_HEADS_B), jnp.float32))
    else:
        k_past, v_past, C0, n0, m0, conv_prev = past
        P = k_past.shape[2]
        k_all = jnp.concatenate([k_past.astype(ka.dtype), ka], axis=2)
        v_all = jnp.concatenate([v_past.astype(va.dtype), va], axis=2)
        state0 = (C0.astype(jnp.float32), n0.astype(jnp.float32), m0.astype(jnp.float32))
    q_pos = P + jnp.arange(T, dtype=jnp.int32)
    k_pos = jnp.arange(P + T, dtype=jnp.int32)
    o_a = _sb_attention(qa, k_all, v_all, q_pos, k_pos).transpose(0, 2, 1, 3).reshape(B, T, D_A)

    qk, conv_new = _causal_conv(jnp.concatenate([qb_raw, kb_raw], axis=-1), conv_prev, w_conv, b_conv)
    qb, kb = jnp.split(qk, 2, axis=-1)
    to_h = lambda t: t.reshape(B, T, N_HEADS_B, HEAD_DIM_B).transpose(0, 2, 1, 3).astype(jnp.float32)
    qb, kb, vb4 = to_h(qb), to_h(kb) * (HEAD_DIM_B ** -0.5), to_h(vb)
    ig = (ib + b_i).astype(jnp.float32).transpose(0, 2, 1)
    lf = jax.nn.log_sigmoid((fb + b_f).astype(jnp.float32)).transpose(0, 2, 1)
    hb, (C_new, n_new, m_new) = _mlstm(qb, kb, vb4, ig, lf, state0)
    hb = _rmsnorm(hb, g_h[:, None, :]).transpose(0, 2, 1, 3).reshape(B, T, D_B)
    o_b = hb.astype(x.dtype) * jax.nn.sigmoid(ob)

    y = jnp.dot(jnp.concatenate([o_a, o_b], axis=-1), w_out)
    x = x + gt1 * y
    h2 = _rmsnorm(x, g_norm2) * (1 + sc2) + sh2
    x = x + gt2 * jnp.dot(jnp.square(jax.nn.relu(jnp.dot(h2, w_ff1))), w_ff2)
    return x, (ka, va, C_new, n_new, m_new, conv_new)


def setup_inputs(seed: int = 0) -> dict:
    key = jax.random.key(seed)
    ks = jax.random.split(key, 32)
    nrm = lambda k, shape, s: jax.random.normal(k, shape, jnp.float32) * s
    D = D_MODEL
    return {
        'x_prompt': nrm(ks[0], (BATCH, SEQ, D), 1.0),
        'x_sample': nrm(ks[1], (DEC_BATCH, DEC_SEQ, D), 1.0),
        'c_prompt': nrm(ks[2], (BATCH, D), 1.0),
        'c_sample': nrm(ks[3], (DEC_BATCH, D), 1.0),
        'cache_k': nrm(ks[4], (DEPTH, DEC_BATCH, N_HEADS_A, PAST_LEN, HEAD_DIM_A), 1.0),
        'cache_v': nrm(ks[5], (DEPTH, DEC_BATCH, N_HEADS_A, PAST_LEN, HEAD_DIM_A), 1.0),
        'state_C': nrm(ks[6], (DEPTH, DEC_BATCH, N_HEADS_B, HEAD_DIM_B, HEAD_DIM_B), 0.5),
        'state_n': nrm(ks[7], (DEPTH, DEC_BATCH, N_HEADS_B, HEAD_DIM_B), 0.5),
        'state_m': nrm(ks[8], (DEPTH, DEC_BATCH, N_HEADS_B), 0.5),
        'state_conv': nrm(ks[9], (DEPTH, DEC_BATCH, CONV_W - 1, 2 * D_B), 1.0),
        'w_ada': nrm(ks[10], (DEPTH, D, 6 * D), 0.5 * D ** -0.5),
        'b_ada': nrm(ks[11], (DEPTH, 6 * D), 0.02),
        'g_norm1': 1.0 + nrm(ks[12], (DEPTH, D), 0.02),
        'w_in': nrm(ks[13], (DEPTH, D, D_IN), D ** -0.5),
        'g_q': 1.0 + nrm(ks[14], (DEPTH, HEAD_DIM_A), 0.02),
        'g_k': 1.0 + nrm(ks[15], (DEPTH, HEAD_DIM_A), 0.02),
        'w_conv': nrm(ks[16], (DEPTH, CONV_W, 2 * D_B), CONV_W ** -0.5),
        'b_conv': nrm(ks[17], (DEPTH, 2 * D_B), 0.02),
        'b_i': nrm(ks[18], (DEPTH, N_HEADS_B), 0.1),
        'b_f': jnp.broadcast_to(jnp.linspace(3.0, 6.0, N_HEADS_B), (DEPTH, N_HEADS_B)) + nrm(ks[19], (DEPTH, N_HEADS_B), 0.01),
        'g_h': 1.0 + nrm(ks[20], (DEPTH, N_HEADS_B, HEAD_DIM_B), 0.02),
        'w_out': nrm(ks[21], (DEPTH, D, D), D ** -0.5),
        'g_norm2': 1.0 + nrm(ks[22], (DEPTH, D), 0.02),
        'w_ff1': nrm(ks[23], (DEPTH, D, D_FF), D ** -0.5),
        'w_ff2': nrm(ks[24], (DEPTH, D_FF, D), D_FF ** -0.5),
    }


def _stack(states, i):
    return jnp.stack([s[i] for s in states], axis=0)


def reference(x_prompt, x_sample, c_prompt, c_sample, cache_k, cache_v, state_C, state_n, state_m,
              state_conv, w_ada, b_ada, g_norm1, w_in, g_q, g_k, w_conv, b_conv, b_i, b_f, g_h,
              w_out, g_norm2, w_ff1, w_ff2):
    y_prompt, y_sample = x_prompt, x_sample
    new_p, new_s = [], []
    for l in range(DEPTH):
        wl = (w_ada[l], b_ada[l], g_norm1[l], w_in[l], g_q[l], g_k[l], w_conv[l], b_conv[l],
              b_i[l], b_f[l], g_h[l], w_out[l], g_norm2[l], w_ff1[l], w_ff2[l])
        y_prompt, sp = _layer(y_prompt, c_prompt, None, *wl)
        past = (cache_k[l], cache_v[l], state_C[l], state_n[l], state_m[l], state_conv[l])
        y_sample, ss = _layer(y_sample, c_sample, past, *wl)
        new_p.append(sp)
        new_s.append(ss)
    k_prompt, v_prompt = _stack(new_p, 0), _stack(new_p, 1)
    C_prompt, n_prompt, m_prompt, conv_prompt = _stack(new_p, 2), _stack(new_p, 3), _stack(new_p, 4), _stack(new_p, 5)
    k_sample, v_sample = _stack(new_s, 0), _stack(new_s, 1)
    C_sample, n_sample, m_sample, conv_sample = _stack(new_s, 2), _stack(new_s, 3), _stack(new_s, 4), _stack(new_s, 5)
    return (y_prompt, y_sample, k_prompt, v_prompt, C_prompt, n_prompt, m_prompt, conv_prompt,
            k_sample, v_sample, C_sample, n_sample, m_sample, conv_sample)
```

```python
from contextlib import ExitStack

import numpy as np

import concourse.bass as bass
import concourse.mybir as mybir
from concourse.bass_utils import run_bass_kernel_spmd

F32 = mybir.dt.float32
BF16 = mybir.dt.bfloat16
AF = mybir.ActivationFunctionType
ALU = mybir.AluOpType
AX = mybir.AxisListType

D = 2048
DIN = 7176
DFF = 8192
EPS = 1e-6
NEG = -30000.0
NCG = 15


class StopBuild(Exception):
    pass


class Cfg:
    def __init__(self, nblk=64, past=4096, ts=16, ns=2):
        self.nblk = nblk
        self.past = past
        self.ts = ts
        self.ns = ns
        self.T = nblk * 128
        self.nown = nblk // 2
        self.Town = self.nown * 128
        self.ngrp = nblk // 4


class Eng:
    def __init__(self, k, e, name):
        self.k = k
        self.e = e
        self.name = name
        self.sem = k.es.enter_context(k.nc.semaphore("s_" + name))
        self.count = 0
        self.seen = {}

    def wait(self, tok):
        if tok is None:
            return
        src, v = tok
        if src is self and self.name == "pe":
            return
        if self.seen.get(id(src), 0) >= v:
            return
        self.e.wait_ge(src.sem, v)
        self.seen[id(src)] = v

    def signal(self, ins):
        self.count += 1
        ins.then_inc(self.sem, 1)
        return (self, self.count)


class DmaSem:
    def __init__(self, k, name):
        self.sem = k.es.enter_context(k.nc.semaphore("d_" + name))
        self.count = 0


class Buf:
    def __init__(self, name=""):
        self.name = name
        self.w = None
        self.r = {}

    def note_read(self, tok):
        self.r[id(tok[0])] = tok

    def note_write(self, tok):
        self.w = tok
        self.r = {}


class Builder:
    def __init__(self, nc, es):
        self.nc = nc
        self.es = es
        self.pe = Eng(self, nc.tensor, "pe")
        self.act = Eng(self, nc.scalar, "act")
        self.dve = Eng(self, nc.vector, "dve")
        self.pool = Eng(self, nc.gpsimd, "pool")
        self.sp = Eng(self, nc.sync, "sp")
        self.pe_flush = None
        self.dsems = []
        self.ring = []
        self.ring_i = 0
        self.nt = 0

    def sb(self, stack, name, shape, dt):
        self.nt += 1
        return stack.enter_context(self.nc.sbuf_tensor("%s_%d" % (name, self.nt), list(shape), dt))

    def ps(self, stack, name, shape, dt):
        self.nt += 1
        return stack.enter_context(self.nc.psum_tensor("%s_%d" % (name, self.nt), list(shape), dt))

    def dsem(self, name):
        d = DmaSem(self, name + str(len(self.dsems)))
        self.dsems.append(d)
        return d

    def I(self, eng, fn, R=(), W=(), sig=True):
        for b in R:
            eng.wait(b.w)
        for b in W:
            eng.wait(b.w)
            for t in list(b.r.values()):
                eng.wait(t)
        ins = fn()
        if not hasattr(eng, "pend"):
            eng.pend = ([], [])
        if sig:
            tok = eng.signal(ins)
            if eng.name == "pe" and self.pe_flush is not None:
                self.nc.tensor.ldweights(self.pe_flush)
            for b in list(R) + eng.pend[0]:
                b.note_read(tok)
            for b in list(W) + eng.pend[1]:
                b.note_write(tok)
            eng.pend = ([], [])
            return tok
        eng.pend[0].extend(R)
        eng.pend[1].extend(W)
        return None

    def dma(self, q, out, in_, R=(), W=(), ds=None, **kw):
        for b in R:
            q.wait(b.w)
        for b in W:
            q.wait(b.w)
            for t in list(b.r.values()):
                q.wait(t)
        if ds is None:
            if len(self.ring) < 24:
                ds = self.dsem("r")
                self.ring.append(ds)
            else:
                ds = self.ring[self.ring_i % len(self.ring)]
                self.ring_i += 1
                q.wait((ds, ds.count))
        ins = q.e.dma_start(out=out, in_=in_, **kw)
        ins.then_inc(ds.sem, 16)
        ds.count += 16
        tok = (ds, ds.count)
        for b in R:
            b.note_read(tok)
        for b in W:
            b.note_write(tok)
        return tok

    def barrier(self, sc):
        nc = self.nc
        for d in self.dsems:
            if d.count:
                self.sp.wait((d, d.count))
        t_pe = self.I(self.pe, lambda: nc.tensor.matmul(sc["ps"][0:1, 0:2], sc["one_b"][0:1, 0:1], sc["one_b"][0:1, 0:2], start=True, stop=True), W=[sc["ps_buf"]])
        t_act = self.act.signal(nc.scalar.copy(out=sc["sa"][0:1, 0:1], in_=sc["one_f"][0:1, 0:1]))
        t_dve = self.dve.signal(nc.vector.tensor_copy(out=sc["sd"][0:1, 0:1], in_=sc["one_f"][0:1, 0:1]))
        t_pool = self.pool.signal(nc.gpsimd.tensor_copy(out=sc["sp"][0:1, 0:1], in_=sc["one_f"][0:1, 0:1]))
        for t in (t_pe, t_act, t_dve, t_pool):
            self.sp.wait(t)
        self.sp.count += 1
        nc.sync.sem_inc(self.sp.sem, 1)
        t_sp = (self.sp, self.sp.count)
        for e in (self.pe, self.act, self.dve, self.pool):
            e.wait(t_sp)
        return t_sp


CONST_NAMES = ["ident", "ones", "triinc", "causneg", "sel127", "sel15", "stri01", "trige", "causnegT"]


def make_consts():
    i = np.arange(128)
    c = {}
    c["ident"] = np.eye(128, dtype=np.float32)
    c["ones"] = np.ones((128, 128), np.float32)
    c["triinc"] = (i[:, None] <= i[None, :]).astype(np.float32)
    c["causneg"] = np.where(i[None, :] <= i[:, None], 0.0, NEG).astype(np.float32)
    s = np.zeros((128, 128), np.float32); s[127, :] = 1.0
    c["sel127"] = s
    s = np.zeros((128, 128), np.float32); s[15, :] = 1.0
    c["sel15"] = s
    c["stri01"] = (i[:, None] < i[None, :]).astype(np.float32)
    c["trige"] = (i[:, None] >= i[None, :]).astype(np.float32)
    c["causnegT"] = np.ascontiguousarray(c["causneg"].T)
    return np.concatenate([c[n] for n in CONST_NAMES], axis=1)


def build_program(cfg):
    nc = bass.Bass("TRN2", target_bir_lowering=False)
    try:
        _build_body(nc, cfg)
    except StopBuild:
        pass
    return nc


def _build_body(nc, cfg):
    T, Town, NB, NS, TS, PAST = cfg.T, cfg.Town, cfg.nblk, cfg.ns, cfg.ts, cfg.past
    NTOKS = NS * TS

    def din(name, shape):
        return nc.dram_tensor(name, list(shape), F32, kind="ExternalInput").ap()

    def dout(name, shape):
        return nc.dram_tensor(name, list(shape), F32, kind="ExternalOutput").ap()

    def dscr(name, shape, dt=BF16):
        return nc.dram_tensor(name, list(shape), dt).ap()

    xs = din("xs", (T, D))
    cmd = din("cm", (128, 2))
    cvec = din("cvec", (1 + NS, D))
    xsamp = din("xsamp", (NTOKS, D))
    ck = din("ck", (NS, 8, PAST, 128))
    cv = din("cv", (NS, 8, PAST, 128))
    sC = din("sC", (NS, 4, 256, 256))
    sn = din("sn", (NS, 4, 256))
    sm = din("sm", (NS, 4))
    sconv = din("sconv", (NS, 3, 2048))
    w_ada = din("w_ada", (D, 6 * D))
    b_ada = din("b_ada", (6 * D,))
    g_norm1 = din("g_norm1", (D,))
    w_in = din("w_in", (D, DIN))
    g_q = din("g_q", (128,))
    g_k = din("g_k", (128,))
    w_conv = din("w_conv", (4, 2048))
    b_conv = din("b_conv", (2048,))
    b_i = din("b_i", (4,))
    b_f = din("b_f", (4,))
    g_h = din("g_h", (4, 256))
    w_out = din("w_out", (D, D))
    g_norm2 = din("g_norm2", (D,))
    w_ff1 = din("w_ff1", (D, DFF))
    w_ff2 = din("w_ff2", (DFF, D))
    constd = din("consts", (128, 128 * len(CONST_NAMES)))

    y_o = dout("y", (Town, D))
    k_o = dout("ko", (8, T, 128))
    v_o = dout("vo", (8, T, 128))
    C_o = dout("Co", (4, 256, 256))
    n_o = dout("no", (4, 256))
    m_o = dout("mo", (1, 4))
    conv_o = dout("convo", (3, 2048))
    ys_o = dout("ys", (NTOKS, D))
    ks_o = dout("kso", (NS, 8, TS, 128))
    vs_o = dout("vso", (NS, 8, TS, 128))
    Cs_o = dout("Cso", (NS, 4, 256, 256))
    ns_o = dout("nso", (NS, 4, 256))
    ms_o = dout("mso", (NS, 4))
    convs_o = dout("convso", (NS, 3, 2048))

    winS = dscr("winS", (NCG, 128, 16, 512))
    woutS = dscr("woutS", (4, 128, 16, 512))
    wff1S = dscr("wff1S", (16, 128, 16, 512))
    wff2S = dscr("wff2S", (4, 4, 128, 16, 512))
    KTs = dscr("KTs", (8, 128, T))
    Vss = dscr("Vss", (8, 128, NB, 128))
    QTs = dscr("QTs", (8, 128, Town))
    OTs = dscr("OTs", (16, 128, Town))

    with ExitStack() as es:
        K = Builder(nc, es)
        I, dma = K.I, K.dma
        pe, act, dve, pool, sp = K.pe, K.act, K.dve, K.pool, K.sp
        P = es

        cst = K.sb(P, "cst", (128, 128 * len(CONST_NAMES)), F32)
        b_cst = Buf("cst")
        cF = {n: cst[:, i * 128:(i + 1) * 128] for i, n in enumerate(CONST_NAMES)}
        identB = K.sb(P, "identB", (128, 128), BF16)
        onesB = K.sb(P, "onesB", (128, 128), BF16)
        trigeB = K.sb(P, "trigeB", (128, 128), BF16)
        b_cb = Buf("cstb")
        cmt = K.sb(P, "cmt", (128, 2), F32)
        b_cm = Buf("cm")
        modT = K.sb(P, "modT", (128, 96, 1 + NS), F32)
        A1 = K.sb(P, "A1", (128, 16, 1 + NS), F32)
        A2 = K.sb(P, "A2", (128, 16, 1 + NS), F32)
        b_mod = Buf("mod")
        gqB = K.sb(P, "gqB", (128, 128), F32)
        gkB = K.sb(P, "gkB", (128, 128), F32)
        ghB = K.sb(P, "ghB", (128, 1024), F32)
        biB = K.sb(P, "biB", (128, 4), F32)
        bfB = K.sb(P, "bfB", (128, 4), F32)
        wcT = K.sb(P, "wcT", (128, 16, 4), F32)
        bcT = K.sb(P, "bcT", (128, 16), F32)
        b_small = Buf("small")
        epsT = K.sb(P, "epsT", (128, 1), F32)
        scr = {"one_f": cF["ones"], "one_b": onesB,
               "sa": K.sb(P, "bsa", (1, 2), F32), "sd": K.sb(P, "bsd", (1, 2), F32), "sp": K.sb(P, "bsp", (1, 2), F32)}
        psF = [K.ps(P, "psF%d" % i, (128, 512), F32) for i in range(6)]
        bF = [Buf("psF%d" % i) for i in range(6)]
        psT = [K.ps(P, "psT%d" % i, (128, 1024), BF16) for i in range(2)]
        bT = [Buf("psT%d" % i) for i in range(2)]
        scr["ps"] = psF[0]
        scr["ps_buf"] = bF[0]
        flushw = K.sb(P, "flushw", (128, 2), BF16)
        nc.vector.memset(flushw[:], 0.0)
        K.pe_flush = flushw[:, 0:1]
        import os as _os
        _stop = _os.environ.get("KSTOP", "")

        def ckpt(tag):
            if _stop == tag:
                K.barrier(scr)
                raise StopBuild()

        dma(sp, cst[:], constd, W=[b_cst])
        dma(sp, cmt[:], cmd, W=[b_cm])
        I(dve, lambda: nc.vector.tensor_copy(out=identB[:], in_=cF["ident"]), R=[b_cst], W=[b_cb])
        I(dve, lambda: nc.vector.tensor_copy(out=onesB[:], in_=cF["ones"]), R=[b_cst], W=[b_cb])
        I(dve, lambda: nc.vector.tensor_copy(out=trigeB[:], in_=cF["trige"]), R=[b_cst], W=[b_cb])
        I(dve, lambda: nc.vector.memset(epsT[:], EPS), W=[b_small])
        dma(sp, gqB[:], g_q.unsqueeze(0).partition_broadcast(128).rearrange("p a d -> p (a d)"), W=[b_small])
        dma(sp, gkB[:], g_k.unsqueeze(0).partition_broadcast(128).rearrange("p a d -> p (a d)"), W=[b_small])
        dma(sp, ghB[:], g_h.rearrange("h d -> (h d)").unsqueeze(0).partition_broadcast(128).rearrange("p a d -> p (a d)"), W=[b_small])
        dma(sp, biB[:], b_i.unsqueeze(0).partition_broadcast(128).rearrange("p a d -> p (a d)"), W=[b_small])
        dma(sp, bfB[:], b_f.unsqueeze(0).partition_broadcast(128).rearrange("p a d -> p (a d)"), W=[b_small])

        ckpt("s1")
        b_win = [Buf("win%d" % i) for i in range(NCG)]
        b_wout = [Buf("wout%d" % i) for i in range(4)]
        b_wff1 = [Buf("wff1%d" % i) for i in range(16)]
        b_wff2 = [[Buf("wff2%d_%d" % (i, j)) for j in range(4)] for i in range(4)]
        cvt_sems = [K.dsem("cvt") for _ in range(3)]
        ncv = [0]

        def convert(dst, src, buf):
            ds = cvt_sems[ncv[0] % 3]
            ncv[0] += 1
            pool.wait((ds, ds.count))
            dma(pool, dst, src, W=[buf], ds=ds)

        A_ORDER = [2, 3, 4, 5, 8, 9, 10, 11, 14, 6, 7, 0, 1, 12, 13]
        for cg in A_ORDER:
            ncol = 512 if cg < 14 else 8
            convert(winS[cg, :, :, 0:ncol],
                    w_in[:, cg * 512:cg * 512 + ncol].rearrange("(kc p) n -> p kc n", p=128), b_win[cg])

        ckpt("s2")
        with ExitStack() as S0:
            cT = K.sb(S0, "cT", (1 + NS, D), F32)
            b_cT = Buf()
            dma(sp, cT[:], cvec, W=[b_cT])
            I(act, lambda: nc.scalar.activation(out=cT[:], in_=cT[:], func=AF.Silu), R=[], W=[b_cT])
            scT = K.sb(S0, "scT", (128, 16, 1 + NS), F32)
            b_scT = Buf()
            nr = 1 + NS
            for c in range(16):
                I(pe, lambda c=c: nc.tensor.transpose(psF[0][:, c * nr:(c + 1) * nr], cT[:, c * 128:(c + 1) * 128], cF["ident"][0:nr, 0:nr]),
                  R=[b_cT, b_cst], W=[bF[0]], sig=(c == 15))
            I(dve, lambda: nc.vector.tensor_copy(out=scT[:].rearrange("p a b -> p (a b)"), in_=psF[0][:, 0:16 * nr]), R=[bF[0]], W=[b_scT])
            ckpt("s3")
            rowsT = K.sb(S0, "rowsT", (128, 128), F32)
            b_rows = Buf()
            dma(sp, rowsT[0:96, :], b_ada.rearrange("(c p) -> c p", p=128), W=[b_rows])
            dma(sp, rowsT[96:112, :], g_norm1.rearrange("(c p) -> c p", p=128), W=[b_rows])
            dma(sp, rowsT[112:128, :], g_norm2.rearrange("(c p) -> c p", p=128), W=[b_rows])
            rows2 = K.sb(S0, "rows2", (128, 128), F32)
            I(dve, lambda: nc.vector.memset(rows2[:], 0.0), W=[b_rows])
            dma(sp, rows2[0:64, :], w_conv.rearrange("j (c p) -> (j c) p", p=128), W=[b_rows])
            dma(sp, rows2[64:80, :], b_conv.rearrange("(c p) -> c p", p=128), W=[b_rows])
            colsT = K.sb(S0, "colsT", (128, 256), F32)
            b_cols = Buf()
            I(pe, lambda: nc.tensor.transpose(psF[1][:, 0:128], rowsT[:], cF["ident"]), R=[b_rows, b_cst], W=[bF[1]], sig=False)
            I(pe, lambda: nc.tensor.transpose(psF[1][:, 128:256], rows2[:], cF["ident"]), R=[b_rows, b_cst], W=[bF[1]])
            I(dve, lambda: nc.vector.tensor_copy(out=colsT[:], in_=psF[1][:, 0:256]), R=[bF[1]], W=[b_cols])
            I(dve, lambda: nc.vector.tensor_copy(out=wcT[:], in_=colsT[:, 128:192].rearrange("p (j c) -> p c j", j=4)), R=[b_cols], W=[b_small])
            I(dve, lambda: nc.vector.tensor_copy(out=bcT[:], in_=colsT[:, 192:208]), R=[b_cols], W=[b_small])
            ckpt("s4")
            wpan = [K.sb(S0, "wpan%d" % i, (128, 16, 512), F32) for i in range(2)]
            b_wpan = [Buf(), Buf()]
            for jg in range(24):
                s = jg % 2
                dma(sp, wpan[s][:], w_ada[:, jg * 512:(jg + 1) * 512].rearrange("(kc p) n -> p kc n", p=128), W=[b_wpan[s]])
                for jj in range(4):
                    j = jg * 4 + jj
                    for kc in range(16):
                        I(pe, lambda j=j, jj=jj, s=s, kc=kc: nc.tensor.matmul(psF[2][:, j * nr:(j + 1) * nr], wpan[s][:, kc, jj * 128:(jj + 1) * 128],
                                                                       scT[:, kc, :], start=(kc == 0), stop=(kc == 15)),
                          R=[b_wpan[s], b_scT], W=[bF[2]], sig=(kc == 15 and jj == 3))
            ckpt("s5")
            I(dve, lambda: nc.vector.tensor_tensor(out=modT[:], in0=psF[2][:, 0:96 * nr].rearrange("p (a b) -> p a b", b=nr),
                                                   in1=colsT[:, 0:96].unsqueeze(2).broadcast_to([128, 96, nr]), op=ALU.add),
              R=[bF[2], b_cols], W=[b_mod])
            ckpt("s6")
            I(dve, lambda: nc.vector.scalar_tensor_tensor(out=A1[:], in0=modT[:, 16:32, :], scalar=1.0,
                                                          in1=colsT[:, 96:112].unsqueeze(2).broadcast_to([128, 16, nr]), op0=ALU.add, op1=ALU.mult),
              R=[b_mod, b_cols], W=[b_mod])
            I(dve, lambda: nc.vector.scalar_tensor_tensor(out=A2[:], in0=modT[:, 64:80, :], scalar=1.0,
                                                          in1=colsT[:, 112:128].unsqueeze(2).broadcast_to([128, 16, nr]), op0=ALU.add, op1=ALU.mult),
              R=[b_mod, b_cols], W=[b_mod])
            K.barrier(scr)

        _dgc = {}

        def row_bcast(stack, dst, srcT, dstbuf):
            if id(stack) not in _dgc:
                _dgc[id(stack)] = (K.sb(stack, "dg", (128, 16, 128), F32), Buf())
            dg, b_dg = _dgc[id(stack)]
            I(dve, lambda: nc.vector.tensor_tensor(out=dg[:], in0=cF["ident"].unsqueeze(1).broadcast_to([128, 16, 128]),
                                                   in1=srcT.unsqueeze(2).broadcast_to([128, 16, 128]), op=ALU.mult),
              R=[b_cst, b_mod], W=[b_dg])
            for q in range(4):
                I(pe, lambda q=q: nc.tensor.matmul(psF[q][:], cF["ones"], dg[:, 4 * q:4 * q + 4, :].rearrange("p a b -> p (a b)"), start=True, stop=True),
                  R=[b_dg, b_cst], W=[bF[q]])
                I(dve, lambda q=q: nc.vector.tensor_copy(out=dst[:, q * 512:(q + 1) * 512], in_=psF[q][:]), R=[bF[q]], W=[dstbuf])

        ckpt("s8")
        for cg in range(4):
            convert(woutS[cg], w_out[:, cg * 512:(cg + 1) * 512].rearrange("(kc p) n -> p kc n", p=128), b_wout[cg])
        for cg in range(16):
            convert(wff1S[cg], w_ff1[:, cg * 512:(cg + 1) * 512].rearrange("(kc p) n -> p kc n", p=128), b_wff1[cg])
        for cg in range(4):
            for kq in range(4):
                convert(wff2S[cg, kq], w_ff2[kq * 2048:(kq + 1) * 2048, cg * 512:(cg + 1) * 512].rearrange("(kc p) n -> p kc n", p=128), b_wff2[cg][kq])

        SCALE_A = 128.0 ** -0.5
        rr = [0]

        def next_bank():
            i = 3 + rr[0] % 3
            rr[0] += 1
            return i

        def mk_mlstm(stack, L):
            M = type("M", (), {})()

            def t(name, shape, dt=F32):
                setattr(M, name, K.sb(stack, "m_" + name, shape, dt))
                setattr(M, "b_" + name, Buf(name))
            for nm in ["fz", "e4", "sp4", "ig", "g", "nb", "cmx", "na", "wi", "wia", "den", "r", "emt", "ssq", "rsd", "fac", "wst", "dsub"]:
                t(nm, (L, 4))
            t("am", (L, 8)); t("dn", (L, 8))
            t("dg", (L, 4, L)); t("Gm", (L, 4, L)); t("ET", (L, 4, L)); t("PT", (L, 4, L), BF16)
            t("cn4", (L, 4, L))
            t("qc", (L, 2, 256)); t("num", (L, 4, 256)); t("kTW", (L, 8, 128), BF16); t("junk", (L, 256))
            t("aLs", (128, 8)); t("dec", (128, 4)); t("dsb", (128, 4))
            I(dve, lambda: nc.vector.tensor_copy(out=M.cn4[:], in_=cF["causnegT"][0:L, 0:L].unsqueeze(1).broadcast_to([L, 4, L])), R=[b_cst], W=[M.b_cn4])
            M.L = L
            return M

        def mk_state(stack):
            St = type("St", (), {})()
            St.C32 = K.sb(stack, "C32", (128, 2, 4, 256), F32); St.b_C32 = Buf()
            St.Cbf = K.sb(stack, "Cbf", (128, 2, 4, 256), BF16); St.b_Cbf = Buf()
            St.n32 = K.sb(stack, "n32", (128, 2, 4), F32); St.b_n32 = Buf()
            St.nbf = K.sb(stack, "nbf", (128, 2, 4), BF16); St.b_nbf = Buf()
            St.mB = K.sb(stack, "mB", (128, 4), F32); St.b_m = Buf()
            return St

        def state_refresh_bf(St):
            I(act, lambda: nc.scalar.copy(out=St.Cbf[:].rearrange("p a h v -> p (a h v)"), in_=St.C32[:].rearrange("p a h v -> p (a h v)")), R=[St.b_C32], W=[St.b_Cbf])
            I(dve, lambda: nc.vector.tensor_copy(out=St.nbf[:].rearrange("p a h -> p (a h)"), in_=St.n32[:].rearrange("p a h -> p (a h)")), R=[St.b_n32], W=[St.b_nbf])

        def mlstm_block(M, St, own, qT, b_qT, kT, b_kT, vB, b_vB, gt, b_gt, sgg, b_sgg, ob, b_ob, dummy, sel):
            L = M.L
            idL = cF["ident"][0:L, 0:L]
            onL = cF["ones"][0:L, 0:L]
            b3 = lambda ap: ap.unsqueeze(2).broadcast_to([L, 4, L])
            m3 = lambda ap: ap.unsqueeze(1).broadcast_to([L, 4, L])
            p3 = lambda bank: psF[bank][0:L, 0:4 * L].rearrange("p (h s) -> p h s", h=4)
            I(dve, lambda: nc.vector.tensor_tensor(out=M.fz[:], in0=gt[:, 4:8], in1=bfB[0:L, :], op=ALU.add), R=[b_gt, b_small], W=[M.b_fz])
            I(act, lambda: nc.scalar.activation(out=M.e4[:], in_=M.fz[:], func=AF.Exp, scale=-1.0), R=[M.b_fz], W=[M.b_e4])
            I(act, lambda: nc.scalar.activation(out=M.sp4[:], in_=M.e4[:], func=AF.Ln, bias=1.0), R=[M.b_e4], W=[M.b_sp4])
            I(dve, lambda: nc.vector.tensor_tensor(out=M.ig[:], in0=gt[:, 0:4], in1=biB[0:L, :], op=ALU.add), R=[b_gt, b_small], W=[M.b_ig])
            if dummy:
                I(dve, lambda: nc.vector.tensor_scalar(out=M.ig[:], in0=M.ig[:], scalar1=cmt[0:L, 1:2], scalar2=None, op0=ALU.add), R=[b_cm], W=[M.b_ig])
                I(dve, lambda: nc.vector.tensor_scalar(out=M.sp4[:], in0=M.sp4[:], scalar1=cmt[0:L, 0:1], scalar2=None, op0=ALU.mult), R=[b_cm], W=[M.b_sp4])
            yield None
            I(pe, lambda: nc.tensor.matmul(psF[0][0:L, 0:4], cF["triinc"][0:L, 0:L], M.sp4[:], start=True, stop=True), R=[M.b_sp4, b_cst], W=[bF[0]])
            I(dve, lambda: nc.vector.tensor_tensor(out=M.g[:], in0=M.ig[:], in1=psF[0][0:L, 0:4], op=ALU.add), R=[M.b_ig], W=[M.b_g, bF[0]])
            I(dve, lambda: nc.vector.tensor_copy(out=M.nb[:], in_=psF[0][0:L, 0:4]), W=[M.b_nb, bF[0]])
            yield None
            I(dve, lambda: nc.vector.tensor_tensor(out=M.dg[:], in0=m3(idL), in1=b3(M.g[:]), op=ALU.mult), R=[M.b_g, b_cst], W=[M.b_dg])
            I(pe, lambda: nc.tensor.matmul(psF[1][0:L, 0:4 * L], onL, M.dg[:].rearrange("p h s -> p (h s)"), start=True, stop=True), R=[M.b_dg, b_cst], W=[bF[1]])
            I(dve, lambda: nc.vector.tensor_tensor(out=M.Gm[:], in0=p3(1), in1=m3(cF["causneg"][0:L, 0:L]), op=ALU.add), R=[b_cst], W=[M.b_Gm, bF[1]])
            I(dve, lambda: nc.vector.tensor_reduce(out=M.cmx[:], in_=M.Gm[:], axis=AX.X, op=ALU.max), R=[M.b_Gm], W=[M.b_cmx])
            I(dve, lambda: nc.vector.tensor_tensor(out=M.am[:, 0:4], in0=M.cmx[:], in1=St.mB[0:L, :], op=ALU.max), R=[M.b_cmx, St.b_m], W=[M.b_am])
            I(dve, lambda: nc.vector.tensor_tensor(out=M.am[:, 4:8], in0=M.am[:, 0:4], in1=M.nb[:], op=ALU.subtract), R=[M.b_nb], W=[M.b_am])
            I(pe, lambda: nc.tensor.matmul(psF[0][:, 8:16], cF[sel][0:L, :], M.am[:], start=True, stop=True), R=[M.b_am, b_cst], W=[bF[0]])
            I(dve, lambda: nc.vector.tensor_copy(out=M.aLs[:], in_=psF[0][:, 8:16]), W=[M.b_aLs, bF[0]])
            yield None
            if own:
                I(dve, lambda: nc.vector.tensor_scalar(out=M.na[:], in0=M.am[:, 0:4], scalar1=-1.0, scalar2=None, op0=ALU.mult), R=[M.b_am], W=[M.b_na])
                I(dve, lambda: nc.vector.tensor_tensor(out=M.dg[:], in0=m3(idL), in1=b3(M.na[:]), op=ALU.mult), R=[M.b_na, b_cst], W=[M.b_dg])
                I(pe, lambda: nc.tensor.matmul(psF[1][0:L, 0:4 * L], onL, M.dg[:].rearrange("p h s -> p (h s)"), start=True, stop=False), R=[M.b_dg, b_cst], W=[bF[1]], sig=False)
                I(pe, lambda: nc.tensor.matmul(psF[1][0:L, 0:4 * L], idL, M.cn4[:].rearrange("p h s -> p (h s)"), start=False, stop=True), R=[M.b_cn4, b_cst], W=[bF[1]])
                for h in range(4):
                    I(act, lambda h=h: nc.scalar.activation(out=M.ET[:, h, :], in_=psF[1][0:L, h * L:(h + 1) * L], func=AF.Exp, bias=M.g[:, h:h + 1], scale=1.0),
                      R=[M.b_g], W=[M.b_ET, bF[1]])
                yield None
                for h in range(4):
                    for half in range(2):
                        I(pe, lambda h=h, half=half: nc.tensor.matmul(psF[2][0:L, h * L:(h + 1) * L], kT(h, half), qT(h, half), start=(half == 0), stop=(half == 1)),
                          R=[b_kT, b_qT], W=[bF[2]], sig=(h == 3 and half == 1))
                I(dve, lambda: nc.vector.scalar_tensor_tensor(out=M.PT[:], in0=p3(2), scalar=1.0 / 16, in1=M.ET[:], op0=ALU.mult, op1=ALU.mult), R=[M.b_ET], W=[M.b_PT, bF[2]])
                yield None
                I(dve, lambda: nc.vector.tensor_tensor(out=M.wia[:], in0=St.mB[0:L, :], in1=M.am[:, 0:4], op=ALU.subtract), R=[St.b_m, M.b_am], W=[M.b_wia])
                I(act, lambda: nc.scalar.activation(out=M.wi[:], in_=M.wia[:], func=AF.Exp), R=[M.b_wia], W=[M.b_wi])
                for h in range(4):
                    I(pe, lambda h=h: nc.tensor.matmul(psF[0][0:L, 16 + h:17 + h], M.PT[:, h, :], onesB[0:L, 0:1], start=True, stop=True), R=[M.b_PT, b_cb], W=[bF[0]], sig=False)
                for h in range(4):
                    for half in range(2):
                        I(pe, lambda h=h, half=half: nc.tensor.matmul(psF[0][0:L, 20 + h:21 + h], qT(h, half), St.nbf[:, half, h:h + 1], start=(half == 0), stop=(half == 1)),
                          R=[b_qT, St.b_nbf], W=[bF[0]], sig=(h == 3 and half == 1))
                I(dve, lambda: nc.vector.tensor_copy(out=M.dn[:], in_=psF[0][0:L, 16:24]), W=[M.b_dn, bF[0]])
                I(dve, lambda: nc.vector.tensor_tensor(out=M.den[:], in0=M.dn[:, 4:8], in1=M.wi[:], op=ALU.mult), R=[M.b_dn, M.b_wi], W=[M.b_den])
                I(dve, lambda: nc.vector.tensor_tensor(out=M.den[:], in0=M.den[:], in1=M.dn[:, 0:4], op=ALU.add), R=[M.b_dn], W=[M.b_den])
                yield None
                I(act, lambda: nc.scalar.activation(out=M.emt[:], in_=M.am[:, 4:8], func=AF.Exp, scale=-1.0), R=[M.b_am], W=[M.b_emt])
                I(dve, lambda: nc.vector.tensor_scalar(out=M.dsub[:], in0=M.den[:], scalar1=-1.0, scalar2=None, op0=ALU.mult), R=[M.b_den], W=[M.b_dsub])
                I(dve, lambda: nc.vector.tensor_tensor(out=M.den[:], in0=M.den[:], in1=M.dsub[:], op=ALU.max), R=[M.b_dsub], W=[M.b_den])
                I(dve, lambda: nc.vector.tensor_tensor(out=M.den[:], in0=M.den[:], in1=M.emt[:], op=ALU.max), R=[M.b_emt], W=[M.b_den])
                I(dve, lambda: nc.vector.reciprocal(out=M.r[:], in_=M.den[:]), R=[M.b_den], W=[M.b_r])
                yield None
                for hp in range(2):
                    for hh in range(2):
                        h = 2 * hp + hh
                        I(pe, lambda h=h, hh=hh: nc.tensor.matmul(psF[1][0:L, hh * 256:(hh + 1) * 256], M.PT[:, h, :], vB[:, h, :], start=True, stop=True),
                          R=[M.b_PT, b_vB], W=[bF[1]], sig=(hh == 1))
                    for hh in range(2):
                        h = 2 * hp + hh
                        for half in range(2):
                            I(pe, lambda h=h, hh=hh, half=half: nc.tensor.matmul(psF[2][0:L, hh * 256:(hh + 1) * 256], qT(h, half), St.Cbf[:, half, h, :], start=(half == 0), stop=(half == 1)),
                              R=[b_qT, St.b_Cbf], W=[bF[2]], sig=(hh == 1 and half == 1))
                    for hh in range(2):
                        h = 2 * hp + hh
                        I(act, lambda h=h, hh=hh: nc.scalar.activation(out=M.qc[:, hh, :], in_=psF[2][0:L, hh * 256:(hh + 1) * 256], func=AF.Copy, scale=M.wi[:, h:h + 1]),
                          R=[M.b_wi], W=[M.b_qc, bF[2]])
                    I(dve, lambda hp=hp: nc.vector.tensor_tensor(out=M.num[:, 2 * hp:2 * hp + 2, :], in0=psF[1][0:L, 0:512].rearrange("p (a v) -> p a v", a=2), in1=M.qc[:], op=ALU.add),
                      R=[M.b_qc], W=[M.b_num, bF[1]])
                yield None
                for h in range(4):
                    I(act, lambda h=h: nc.scalar.activation(out=M.junk[:], in_=M.num[:, h, :], func=AF.Square, scale=M.r[:, h:h + 1], accum_out=M.ssq[:, h:h + 1]),
                      R=[M.b_num, M.b_r], W=[M.b_junk, M.b_ssq])
                I(act, lambda: nc.scalar.activation(out=M.rsd[:], in_=M.ssq[:], func=AF.Ln, scale=1.0 / 256, bias=epsT[0:L, :]), R=[M.b_ssq, b_small], W=[M.b_rsd])
                I(act, lambda: nc.scalar.activation(out=M.rsd[:], in_=M.rsd[:], func=AF.Exp, scale=-0.5), W=[M.b_rsd])
                I(dve, lambda: nc.vector.tensor_tensor(out=M.fac[:], in0=M.r[:], in1=M.rsd[:], op=ALU.mult), R=[M.b_r, M.b_rsd], W=[M.b_fac])
                for h in range(4):
                    I(dve, lambda h=h: nc.vector.scalar_tensor_tensor(out=ob[:, h * 256:(h + 1) * 256], in0=M.num[:, h, :], scalar=M.fac[:, h:h + 1], in1=sgg[:, h * 256:(h + 1) * 256], op0=ALU.mult, op1=ALU.mult),
                      R=[M.b_num, M.b_fac, b_sgg], W=[b_ob])
            yield None
            I(dve, lambda: nc.vector.tensor_tensor(out=M.dsub[:], in0=M.g[:], in1=M.aLs[0:L, 0:4], op=ALU.subtract), R=[M.b_g, M.b_aLs], W=[M.b_dsub])
            I(act, lambda: nc.scalar.activation(out=M.wst[:], in_=M.dsub[:], func=AF.Exp), R=[M.b_dsub], W=[M.b_wst])
            I(dve, lambda: nc.vector.tensor_tensor(out=M.dsb[:], in0=St.mB[:], in1=M.aLs[:, 0:4], op=ALU.subtract), R=[St.b_m, M.b_aLs], W=[M.b_dsb])
            I(act, lambda: nc.scalar.activation(out=M.dec[:], in_=M.dsb[:], func=AF.Exp), R=[M.b_dsb], W=[M.b_dec])
            for h in range(4):
                for half in range(2):
                    idx = 2 * h + half
                    I(pe, lambda h=h, half=half, idx=idx: nc.tensor.transpose(psT[1][0:L, idx * 128:(idx + 1) * 128], kT(h, half), identB[:]),
                      R=[b_kT, b_cb], W=[bT[1]], sig=(idx == 7))
            for h in range(4):
                I(dve, lambda h=h: nc.vector.tensor_scalar(out=M.kTW[:, 2 * h:2 * h + 2, :], in0=psT[1][0:L, 2 * h * 128:(2 * h + 2) * 128].rearrange("p (a d) -> p a d", a=2),
                                                           scalar1=M.wst[:, h:h + 1], scalar2=1.0 / 16, op0=ALU.mult, op1=ALU.mult),
                  R=[M.b_wst], W=[M.b_kTW, bT[1]])
            yield None
            for half in range(2):
                for hp in range(2):
                    bk = 1 + hp
                    for hh in range(2):
                        h = 2 * hp + hh
                        I(pe, lambda h=h, hh=hh, half=half, bk=bk: nc.tensor.matmul(psF[bk][:, hh * 256:(hh + 1) * 256], M.kTW[:, 2 * h + half, :], vB[:, h, :], start=True, stop=True),
                          R=[M.b_kTW, b_vB], W=[bF[bk]], sig=(hh == 1))
                    for hh in range(2):
                        h = 2 * hp + hh
                        I(dve, lambda h=h, hh=hh, half=half, bk=bk: nc.vector.scalar_tensor_tensor(out=St.C32[:, half, h, :], in0=St.C32[:, half, h, :], scalar=M.dec[:, h:h + 1],
                                                                                               in1=psF[bk][:, hh * 256:(hh + 1) * 256], op0=ALU.mult, op1=ALU.add),
                          R=[M.b_dec], W=[St.b_C32, bF[bk]])
            yield None
            for half in range(2):
                for h in range(4):
                    c0 = 24 + half * 4 + h
                    I(pe, lambda h=h, half=half, c0=c0: nc.tensor.matmul(psF[0][:, c0:c0 + 1], M.kTW[:, 2 * h + half, :], onesB[0:L, 0:1], start=True, stop=True),
                      R=[M.b_kTW, b_cb], W=[bF[0]], sig=(half == 1 and h == 3))
            for half in range(2):
                I(dve, lambda half=half: nc.vector.tensor_tensor(out=St.n32[:, half, :], in0=St.n32[:, half, :], in1=M.dec[:], op=ALU.mult), R=[M.b_dec], W=[St.b_n32])
            I(dve, lambda: nc.vector.tensor_tensor(out=St.n32[:].rearrange("p a h -> p (a h)"), in0=St.n32[:].rearrange("p a h -> p (a h)"), in1=psF[0][:, 24:32], op=ALU.add),
              W=[St.b_n32, bF[0]])
            state_refresh_bf(St)
            I(dve, lambda: nc.vector.tensor_copy(out=St.mB[:], in_=M.aLs[:, 4:8]), R=[M.b_aLs], W=[St.b_m])

        def rms_rstd(ssq_ap, out_ap, n, scale, b_in, b_out):
            I(act, lambda: nc.scalar.activation(out=out_ap, in_=ssq_ap, func=AF.Ln, scale=scale, bias=epsT[0:n, :]), R=[b_in, b_small], W=[b_out])
            I(act, lambda: nc.scalar.activation(out=out_ap, in_=out_ap, func=AF.Exp, scale=-0.5), W=[b_out])

        def qknorm(n, bank, gB, out_h, junk, b_junk, ssq4, rs4, b_s4, b_out):
            for h in range(4):
                I(act, lambda h=h: nc.scalar.activation(out=junk[0:n, 0:128], in_=psF[bank][0:n, h * 128:(h + 1) * 128], func=AF.Square, accum_out=ssq4[0:n, h:h + 1]),
                  W=[b_junk, b_s4, bF[bank]])
            rms_rstd(ssq4[0:n, :], rs4[0:n, :], n, 1.0 / 128, b_s4, b_s4)
            for h in range(4):
                I(dve, lambda h=h: nc.vector.scalar_tensor_tensor(out=out_h(h), in0=psF[bank][0:n, h * 128:(h + 1) * 128], scalar=rs4[0:n, h:h + 1], in1=gB[0:n, :], op0=ALU.mult, op1=ALU.mult),
                  R=[b_s4, b_small], W=[b_out, bF[bank]])

        def norm_transpose(n, x_ap, b_x, xn, b_xn, ssx, rsx, b_sx, hT_dst, b_hT, Amod, Bmod, ts, valid=False):
            I(act, lambda: nc.scalar.activation(out=xn[0:n, :], in_=x_ap, func=AF.Square, accum_out=ssx[0:n, :]), R=[b_x], W=[b_xn, b_sx])
            rms_rstd(ssx[0:n, :], rsx[0:n, :], n, 1.0 / D, b_sx, b_sx)
            I(dve, lambda: nc.vector.tensor_scalar(out=xn[0:n, :], in0=x_ap, scalar1=rsx[0:n, 0:1], scalar2=None, op0=ALU.mult), R=[b_x, b_sx], W=[b_xn])
            for rnd in range(2):
                for c in range(8 * rnd, 8 * rnd + 8):
                    I(pe, lambda c=c: nc.tensor.transpose(psT[0][:, (c % 8) * 128:(c % 8) * 128 + n], xn[0:n, c * 128:(c + 1) * 128], identB[0:n, 0:n]),
                      R=[b_xn, b_cb], W=[bT[0]], sig=(c % 8 == 7))
                for c in range(8 * rnd, 8 * rnd + 8):
                    src = psT[0][:, (c % 8) * 128:(c % 8) * 128 + n]
                    if c % 2 == 0:
                        I(act, lambda c=c, src=src: nc.scalar.activation(out=hT_dst(c), in_=src, func=AF.Identity, scale=Amod[:, c, ts:ts + 1], bias=Bmod[:, c, ts:ts + 1]),
                          R=[b_mod], W=[b_hT, bT[0]])
                    else:
                        I(dve, lambda c=c, src=src: nc.vector.tensor_scalar(out=hT_dst(c), in0=src, scalar1=Amod[:, c, ts:ts + 1], scalar2=Bmod[:, c, ts:ts + 1], op0=ALU.mult, op1=ALU.add),
                          R=[b_mod], W=[b_hT, bT[0]])

        def load_w(tile, b_tile, src, b_src):
            return dma(sp, tile[:], src, R=[b_src], W=[b_tile])

        def stage_A():
            with ExitStack() as SA:
                xin = K.sb(SA, "xin", (128, D), F32); b_xin = Buf()
                xnb = K.sb(SA, "xnb", (128, D), BF16); b_xnb = Buf()
                ssx = K.sb(SA, "ssx", (128, 1), F32); rsx = K.sb(SA, "rsx", (128, 1), F32); b_sx = Buf()
                hT = K.sb(SA, "hT", (128, 16, 512), BF16); b_hT = Buf()
                wt = [K.sb(SA, "wt%d" % i, (128, 16, 512), BF16) for i in range(2)]; b_wt = [Buf(), Buf()]
                kst = K.sb(SA, "kst", (128, 4, 128), F32); b_kst = Buf()
                vst = K.sb(SA, "vst", (128, 4, 128), F32); b_vst = Buf()
                knb = K.sb(SA, "knb", (128, 512), BF16); b_knb = Buf()
                junkq = K.sb(SA, "junkq", (128, 128), F32); b_junkq = Buf()
                ssq4 = K.sb(SA, "ssq4", (128, 4), F32); rs4 = K.sb(SA, "rs4", (128, 4), F32); b_s4 = Buf()
                kTst = K.sb(SA, "kTst", (128, 8, 512), BF16); b_kTst = Buf()
                vbst = K.sb(SA, "vbst", (128, 8, 4, 128), BF16); b_vbst = Buf()
                raw = K.sb(SA, "raw", (128, 16, 515), BF16); b_raw = Buf()
                rawl = K.sb(SA, "rawl", (128, 16, 3), F32); b_rawl = Buf()
                cvy = [K.sb(SA, "cvy%d" % i, (128, 512), F32) for i in range(2)]; b_cvy = [Buf(), Buf()]
                qTt2 = [K.sb(SA, "qTt%d" % i, (128, 8, 2, 128), BF16) for i in range(2)]; b_qTt2 = [Buf(), Buf()]
                kTt2 = [K.sb(SA, "kTt%d" % i, (128, 8, 512), BF16) for i in range(2)]; b_kTt2 = [Buf(), Buf()]
                vBt2 = [K.sb(SA, "vBt%d" % i, (128, 4, 4, 256), BF16) for i in range(2)]; b_vBt2 = [Buf(), Buf()]
                gts2 = [K.sb(SA, "gts%d" % i, (128, 4, 8), F32) for i in range(2)]; b_gts2 = [Buf(), Buf()]
                sgg2 = [K.sb(SA, "sgg%d" % i, (128, 2, 1024), BF16) for i in range(2)]; b_sgg2 = [Buf(), Buf()]
                sg32 = K.sb(SA, "sg32", (128, 512), F32); b_sg32 = Buf()
                obt = K.sb(SA, "obt", (128, 1024), BF16); b_obt = Buf()
                qTst = K.sb(SA, "qTst", (128, 8, 256), BF16); b_qTst = Buf()
                oTst = K.sb(SA, "oTst", (128, 8, 256), BF16); b_oTst = Buf()
                M = mk_mlstm(SA, 128)
                St = mk_state(SA)
                for tl, bb in ((St.C32, St.b_C32), (St.n32, St.b_n32), (St.mB, St.b_m)):
                    I(dve, lambda tl=tl: nc.vector.memset(tl[:], 0.0), W=[bb])
                state_refresh_bf(St)
                I(dve, lambda: nc.vector.memset(raw[:, :, 0:3], 0.0), W=[b_raw])

                wsl = [0]

                def genA(G):
                    for j in range(4):
                        blk = 4 * G + j
                        dma(sp, xin[:], xs[blk * 128:(blk + 1) * 128, :], W=[b_xin])
                        yield None
                        norm_transpose(128, xin[:], b_xin, xnb, b_xnb, ssx, rsx, b_sx,
                                       lambda c, j=j: hT[:, c, j * 128:(j + 1) * 128], b_hT, A1, modT, 0)
                        if blk == 0:
                            I(dve, lambda: nc.vector.tensor_scalar(out=hT[:, :, 0:128], in0=hT[:, :, 0:128], scalar1=cmt[:, 0:1], scalar2=None, op0=ALU.mult), R=[b_cm], W=[b_hT])
                    for cg in A_ORDER:
                        s = wsl[0] % 2
                        wsl[0] += 1
                        ncol = 512 if cg < 14 else 8
                        dma(sp, wt[s][:, :, 0:ncol], winS[cg, :, :, 0:ncol], R=[b_win[cg]], W=[b_wt[s]])
                        if cg in (6, 7, 8, 9):
                            yield 'need_raw'
                            for cc in range(4):
                                yield None
                                ci = (cg - 6) * 4 + cc
                                bk = next_bank()
                                for kc in range(16):
                                    I(pe, lambda kc=kc, cc=cc, s=s, bk=bk: nc.tensor.matmul(psF[bk][:], wt[s][:, kc, cc * 128:(cc + 1) * 128], hT[:, kc, :], start=(kc == 0), stop=(kc == 15)),
                                      R=[b_wt[s], b_hT], W=[bF[bk]], sig=(kc == 15))
                                I(act, lambda ci=ci, bk=bk: nc.scalar.copy(out=raw[:, ci, 3:515], in_=psF[bk][:]), W=[b_raw, bF[bk]])
                                if G == cfg.ngrp - 1:
                                    I(dve, lambda ci=ci, bk=bk: nc.vector.tensor_copy(out=rawl[:, ci, :], in_=psF[bk][:, 509:512]), W=[b_rawl, bF[bk]])
                            continue
                        blocks = (1, 3) if cg in (0, 1, 12, 13) else (0, 1, 2, 3)
                        for j in blocks:
                            yield None
                            blk = 4 * G + j
                            bk = next_bank()
                            for kc in range(16):
                                I(pe, lambda kc=kc, j=j, s=s, bk=bk, ncol=ncol: nc.tensor.matmul(psF[bk][:, 0:ncol], hT[:, kc, j * 128:(j + 1) * 128], wt[s][:, kc, 0:ncol], start=(kc == 0), stop=(kc == 15)),
                                  R=[b_wt[s], b_hT], W=[bF[bk]], sig=(kc == 15))
                            if cg in (2, 3):
                                h0 = (cg - 2) * 4
                                qknorm(128, bk, gkB, lambda h: kst[:, h, :], junkq, b_junkq, ssq4, rs4, b_s4, b_kst)
                                I(act, lambda: nc.scalar.copy(out=knb[:], in_=kst[:].rearrange("p h d -> p (h d)")), R=[b_kst], W=[b_knb])
                                for h in range(4):
                                    I(pe, lambda h=h: nc.tensor.transpose(psT[0][:, h * 128:(h + 1) * 128], knb[:, h * 128:(h + 1) * 128], identB[:]), R=[b_knb, b_cb], W=[bT[0]], sig=(h == 3))
                                I(dve, lambda h0=h0, j=j: nc.vector.tensor_copy(out=kTst[:, h0:h0 + 4, j * 128:(j + 1) * 128], in_=psT[0][:, 0:512].rearrange("p (h t) -> p h t", h=4)),
                                  W=[b_kTst, bT[0]])
                                dma(act, k_o.rearrange("h t d -> t h d")[blk * 128:(blk + 1) * 128, h0:h0 + 4, :], kst[:], R=[b_kst])
                            elif cg in (4, 5):
                                h0 = (cg - 4) * 4
                                I(act, lambda bk=bk: nc.scalar.copy(out=vst[:].rearrange("p h d -> p (h d)"), in_=psF[bk][:]), W=[b_vst, bF[bk]])
                                I(dve, lambda h0=h0, j=j: nc.vector.tensor_copy(out=vbst[:, h0:h0 + 4, j, :], in_=vst[:]), R=[b_vst], W=[b_vbst])
                                dma(act, v_o.rearrange("h t d -> t h d")[blk * 128:(blk + 1) * 128, h0:h0 + 4, :], vst[:], R=[b_vst])
                            elif cg in (10, 11):
                                h0 = (cg - 10) * 2
                                I(act, lambda h0=h0, j=j, bk=bk: nc.scalar.copy(out=vBt2[G % 2][:, j, h0:h0 + 2, :].rearrange("p h v -> p (h v)"), in_=psF[bk][:]), W=[b_vBt2[G % 2], bF[bk]])
                            elif cg == 14:
                                I(dve, lambda j=j, bk=bk: nc.vector.tensor_copy(out=gts2[G % 2][:, j, :], in_=psF[bk][:, 0:8]), W=[b_gts2[G % 2], bF[bk]])
                            elif cg in (0, 1):
                                h0 = cg * 4
                                jo = j // 2
                                qknorm(128, bk, gqB, lambda h: knb[:, h * 128:(h + 1) * 128], junkq, b_junkq, ssq4, rs4, b_s4, b_knb)
                                for h in range(4):
                                    I(pe, lambda h=h: nc.tensor.transpose(psT[0][:, h * 128:(h + 1) * 128], knb[:, h * 128:(h + 1) * 128], identB[:]), R=[b_knb, b_cb], W=[bT[0]], sig=(h == 3))
                                I(dve, lambda h0=h0, jo=jo: nc.vector.tensor_copy(out=qTst[:, h0:h0 + 4, jo * 128:(jo + 1) * 128], in_=psT[0][:, 0:512].rearrange("p (h t) -> p h t", h=4)),
                                  W=[b_qTst, bT[0]])
                            elif cg in (12, 13):
                                jo = j // 2
                                c0 = (cg - 12) * 512
                                I(act, lambda bk=bk: nc.scalar.activation(out=sg32[:], in_=psF[bk][:], func=AF.Sigmoid), W=[b_sg32, bF[bk]])
                                I(dve, lambda jo=jo, c0=c0: nc.vector.tensor_tensor(out=sgg2[G % 2][:, jo, c0:c0 + 512], in0=sg32[:], in1=ghB[:, c0:c0 + 512], op=ALU.mult), R=[b_sg32, b_small], W=[b_sgg2[G % 2]])
                    dma(act, KTs.rearrange("h d t -> d h t")[:, :, G * 512:(G + 1) * 512], kTst[:], R=[b_kTst])
                    dma(act, Vss.rearrange("h p b d -> p h b d")[:, :, 4 * G:4 * G + 4, :], vbst[:], R=[b_vbst])
                    dma(act, QTs.rearrange("h d t -> d h t")[:, :, G * 256:(G + 1) * 256], qTst[:], R=[b_qTst])

                def genB(G):
                    for ci in range(16):
                        yield None
                        yb = cvy[ci % 2]; b_y = b_cvy[ci % 2]
                        I(dve, lambda ci=ci, yb=yb: nc.vector.tensor_scalar(out=yb[:], in0=raw[:, ci, 0:512], scalar1=wcT[:, ci, 0:1], scalar2=None, op0=ALU.mult), R=[b_raw, b_small], W=[b_y])
                        for jt in range(1, 4):
                            I(dve, lambda ci=ci, yb=yb, jt=jt: nc.vector.scalar_tensor_tensor(out=yb[:], in0=raw[:, ci, jt:jt + 512], scalar=wcT[:, ci, jt:jt + 1], in1=yb[:], op0=ALU.mult, op1=ALU.add),
                              R=[b_raw, b_small], W=[b_y])
                        if ci < 8:
                            I(act, lambda ci=ci, yb=yb: nc.scalar.activation(out=qTt2[G % 2][:, ci, :, :], in_=yb[:].rearrange("p (j t) -> p j t", j=4)[:, 1::2, :], func=AF.Silu, bias=bcT[:, ci:ci + 1]),
                              R=[b_y, b_small], W=[b_qTt2[G % 2]])
                        else:
                            I(act, lambda ci=ci, yb=yb: nc.scalar.activation(out=kTt2[G % 2][:, ci - 8, :], in_=yb[:], func=AF.Silu, bias=bcT[:, ci:ci + 1]),
                              R=[b_y, b_small], W=[b_kTt2[G % 2]])
                    I(dve, lambda: nc.vector.tensor_copy(out=raw[:, :, 0:3], in_=raw[:, :, 512:515]), W=[b_raw])
                    yield 'conv_done'
                    for j in range(4):
                        blk = 4 * G + j
                        own = (j % 2 == 1)
                        jo = j // 2
                        for _ in mlstm_block(M, St, own,
                                    lambda h, half, j=j: qTt2[G % 2][:, 2 * h + half, j // 2, :], b_qTt2[G % 2],
                                    lambda h, half, j=j: kTt2[G % 2][:, 2 * h + half, j * 128:(j + 1) * 128], b_kTt2[G % 2],
                                    vBt2[G % 2][:, j, :, :], b_vBt2[G % 2], gts2[G % 2][:, j, :], b_gts2[G % 2],
                                    sgg2[G % 2][:, jo, :], b_sgg2[G % 2], obt, b_obt, dummy=(blk == 0), sel="sel127"):
                            yield None
                        yield None
                        if own:
                            for c in range(8):
                                I(pe, lambda c=c: nc.tensor.transpose(psT[1][:, c * 128:(c + 1) * 128], obt[:, c * 128:(c + 1) * 128], identB[:]), R=[b_obt, b_cb], W=[bT[1]], sig=(c == 7))
                            I(act, lambda jo=jo: nc.scalar.copy(out=oTst[:, :, jo * 128:(jo + 1) * 128], in_=psT[1][:, :].rearrange("p (c t) -> p c t", c=8)), W=[b_oTst, bT[1]])
                    dma(act, OTs.rearrange("c d t -> d c t")[:, 8:16, G * 256:(G + 1) * 256], oTst[:], R=[b_oTst])

                def drive(gA, gB):
                    a_done = gA is None
                    b_done = gB is None
                    conv_done = b_done
                    while not (a_done and b_done):
                        if not a_done:
                            try:
                                tag = next(gA)
                            except StopIteration:
                                a_done = True
                                tag = None
                            if tag == 'need_raw':
                                while not conv_done and not b_done:
                                    try:
                                        tb_ = next(gB)
                                    except StopIteration:
                                        b_done = True
                                        break
                                    if tb_ == 'conv_done':
                                        conv_done = True
                        if not b_done:
                            try:
                                tb_ = next(gB)
                            except StopIteration:
                                b_done = True
                                tb_ = None
                            if tb_ == 'conv_done':
                                conv_done = True

                drive(genA(0), None)
                for G in range(cfg.ngrp):
                    drive(genA(G + 1) if G + 1 < cfg.ngrp else None, genB(G))
                for a_ in range(2):
                    dma(act, C_o.rearrange("h (a p) v -> a p h v", p=128)[a_], St.C32[:, a_, :, :], R=[St.b_C32])
                with nc.allow_non_contiguous_dma(reason="tiny state vectors"):
                    for a_ in range(2):
                        dma(act, n_o.rearrange("h (a p) -> a p h", p=128)[a_], St.n32[:, a_, :], R=[St.b_n32])
                    dma(act, m_o, St.mB[0:1, :], R=[St.b_m])
                    for r_ in range(3):
                        dma(act, conv_o[r_].rearrange("(c p) -> p c", p=128), rawl[:, :, r_], R=[b_rawl])
                K.barrier(scr)

        def mk_attn(stack, NQ):
            At = type("At", (), {})()

            def t(name, shape, dt, nslot):
                setattr(At, name, [K.sb(stack, "a_%s%d" % (name, i), shape, dt) for i in range(nslot)])
                setattr(At, "b_" + name, [Buf() for _ in range(nslot)])
            t("E", (128, NQ), F32, 3); t("SP", (128, NQ), BF16, 3); t("X", (128, NQ), F32, 2); t("A", (128, NQ), BF16, 3)
            t("ACC", (128, NQ), F32, 2); t("ACCb", (128, NQ), BF16, 3)
            At.zero = K.sb(stack, "a_zero", (128, max(NQ, 128)), BF16); At.b_zero = Buf()
            I(dve, lambda: nc.vector.memset(At.zero[:], 0.0), W=[At.b_zero])
            At.NQ = NQ
            return At

        def attn_head(At, Kslice, b_K, Vslice, b_V, Qt, b_Q, kb_list, zb, tb, ob, extraR=()):
            NQ = At.NQ
            nk = len(kb_list)
            I(pe, lambda: nc.tensor.matmul(psF[ob][:, 0:NQ], At.zero[:, 0:128], At.zero[:, 0:NQ], start=True, stop=False), R=[At.b_zero], W=[bF[ob]])
            for s_ in range(2):
                I(pool, lambda: nc.gpsimd.memset(At.ACC[s_][:], 0.0), W=[At.b_ACC[s_]])
            for s_ in range(3):
                I(pool, lambda: nc.gpsimd.memset(At.ACCb[s_][:], 0.0), W=[At.b_ACCb[s_]])

            def rng(i):
                kb, nkeys, q0, diag, kbias = kb_list[i]
                return kb, nkeys, q0, diag, kbias

            def stage1a(i):
                kb, nkeys, q0, diag, kbias = rng(i)
                E, SP = At.E[i % 3], At.SP[i % 3]
                bE, bSP = At.b_E[i % 3], At.b_SP[i % 3]
                I(pe, lambda: nc.tensor.matmul(psF[zb][0:nkeys, q0:NQ], Kslice(kb), Qt[:, q0:NQ], start=True, stop=True), R=[b_K, b_Q] + list(extraR), W=[bF[zb]])
                if kbias:
                    I(act, lambda: nc.scalar.activation(out=E[0:nkeys, q0:NQ], in_=psF[zb][0:nkeys, q0:NQ], func=AF.Exp, scale=SCALE_A, bias=cmt[0:nkeys, 1:2]), R=[b_cm], W=[bE, bF[zb]])
                else:
                    I(act, lambda: nc.scalar.activation(out=E[0:nkeys, q0:NQ], in_=psF[zb][0:nkeys, q0:NQ], func=AF.Exp, scale=SCALE_A), W=[bE, bF[zb]])
                if diag:
                    dq = min(128, NQ - q0)
                    I(dve, lambda: nc.vector.tensor_tensor(out=E[0:nkeys, q0:q0 + dq], in0=E[0:nkeys, q0:q0 + dq], in1=cF["stri01"][0:nkeys, 0:dq], op=ALU.mult), R=[b_cst], W=[bE])

            def stage1b(i):
                kb, nkeys, q0, diag, kbias = rng(i)
                E, SP = At.E[i % 3], At.SP[i % 3]
                bE, bSP = At.b_E[i % 3], At.b_SP[i % 3]
                I(act, lambda: nc.scalar.activation(out=SP[0:nkeys, q0:NQ], in_=E[0:nkeys, q0:NQ], func=AF.Ln, bias=1.0), R=[bE], W=[bSP])
                if i < nk - 1:
                    o, n_ = i % 2, (i + 1) % 2
                    I(dve, lambda: nc.vector.tensor_tensor(out=At.ACC[n_][0:nkeys, q0:NQ], in0=At.ACC[o][0:nkeys, q0:NQ], in1=SP[0:nkeys, q0:NQ], op=ALU.add),
                      R=[bSP, At.b_ACC[o]], W=[At.b_ACC[n_]])
                    I(dve, lambda: nc.vector.tensor_copy(out=At.ACCb[(i + 1) % 3][0:nkeys, q0:NQ], in_=At.ACC[n_][0:nkeys, q0:NQ]),
                      R=[At.b_ACC[n_]], W=[At.b_ACCb[(i + 1) % 3]])

            def stage2(i):
                kb, nkeys, q0, diag, kbias = rng(i)
                E, SP, X, A = At.E[i % 3], At.SP[i % 3], At.X[i % 2], At.A[i % 3]
                bE, bSP, bX, bA = At.b_E[i % 3], At.b_SP[i % 3], At.b_X[i % 2], At.b_A[i % 3]
                I(pe, lambda: nc.tensor.matmul(psF[tb][0:nkeys, q0:NQ], trigeB[0:nkeys, 0:nkeys], SP[0:nkeys, q0:NQ], start=True, stop=(i == 0)), R=[bSP, b_cb], W=[bF[tb]], sig=(i == 0))
                if i > 0:
                    I(pe, lambda: nc.tensor.matmul(psF[tb][0:nkeys, q0:NQ], onesB[:, 0:nkeys], At.ACCb[i % 3][:, q0:NQ], start=False, stop=True), R=[At.b_ACCb[i % 3], b_cb], W=[bF[tb]])
                I(act, lambda: nc.scalar.activation(out=X[0:nkeys, q0:NQ], in_=psF[tb][0:nkeys, q0:NQ], func=AF.Exp, scale=-1.0), W=[bX, bF[tb]])
                I(pool, lambda: nc.gpsimd.tensor_tensor(out=A[0:nkeys, q0:NQ], in0=X[0:nkeys, q0:NQ], in1=E[0:nkeys, q0:NQ], op=ALU.mult), R=[bX, bE], W=[bA])

            def stage3(i):
                kb, nkeys, q0, diag, kbias = rng(i)
                A, bA = At.A[i % 3], At.b_A[i % 3]
                I(pe, lambda: nc.tensor.matmul(psF[ob][:, q0:NQ], Vslice(kb), A[0:nkeys, q0:NQ], start=False, stop=(i == nk - 1)), R=[b_V, bA] + list(extraR), W=[bF[ob]])

            stage1a(0)
            stage1b(0)
            for i in range(nk):
                if i + 1 < nk:
                    stage1a(i + 1)
                if i >= 2:
                    stage3(i - 2)
                stage2(i)
                if i + 1 < nk:
                    stage1b(i + 1)
            if nk >= 2:
                stage3(nk - 2)
            stage3(nk - 1)

        def stage_B():
            with ExitStack() as SB:
                NQ = 512
                At = mk_attn(SB, NQ)
                Kt = [K.sb(SB, "Kt%d" % i, (128, T), BF16) for i in range(2)]; b_Kt = [Buf(), Buf()]
                Vt = [K.sb(SB, "Vt%d" % i, (128, NB, 128), BF16) for i in range(2)]; b_Vt = [Buf(), Buf()]
                Qt = [K.sb(SB, "Qt%d" % i, (128, NQ), BF16) for i in range(2)]; b_Qt = [Buf(), Buf()]
                oTa = [K.sb(SB, "oTa%d" % i, (128, NQ), BF16) for i in range(2)]; b_oTa = [Buf(), Buf()]
                it = 0
                for g in range(cfg.nown // 4):
                    nkb = 8 * g + 8
                    kb_list = []
                    for kb in reversed(range(nkb)):
                        r = kb - 8 * g
                        if r < 1:
                            q0, diag = 0, False
                        elif r % 2 == 1:
                            q0, diag = (r // 2) * 128, True
                        else:
                            q0, diag = (r // 2) * 128, False
                        kb_list.append((kb, 128, q0, diag, kb == 0))
                    for h in range(8):
                        s = it % 2
                        it += 1
                        dma(sp, Kt[s][:, 0:nkb * 128], KTs[h, :, 0:nkb * 128], W=[b_Kt[s]])
                        dma(sp, Vt[s][:, 0:nkb, :], Vss[h, :, 0:nkb, :], W=[b_Vt[s]])
                        dma(sp, Qt[s][:], QTs[h, :, g * NQ:(g + 1) * NQ], W=[b_Qt[s]])
                        zb, tb, ob = (0, 1, 2) if s == 0 else (3, 4, 5)
                        attn_head(At, lambda kb, s=s: Kt[s][:, kb * 128:(kb + 1) * 128], b_Kt[s],
                                  lambda kb, s=s: Vt[s][:, kb, :], b_Vt[s], Qt[s], b_Qt[s], kb_list, zb, tb, ob)
                        I(act, lambda s=s, ob=ob: nc.scalar.copy(out=oTa[s][:], in_=psF[ob][:, 0:NQ]), W=[b_oTa[s], bF[ob]])
                        dma(sp, OTs[h, :, g * NQ:(g + 1) * NQ], oTa[s][:], R=[b_oTa[s]])
                K.barrier(scr)

        def ffn_tail(stack, ntok_blocks, n, x1, b_x1, h2T, b_h2T, gt2Bt, b_gt2, y_dst, ts, tmp):
            NT = ntok_blocks * n if ntok_blocks > 1 else n
            actT, b_actT, wt, b_wt, yst, b_yst = tmp
            wi_ = [0]
            for cg in range(16):
                s = wi_[0] % 2; wi_[0] += 1
                dma(sp, wt[s][:], wff1S[cg], R=[b_wff1[cg]], W=[b_wt[s]])
                for cc in range(4):
                    hc = cg * 4 + cc
                    bk = next_bank()
                    for kc in range(16):
                        I(pe, lambda kc=kc, cc=cc, s=s, bk=bk: nc.tensor.matmul(psF[bk][:, 0:NT], wt[s][:, kc, cc * 128:(cc + 1) * 128], h2T[:, kc, 0:NT], start=(kc == 0), stop=(kc == 15)),
                          R=[b_wt[s], b_h2T], W=[bF[bk]], sig=(kc == 15))
                    I(act, lambda hc=hc, bk=bk: nc.scalar.activation(out=actT[:, hc, 0:NT], in_=psF[bk][:, 0:NT], func=AF.Relu), W=[b_actT, bF[bk]])
                    I(pool, lambda hc=hc: nc.gpsimd.tensor_tensor(out=actT[:, hc, 0:NT], in0=actT[:, hc, 0:NT], in1=actT[:, hc, 0:NT], op=ALU.mult), W=[b_actT])
            banks = [0, 1, 2, 3]
            for cg in range(4):
                for kq in range(4):
                    s = wi_[0] % 2; wi_[0] += 1
                    dma(sp, wt[s][:], wff2S[cg, kq], R=[b_wff2[cg][kq]], W=[b_wt[s]])
                    for j in range(ntok_blocks):
                        for kc in range(16):
                            hc = kq * 16 + kc
                            I(pe, lambda kc=kc, hc=hc, j=j, s=s: nc.tensor.matmul(psF[banks[j]][0:n, :], actT[:, hc, j * n:(j + 1) * n], wt[s][:, kc, :], start=(hc == 0), stop=(hc == 63)),
                              R=[b_wt[s], b_actT], W=[bF[banks[j]]], sig=(kc == 15))
                for j in range(ntok_blocks):
                    I(dve, lambda j=j, cg=cg: nc.vector.tensor_tensor(out=yst[0:n, :], in0=psF[banks[j]][0:n, :], in1=gt2Bt(j)[0:n, cg * 512:(cg + 1) * 512], op=ALU.mult),
                      R=[b_gt2], W=[b_yst, bF[banks[j]]])
                    I(dve, lambda j=j, cg=cg: nc.vector.tensor_tensor(out=x1[0:n, j, cg * 512:(cg + 1) * 512], in0=x1[0:n, j, cg * 512:(cg + 1) * 512], in1=yst[0:n, :], op=ALU.add),
                      R=[b_yst], W=[b_x1])
            for j in range(ntok_blocks):
                dma(act, y_dst(j), x1[0:n, j, :], R=[b_x1])

        def out_proj(ntok_blocks, n, oT_j, b_oT, x1, b_x1, gt1Bt, b_gt1, wt, b_wt, yst, b_yst):
            for cg in range(4):
                s = cg % 2
                dma(sp, wt[s][:], woutS[cg], R=[b_wout[cg]], W=[b_wt[s]])
                for j in range(ntok_blocks):
                    bk = next_bank()
                    for kc in range(16):
                        I(pe, lambda kc=kc, j=j, s=s, bk=bk: nc.tensor.matmul(psF[bk][0:n, :], oT_j(j, kc), wt[s][:, kc, :], start=(kc == 0), stop=(kc == 15)),
                          R=[b_oT, b_wt[s]], W=[bF[bk]], sig=(kc == 15))
                    I(dve, lambda cg=cg, bk=bk, j=j: nc.vector.tensor_tensor(out=yst[0:n, :], in0=psF[bk][0:n, :], in1=gt1Bt(j)[0:n, cg * 512:(cg + 1) * 512], op=ALU.mult),
                      R=[b_gt1], W=[b_yst, bF[bk]])
                    I(dve, lambda cg=cg, j=j: nc.vector.tensor_tensor(out=x1[0:n, j, cg * 512:(cg + 1) * 512], in0=x1[0:n, j, cg * 512:(cg + 1) * 512], in1=yst[0:n, :], op=ALU.add),
                      R=[b_yst], W=[b_x1])

        def stage_C():
            with ExitStack() as SC:
                gt1B = K.sb(SC, "gt1B", (128, D), F32); gt2B = K.sb(SC, "gt2B", (128, D), F32); b_gtB = Buf()
                with ExitStack() as S1:
                    row_bcast(S1, gt1B, modT[:, 32:48, 0], b_gtB)
                    row_bcast(S1, gt2B, modT[:, 80:96, 0], b_gtB)
                    K.barrier(scr)
                oT = K.sb(SC, "oT", (128, 16, 512), BF16); b_oT = Buf()
                x1 = K.sb(SC, "x1", (128, 4, D), F32); b_x1 = Buf()
                xnb = K.sb(SC, "xnbC", (128, D), BF16); b_xnb = Buf()
                ssx = K.sb(SC, "ssxC", (128, 1), F32); rsx = K.sb(SC, "rsxC", (128, 1), F32); b_sx = Buf()
                h2T = K.sb(SC, "h2T", (128, 16, 512), BF16); b_h2T = Buf()
                actT = K.sb(SC, "actT", (128, 64, 512), BF16); b_actT = Buf()
                wt = [K.sb(SC, "wtC%d" % i, (128, 16, 512), BF16) for i in range(2)]; b_wt = [Buf(), Buf()]
                yst = K.sb(SC, "yst", (128, 512), F32); b_yst = Buf()
                for g in range(cfg.nown // 4):
                    dma(sp, oT[:], OTs.rearrange("c d t -> d c t")[:, :, g * 512:(g + 1) * 512], W=[b_oT])
                    for j in range(4):
                        blk = 2 * (4 * g + j) + 1
                        dma(sp, x1[:, j, :], xs[blk * 128:(blk + 1) * 128, :], W=[b_x1])
                    out_proj(4, 128, lambda j, kc: oT[:, kc, j * 128:(j + 1) * 128], b_oT, x1, b_x1, lambda j: gt1B, b_gtB, wt, b_wt, yst, b_yst)
                    for j in range(4):
                        norm_transpose(128, x1[:, j, :], b_x1, xnb, b_xnb, ssx, rsx, b_sx,
                                       lambda c, j=j: h2T[:, c, j * 128:(j + 1) * 128], b_h2T, A2, modT[:, 48:64, :], 0)
                    ffn_tail(SC, 4, 128, x1, b_x1, h2T, b_h2T, lambda j: gt2B, b_gtB,
                             lambda j, g=g: y_o[(4 * g + j) * 128:(4 * g + j + 1) * 128, :], 0, (actT, b_actT, wt, b_wt, yst, b_yst))
                K.barrier(scr)

        def stage_S():
            n = TS
            nkb = PAST // 128
            with ExitStack() as SS:
                wt = [K.sb(SS, "wtS%d" % i, (128, 16, 512), BF16) for i in range(2)]; b_wt = [Buf(), Buf()]
                xinS = K.sb(SS, "xinS", (n, NS, D), F32); b_xinS = Buf()
                xnb = K.sb(SS, "xnbS", (n, D), BF16); b_xnb = Buf()
                ssx = K.sb(SS, "ssxS", (n, 1), F32); rsx = K.sb(SS, "rsxS", (n, 1), F32); b_sx = Buf()
                hTS = K.sb(SS, "hTS", (128, NS, 16, n), BF16); b_hTS = Buf()
                kstS = K.sb(SS, "kstS", (n, 4, 128), F32); b_kstS = Buf()
                vstS = K.sb(SS, "vstS", (n, 4, 128), F32); b_vstS = Buf()
                knbS = K.sb(SS, "knbS", (n, 512), BF16); b_knbS = Buf()
                junkq = K.sb(SS, "junkqS", (n, 128), F32); b_junkq = Buf()
                ssq4 = K.sb(SS, "ssq4S", (n, 4), F32); rs4 = K.sb(SS, "rs4S", (n, 4), F32); b_s4 = Buf()
                kTn = K.sb(SS, "kTn", (128, NS, 8, n), BF16); b_kTn = Buf()
                qTn = K.sb(SS, "qTn", (128, NS, 8, n), BF16); b_qTn = Buf()
                vnw = K.sb(SS, "vnw", (n, NS, 8, 128), BF16); b_vnw = Buf()
                rawS = K.sb(SS, "rawS", (128, NS, 16, 3 + n), BF16); b_rawS = Buf()
                rawH = K.sb(SS, "rawH", (128, NS, 16, 3), F32); b_rawH = Buf()
                rawlS = K.sb(SS, "rawlS", (128, NS, 16, 3), F32); b_rawlS = Buf()
                cvy = K.sb(SS, "cvyS", (128, n), F32); b_cvy = Buf()
                qTtS = K.sb(SS, "qTtS", (128, NS, 8, n), BF16); b_qTtS = Buf()
                kTtS = K.sb(SS, "kTtS", (128, NS, 8, n), BF16); b_kTtS = Buf()
                vBS = K.sb(SS, "vBS", (n, NS, 4, 256), BF16); b_vBS = Buf()
                gtsS = K.sb(SS, "gtsS", (n, NS, 8), F32); b_gtsS = Buf()
                sg32 = K.sb(SS, "sg32S", (n, 512), F32); b_sg32 = Buf()
                sggS = K.sb(SS, "sggS", (n, NS, 1024), BF16); b_sggS = Buf()
                obtS = K.sb(SS, "obtS", (n, 1024), BF16); b_obtS = Buf()
                oTS = K.sb(SS, "oTS", (128, NS, 16, n), BF16); b_oTS = Buf()
                M16 = mk_mlstm(SS, n)
                St = mk_state(SS)
                dma(sp, xinS[:], xsamp.rearrange("(s t) f -> t s f", t=n), W=[b_xinS])
                with nc.allow_non_contiguous_dma(reason="conv state transposed load (tiny)"):
                    for sb_ in range(NS):
                        for r_ in range(3):
                            dma(sp, rawH[:, sb_, :, r_], sconv[sb_, r_].rearrange("(c p) -> p c", p=128), W=[b_rawH])
                I(dve, lambda: nc.vector.tensor_copy(out=rawS[:, :, :, 0:3], in_=rawH[:]), R=[b_rawH], W=[b_rawS])
                for sb_ in range(NS):
                    norm_transpose(n, xinS[:, sb_, :], b_xinS, xnb, b_xnb, ssx, rsx, b_sx,
                                   lambda c, sb_=sb_: hTS[:, sb_, c, :], b_hTS, A1, modT, 1 + sb_)
                ckpt('S1')
                wsl = [0]
                for cg in A_ORDER:
                    s = wsl[0] % 2
                    wsl[0] += 1
                    ncol = 512 if cg < 14 else 8
                    dma(sp, wt[s][:, :, 0:ncol], winS[cg, :, :, 0:ncol], R=[b_win[cg]], W=[b_wt[s]])
                    for sb_ in range(NS):
                        if cg in (6, 7, 8, 9):
                            for cc in range(4):
                                ci = (cg - 6) * 4 + cc
                                bk = next_bank()
                                for kc in range(16):
                                    I(pe, lambda: nc.tensor.matmul(psF[bk][:, 0:n], wt[s][:, kc, cc * 128:(cc + 1) * 128], hTS[:, sb_, kc, :], start=(kc == 0), stop=(kc == 15)),
                                      R=[b_wt[s], b_hTS], W=[bF[bk]], sig=(kc == 15))
                                I(act, lambda: nc.scalar.copy(out=rawS[:, sb_, ci, 3:3 + n], in_=psF[bk][:, 0:n]), W=[b_rawS, bF[bk]])
                                I(dve, lambda: nc.vector.tensor_copy(out=rawlS[:, sb_, ci, :], in_=psF[bk][:, n - 3:n]), W=[b_rawlS, bF[bk]])
                            continue
                        bk = next_bank()
                        for kc in range(16):
                            I(pe, lambda: nc.tensor.matmul(psF[bk][0:n, 0:ncol], hTS[:, sb_, kc, :], wt[s][:, kc, 0:ncol], start=(kc == 0), stop=(kc == 15)),
                              R=[b_wt[s], b_hTS], W=[bF[bk]], sig=(kc == 15))
                        if cg in (2, 3):
                            h0 = (cg - 2) * 4
                            qknorm(n, bk, gkB, lambda h: kstS[:, h, :], junkq, b_junkq, ssq4, rs4, b_s4, b_kstS)
                            I(act, lambda: nc.scalar.copy(out=knbS[:], in_=kstS[:].rearrange("p h d -> p (h d)")), R=[b_kstS], W=[b_knbS])
                            for h in range(4):
                                I(pe, lambda: nc.tensor.transpose(psT[1][:, h * 128:h * 128 + n], knbS[:, h * 128:(h + 1) * 128], identB[0:n, 0:n]), R=[b_knbS, b_cb], W=[bT[1]], sig=(h == 3))
                            I(dve, lambda: nc.vector.tensor_copy(out=kTn[:, sb_, h0:h0 + 4, :], in_=psT[1][:, 0:512].rearrange("p (h t) -> p h t", h=4)[:, :, 0:n]), W=[b_kTn, bT[1]])
                            dma(act, ks_o[sb_].rearrange("h t d -> t h d")[:, h0:h0 + 4, :], kstS[:], R=[b_kstS])
                        elif cg in (4, 5):
                            h0 = (cg - 4) * 4
                            I(act, lambda: nc.scalar.copy(out=vstS[:].rearrange("p h d -> p (h d)"), in_=psF[bk][0:n, :]), W=[b_vstS, bF[bk]])
                            I(dve, lambda: nc.vector.tensor_copy(out=vnw[:, sb_, h0:h0 + 4, :], in_=vstS[:]), R=[b_vstS], W=[b_vnw])
                            dma(act, vs_o[sb_].rearrange("h t d -> t h d")[:, h0:h0 + 4, :], vstS[:], R=[b_vstS])
                        elif cg in (10, 11):
                            h0 = (cg - 10) * 2
                            I(act, lambda: nc.scalar.copy(out=vBS[:, sb_, h0:h0 + 2, :].rearrange("p h v -> p (h v)"), in_=psF[bk][0:n, :]), W=[b_vBS, bF[bk]])
                        elif cg == 14:
                            I(dve, lambda: nc.vector.tensor_copy(out=gtsS[:, sb_, :], in_=psF[bk][0:n, 0:8]), W=[b_gtsS, bF[bk]])
                        elif cg in (0, 1):
                            h0 = cg * 4
                            qknorm(n, bk, gqB, lambda h: knbS[:, h * 128:(h + 1) * 128], junkq, b_junkq, ssq4, rs4, b_s4, b_knbS)
                            for h in range(4):
                                I(pe, lambda: nc.tensor.transpose(psT[1][:, h * 128:h * 128 + n], knbS[:, h * 128:(h + 1) * 128], identB[0:n, 0:n]), R=[b_knbS, b_cb], W=[bT[1]], sig=(h == 3))
                            I(dve, lambda: nc.vector.tensor_copy(out=qTn[:, sb_, h0:h0 + 4, :], in_=psT[1][:, 0:512].rearrange("p (h t) -> p h t", h=4)[:, :, 0:n]), W=[b_qTn, bT[1]])
                        elif cg in (12, 13):
                            c0 = (cg - 12) * 512
                            I(act, lambda: nc.scalar.activation(out=sg32[:], in_=psF[bk][0:n, :], func=AF.Sigmoid), W=[b_sg32, bF[bk]])
                            I(dve, lambda: nc.vector.tensor_tensor(out=sggS[:, sb_, c0:c0 + 512], in0=sg32[:], in1=ghB[0:n, c0:c0 + 512], op=ALU.mult), R=[b_sg32, b_small], W=[b_sggS])
                ckpt('S2')
                with nc.allow_non_contiguous_dma(reason="conv state transposed store (tiny)"):
                    for sb_ in range(NS):
                        for r_ in range(3):
                            dma(act, convs_o[sb_, r_].rearrange("(c p) -> p c", p=128), rawlS[:, sb_, :, r_], R=[b_rawlS])
                for sb_ in range(NS):
                    for ci in range(16):
                        I(dve, lambda: nc.vector.tensor_scalar(out=cvy[:], in0=rawS[:, sb_, ci, 0:n], scalar1=wcT[:, ci, 0:1], scalar2=None, op0=ALU.mult), R=[b_rawS, b_small], W=[b_cvy])
                        for jt in range(1, 4):
                            I(dve, lambda: nc.vector.scalar_tensor_tensor(out=cvy[:], in0=rawS[:, sb_, ci, jt:jt + n], scalar=wcT[:, ci, jt:jt + 1], in1=cvy[:], op0=ALU.mult, op1=ALU.add),
                              R=[b_rawS, b_small], W=[b_cvy])
                        dst = qTtS[:, sb_, ci, :] if ci < 8 else kTtS[:, sb_, ci - 8, :]
                        I(act, lambda: nc.scalar.activation(out=dst, in_=cvy[:], func=AF.Silu, bias=bcT[:, ci:ci + 1]), R=[b_cvy, b_small], W=[b_qTtS if ci < 8 else b_kTtS])
                ckpt('S3')
                for sb_ in range(NS):
                    for a_ in range(2):
                        dma(sp, St.C32[:, a_, :, :], sC[sb_].rearrange("h (a p) v -> a p h v", p=128)[a_], W=[St.b_C32])
                    with nc.allow_non_contiguous_dma(reason="tiny state vectors"):
                        for a_ in range(2):
                            dma(sp, St.n32[:, a_, :], sn[sb_].rearrange("h (a p) -> a p h", p=128)[a_], W=[St.b_n32])
                        dma(sp, St.mB[:], sm[sb_:sb_ + 1, :].partition_broadcast(128).rearrange("p a d -> p (a d)"), W=[St.b_m])
                    state_refresh_bf(St)
                    for _ in mlstm_block(M16, St, True,
                                lambda h, half: qTtS[:, sb_, 2 * h + half, :], b_qTtS,
                                lambda h, half: kTtS[:, sb_, 2 * h + half, :], b_kTtS,
                                vBS[:, sb_, :, :], b_vBS, gtsS[:, sb_, :], b_gtsS,
                                sggS[:, sb_, :], b_sggS, obtS, b_obtS, dummy=False, sel="sel15"):
                        pass
                    for c in range(8):
                        I(pe, lambda: nc.tensor.transpose(psT[1][:, c * 128:c * 128 + n], obtS[:, c * 128:(c + 1) * 128], identB[0:n, 0:n]), R=[b_obtS, b_cb], W=[bT[1]], sig=(c == 7))
                    I(act, lambda: nc.scalar.copy(out=oTS[:, sb_, 8:16, :], in_=psT[1][:, :].rearrange("p (c t) -> p c t", c=8)[:, :, 0:n]), W=[b_oTS, bT[1]])
                    for a_ in range(2):
                        dma(act, Cs_o[sb_].rearrange("h (a p) v -> a p h v", p=128)[a_], St.C32[:, a_, :, :], R=[St.b_C32])
                    with nc.allow_non_contiguous_dma(reason="tiny state vectors"):
                        for a_ in range(2):
                            dma(act, ns_o[sb_].rearrange("h (a p) -> a p h", p=128)[a_], St.n32[:, a_, :], R=[St.b_n32])
                        dma(act, ms_o[sb_:sb_ + 1, :], St.mB[0:1, :], R=[St.b_m])
                ckpt('S4')
                with ExitStack() as SAT:
                    At = mk_attn(SAT, n)
                    Kc = [K.sb(SAT, "Kc%d" % i, (128, nkb, 128), BF16) for i in range(2)]; b_Kc = [Buf(), Buf()]
                    Vc = [K.sb(SAT, "Vc%d" % i, (128, nkb, 128), BF16) for i in range(2)]; b_Vc = [Buf(), Buf()]
                    KcT = [K.sb(SAT, "KcT%d" % i, (128, nkb * 128), BF16) for i in range(2)]; b_KcT = [Buf(), Buf()]
                    it = 0
                    for sb_ in range(NS):
                        for h in range(8):
                            s = it % 2
                            it += 1
                            for b0 in range(0, nkb, 16):
                                b1 = min(nkb, b0 + 16)
                                dma(pool, Kc[s][:, b0:b1, :], ck[sb_, h].rearrange("(b p) d -> p b d", p=128)[:, b0:b1, :], W=[b_Kc[s]])
                                dma(pool, Vc[s][:, b0:b1, :], cv[sb_, h].rearrange("(b p) d -> p b d", p=128)[:, b0:b1, :], W=[b_Vc[s]])
                            for kb in range(nkb):
                                I(pe, lambda: nc.tensor.transpose(psT[s][:, (kb % 8) * 128:(kb % 8 + 1) * 128], Kc[s][:, kb, :], identB[:]), R=[b_Kc[s], b_cb], W=[bT[s]], sig=(kb % 8 == 7 or kb == nkb - 1))
                                if kb % 8 == 7 or kb == nkb - 1:
                                    k0 = (kb // 8) * 8
                                    w = (kb - k0 + 1) * 128
                                    I(dve, lambda: nc.vector.tensor_copy(out=KcT[s][:, k0 * 128:k0 * 128 + w], in_=psT[s][:, 0:w]), W=[b_KcT[s], bT[s]])
                            kb_list = [("new", n, 0, True, False)] + [(kb, 128, 0, False, False) for kb in reversed(range(nkb))]
                            zb, tb, ob = (0, 1, 2) if s == 0 else (3, 4, 5)
                            Ksl = lambda kb: kTn[:, sb_, h, :] if kb == "new" else KcT[s][:, kb * 128:(kb + 1) * 128]
                            Vsl = lambda kb: vnw[:, sb_, h, :] if kb == "new" else Vc[s][:, kb, :]
                            attn_head(At, Ksl, b_KcT[s], Vsl, b_Vc[s], qTn[:, sb_, h, :], b_qTn, kb_list, zb, tb, ob, extraR=[b_kTn, b_vnw])
                            I(act, lambda: nc.scalar.copy(out=oTS[:, sb_, h, :], in_=psF[ob][:, 0:n]), W=[b_oTS, bF[ob]])
                    K.barrier(scr)
                ckpt('S5')
                gtS = [[K.sb(SS, "gtS%d_%d" % (i, j), (128, D), F32) for j in range(NS)] for i in range(2)]; b_gtS = Buf()
                with ExitStack() as S1:
                    for sb_ in range(NS):
                        row_bcast(S1, gtS[0][sb_], modT[:, 32:48, 1 + sb_], b_gtS)
                        row_bcast(S1, gtS[1][sb_], modT[:, 80:96, 1 + sb_], b_gtS)
                    K.barrier(scr)
                ckpt('S6')
                yst = K.sb(SS, "ystS", (n, 512), F32); b_yst = Buf()
                h2TS = K.sb(SS, "h2TS", (128, 16, NS * n), BF16); b_h2TS = Buf()
                actT = K.sb(SS, "actTS", (128, 64, NS * n), BF16); b_actT = Buf()
                out_proj(NS, n, lambda j, kc: oTS[:, j, kc, :], b_oTS, xinS, b_xinS, lambda j: gtS[0][j], b_gtS, wt, b_wt, yst, b_yst)
                ckpt('S7')
                for sb_ in range(NS):
                    norm_transpose(n, xinS[:, sb_, :], b_xinS, xnb, b_xnb, ssx, rsx, b_sx,
                                   lambda c, sb_=sb_: h2TS[:, c, sb_ * n:(sb_ + 1) * n], b_h2TS, A2, modT[:, 48:64, :], 1 + sb_)
                ckpt('S8')
                ffn_tail(SS, NS, n, xinS, b_xinS, h2TS, b_h2TS, lambda j: gtS[1][j], b_gtS,
                         lambda j: ys_o[j * n:(j + 1) * n, :], 0, (actT, b_actT, wt, b_wt, yst, b_yst))
                K.barrier(scr)

        ckpt("setup")
        stage_A()
        ckpt("A")
        stage_B()
        ckpt("B")
        stage_C()
        ckpt("C")
        stage_S()


_PROG_CACHE = {}


def _core_inputs(c, inp, cfg, consts):
    b, p = c // 2, c % 2
    f = lambda a: np.ascontiguousarray(np.asarray(a, dtype=np.float32))
    x = np.asarray(inp["x_prompt"][b], dtype=np.float32)
    if p == 1:
        xs = x
    else:
        xs = np.concatenate([np.zeros((128, D), np.float32), x[:cfg.T - 128]], axis=0)
    cm = np.zeros((128, 2), np.float32)
    if p == 1:
        cm[:, 0] = 1.0
    else:
        cm[:, 1] = NEG
    ns = cfg.ns
    sl = slice(ns * c, ns * c + ns)
    m = {
        "xs": f(xs), "cm": cm,
        "cvec": f(np.concatenate([np.asarray(inp["c_prompt"])[b:b + 1], np.asarray(inp["c_sample"])[sl]], axis=0)),
        "xsamp": f(np.asarray(inp["x_sample"])[sl].reshape(ns * cfg.ts, D)),
        "ck": f(np.asarray(inp["cache_k"])[0, sl]), "cv": f(np.asarray(inp["cache_v"])[0, sl]),
        "sC": f(np.asarray(inp["state_C"])[0, sl]), "sn": f(np.asarray(inp["state_n"])[0, sl]),
        "sm": f(np.asarray(inp["state_m"])[0, sl]), "sconv": f(np.asarray(inp["state_conv"])[0, sl]),
        "consts": consts,
    }
    for k in ("w_ada", "b_ada", "g_norm1", "w_in", "g_q", "g_k", "w_conv", "b_conv", "b_i", "b_f", "g_h",
              "w_out", "g_norm2", "w_ff1", "w_ff2"):
        m[k] = f(np.asarray(inp[k])[0])
    return m


def run_cores(inp, cores):
    B, SEQ, _ = inp["x_prompt"].shape
    DEC_B, TS, _ = inp["x_sample"].shape
    PAST = inp["cache_k"].shape[3]
    cfg = Cfg(nblk=SEQ // 128, past=PAST, ts=TS, ns=DEC_B // 8)
    key = (cfg.nblk, cfg.past, cfg.ts, cfg.ns)
    if key not in _PROG_CACHE:
        _PROG_CACHE[key] = build_program(cfg)
    nc = _PROG_CACHE[key]
    consts = make_consts()
    in_maps = [_core_inputs(c, inp, cfg, consts) for c in cores]
    res = run_bass_kernel_spmd(nc, in_maps, core_ids=list(range(len(cores))))
    return cfg, res.results


def kernel(**inp):
    B, SEQ, _ = inp["x_prompt"].shape
    DEC_B, TS, _ = inp["x_sample"].shape
    cfg, res = run_cores(inp, list(range(8)))
    ns = cfg.ns
    y_prompt = np.zeros((B, SEQ, D), np.float32)
    k_prompt = np.zeros((1, B, 8, SEQ, 128), np.float32)
    v_prompt = np.zeros((1, B, 8, SEQ, 128), np.float32)
    C_prompt = np.zeros((1, B, 4, 256, 256), np.float32)
    n_prompt = np.zeros((1, B, 4, 256), np.float32)
    m_prompt = np.zeros((1, B, 4), np.float32)
    conv_prompt = np.zeros((1, B, 3, 2048), np.float32)
    y_sample = np.zeros((DEC_B, TS, D), np.float32)
    k_sample = np.zeros((1, DEC_B, 8, TS, 128), np.float32)
    v_sample = np.zeros((1, DEC_B, 8, TS, 128), np.float32)
    C_sample = np.zeros((1, DEC_B, 4, 256, 256), np.float32)
    n_sample = np.zeros((1, DEC_B, 4, 256), np.float32)
    m_sample = np.zeros((1, DEC_B, 4), np.float32)
    conv_sample = np.zeros((1, DEC_B, 3, 2048), np.float32)
    for c in range(8):
        b, p = c // 2, c % 2
        r = res[c]
        yb = y_prompt[b].reshape(cfg.nblk, 128, D)
        yo = r["y"].reshape(cfg.nown, 128, D)
        if p == 1:
            yb[1::2] = yo
            k_prompt[0, b] = r["ko"]
            v_prompt[0, b] = r["vo"]
            C_prompt[0, b] = r["Co"]
            n_prompt[0, b] = r["no"]
            m_prompt[0, b] = r["mo"][0]
            conv_prompt[0, b] = r["convo"]
        else:
            yb[0::2] = yo
        sl = slice(ns * c, ns * c + ns)
        y_sample[sl] = r["ys"].reshape(ns, TS, D)
        k_sample[0, sl] = r["kso"]
        v_sample[0, sl] = r["vso"]
        C_sample[0, sl] = r["Cso"]
        n_sample[0, sl] = r["nso"]
        m_sample[0, sl] = r["mso"]
        conv_sample[0, sl] = r["convso"]
    return (y_prompt, y_sample, k_prompt, v_prompt, C_prompt, n_prompt, m_prompt, conv_prompt,
            k_sample, v_sample, C_sample, n_sample, m_sample, conv_sample)
```

```python
from contextlib import ExitStack

import numpy as np

import concourse.bass as bass
import concourse.mybir as mybir
from concourse.bass_utils import run_bass_kernel_spmd

F32 = mybir.dt.float32
BF16 = mybir.dt.bfloat16
AF = mybir.ActivationFunctionType
ALU = mybir.AluOpType
AX = mybir.AxisListType

D = 2048
DIN = 7176
DFF = 8192
EPS = 1e-6
NEG = -30000.0
NCG = 15


class StopBuild(Exception):
    pass


class Cfg:
    def __init__(self, nblk=64, past=4096, ts=16, ns=2):
        self.nblk = nblk
        self.past = past
        self.ts = ts
        self.ns = ns
        self.T = nblk * 128
        self.nown = nblk // 2
        self.Town = self.nown * 128
        self.ngrp = nblk // 4


class Eng:
    def __init__(self, k, e, name):
        self.k = k
        self.e = e
        self.name = name
        self.sem = k.es.enter_context(k.nc.semaphore("s_" + name))
        self.count = 0
        self.seen = {}

    def wait(self, tok):
        if tok is None:
            return
        src, v = tok
        if src is self and self.name == "pe":
            return
        if self.seen.get(id(src), 0) >= v:
            return
        self.e.wait_ge(src.sem, v)
        self.seen[id(src)] = v

    def signal(self, ins):
        self.count += 1
        ins.then_inc(self.sem, 1)
        return (self, self.count)


class DmaSem:
    def __init__(self, k, name):
        self.sem = k.es.enter_context(k.nc.semaphore("d_" + name))
        self.count = 0


class Buf:
    def __init__(self, name=""):
        self.name = name
        self.w = None
        self.r = {}

    def note_read(self, tok):
        self.r[id(tok[0])] = tok

    def note_write(self, tok):
        self.w = tok
        self.r = {}


class Builder:
    def __init__(self, nc, es):
        self.nc = nc
        self.es = es
        self.pe = Eng(self, nc.tensor, "pe")
        self.act = Eng(self, nc.scalar, "act")
        self.dve = Eng(self, nc.vector, "dve")
        self.pool = Eng(self, nc.gpsimd, "pool")
        self.sp = Eng(self, nc.sync, "sp")
        self.pe_flush = None
        self.dsems = []
        self.ring = []
        self.ring_i = 0
        self.nt = 0

    def sb(self, stack, name, shape, dt):
        self.nt += 1
        return stack.enter_context(self.nc.sbuf_tensor("%s_%d" % (name, self.nt), list(shape), dt))

    def ps(self, stack, name, shape, dt):
        self.nt += 1
        return stack.enter_context(self.nc.psum_tensor("%s_%d" % (name, self.nt), list(shape), dt))

    def dsem(self, name):
        d = DmaSem(self, name + str(len(self.dsems)))
        self.dsems.append(d)
        return d

    def I(self, eng, fn, R=(), W=(), sig=True):
        for b in R:
            eng.wait(b.w)
        for b in W:
            eng.wait(b.w)
            for t in list(b.r.values()):
                eng.wait(t)
        ins = fn()
        if not hasattr(eng, "pend"):
            eng.pend = ([], [])
        if sig:
            tok = eng.signal(ins)
            if eng.name == "pe" and self.pe_flush is not None:
                self.nc.tensor.ldweights(self.pe_flush)
            for b in list(R) + eng.pend[0]:
                b.note_read(tok)
            for b in list(W) + eng.pend[1]:
                b.note_write(tok)
            eng.pend = ([], [])
            return tok
        eng.pend[0].extend(R)
        eng.pend[1].extend(W)
        return None

    def dma(self, q, out, in_, R=(), W=(), ds=None, **kw):
        for b in R:
            q.wait(b.w)
        for b in W:
            q.wait(b.w)
            for t in list(b.r.values()):
                q.wait(t)
        if ds is None:
            if len(self.ring) < 24:
                ds = self.dsem("r")
                self.ring.append(ds)
            else:
                ds = self.ring[self.ring_i % len(self.ring)]
                self.ring_i += 1
                q.wait((ds, ds.count))
        ins = q.e.dma_start(out=out, in_=in_, **kw)
        ins.then_inc(ds.sem, 16)
        ds.count += 16
        tok = (ds, ds.count)
        for b in R:
            b.note_read(tok)
        for b in W:
            b.note_write(tok)
        return tok

    def barrier(self, sc):
        nc = self.nc
        for d in self.dsems:
            if d.count:
                self.sp.wait((d, d.count))
        t_pe = self.I(self.pe, lambda: nc.tensor.matmul(sc["ps"][0:1, 0:2], sc["one_b"][0:1, 0:1], sc["one_b"][0:1, 0:2], start=True, stop=True), W=[sc["ps_buf"]])
        t_act = self.act.signal(nc.scalar.copy(out=sc["sa"][0:1, 0:1], in_=sc["one_f"][0:1, 0:1]))
        t_dve = self.dve.signal(nc.vector.tensor_copy(out=sc["sd"][0:1, 0:1], in_=sc["one_f"][0:1, 0:1]))
        t_pool = self.pool.signal(nc.gpsimd.tensor_copy(out=sc["sp"][0:1, 0:1], in_=sc["one_f"][0:1, 0:1]))
        for t in (t_pe, t_act, t_dve, t_pool):
            self.sp.wait(t)
        self.sp.count += 1
        nc.sync.sem_inc(self.sp.sem, 1)
        t_sp = (self.sp, self.sp.count)
        for e in (self.pe, self.act, self.dve, self.pool):
            e.wait(t_sp)
        return t_sp


CONST_NAMES = ["ident", "ones", "triinc", "causneg", "sel127", "sel15", "stri01", "trige", "causnegT"]


def make_consts():
    i = np.arange(128)
    c = {}
    c["ident"] = np.eye(128, dtype=np.float32)
    c["ones"] = np.ones((128, 128), np.float32)
    c["triinc"] = (i[:, None] <= i[None, :]).astype(np.float32)
    c["causneg"] = np.where(i[None, :] <= i[:, None], 0.0, NEG).astype(np.float32)
    s = np.zeros((128, 128), np.float32); s[127, :] = 1.0
    c["sel127"] = s
    s = np.zeros((128, 128), np.float32); s[15, :] = 1.0
    c["sel15"] = s
    c["stri01"] = (i[:, None] < i[None, :]).astype(np.float32)
    c["trige"] = (i[:, None] >= i[None, :]).astype(np.float32)
    c["causnegT"] = np.ascontiguousarray(c["causneg"].T)
    return np.concatenate([c[n] for n in CONST_NAMES], axis=1)


def build_program(cfg):
    nc = bass.Bass("TRN2", target_bir_lowering=False)
    try:
        _build_body(nc, cfg)
    except StopBuild:
        pass
    return nc


def _build_body(nc, cfg):
    T, Town, NB, NS, TS, PAST = cfg.T, cfg.Town, cfg.nblk, cfg.ns, cfg.ts, cfg.past
    NTOKS = NS * TS

    def din(name, shape):
        return nc.dram_tensor(name, list(shape), F32, kind="ExternalInput").ap()

    def dout(name, shape):
        return nc.dram_tensor(name, list(shape), F32, kind="ExternalOutput").ap()

    def dscr(name, shape, dt=BF16):
        return nc.dram_tensor(name, list(shape), dt).ap()

    xs = din("xs", (T, D))
    cmd = din("cm", (128, 2))
    cvec = din("cvec", (1 + NS, D))
    xsamp = din("xsamp", (NTOKS, D))
    ck = din("ck", (NS, 8, PAST, 128))
    cv = din("cv", (NS, 8, PAST, 128))
    sC = din("sC", (NS, 4, 256, 256))
    sn = din("sn", (NS, 4, 256))
    sm = din("sm", (NS, 4))
    sconv = din("sconv", (NS, 3, 2048))
    w_ada = din("w_ada", (D, 6 * D))
    b_ada = din("b_ada", (6 * D,))
    g_norm1 = din("g_norm1", (D,))
    w_in = din("w_in", (D, DIN))
    g_q = din("g_q", (128,))
    g_k = din("g_k", (128,))
    w_conv = din("w_conv", (4, 2048))
    b_conv = din("b_conv", (2048,))
    b_i = din("b_i", (4,))
    b_f = din("b_f", (4,))
    g_h = din("g_h", (4, 256))
    w_out = din("w_out", (D, D))
    g_norm2 = din("g_norm2", (D,))
    w_ff1 = din("w_ff1", (D, DFF))
    w_ff2 = din("w_ff2", (DFF, D))
    constd = din("consts", (128, 128 * len(CONST_NAMES)))

    y_o = dout("y", (Town, D))
    k_o = dout("ko", (8, T, 128))
    v_o = dout("vo", (8, T, 128))
    C_o = dout("Co", (4, 256, 256))
    n_o = dout("no", (4, 256))
    m_o = dout("mo", (1, 4))
    conv_o = dout("convo", (3, 2048))
    ys_o = dout("ys", (NTOKS, D))
    ks_o = dout("kso", (NS, 8, TS, 128))
    vs_o = dout("vso", (NS, 8, TS, 128))
    Cs_o = dout("Cso", (NS, 4, 256, 256))
    ns_o = dout("nso", (NS, 4, 256))
    ms_o = dout("mso", (NS, 4))
    convs_o = dout("convso", (NS, 3, 2048))

    winS = dscr("winS", (NCG, 128, 16, 512))
    woutS = dscr("woutS", (4, 128, 16, 512))
    wff1S = dscr("wff1S", (16, 128, 16, 512))
    wff2S = dscr("wff2S", (4, 4, 128, 16, 512))
    KTs = dscr("KTs", (8, 128, T))
    Vss = dscr("Vss", (8, 128, NB, 128))
    QTs = dscr("QTs", (8, 128, Town))
    OTs = dscr("OTs", (16, 128, Town))

    with ExitStack() as es:
        K = Builder(nc, es)
        I, dma = K.I, K.dma
        pe, act, dve, pool, sp = K.pe, K.act, K.dve, K.pool, K.sp
        P = es

        cst = K.sb(P, "cst", (128, 128 * len(CONST_NAMES)), F32)
        b_cst = Buf("cst")
        cF = {n: cst[:, i * 128:(i + 1) * 128] for i, n in enumerate(CONST_NAMES)}
        identB = K.sb(P, "identB", (128, 128), BF16)
        onesB = K.sb(P, "onesB", (128, 128), BF16)
        trigeB = K.sb(P, "trigeB", (128, 128), BF16)
        b_cb = Buf("cstb")
        cmt = K.sb(P, "cmt", (128, 2), F32)
        b_cm = Buf("cm")
        modT = K.sb(P, "modT", (128, 96, 1 + NS), F32)
        A1 = K.sb(P, "A1", (128, 16, 1 + NS), F32)
        A2 = K.sb(P, "A2", (128, 16, 1 + NS), F32)
        b_mod = Buf("mod")
        gqB = K.sb(P, "gqB", (128, 128), F32)
        gkB = K.sb(P, "gkB", (128, 128), F32)
        ghB = K.sb(P, "ghB", (128, 1024), F32)
        biB = K.sb(P, "biB", (128, 4), F32)
        bfB = K.sb(P, "bfB", (128, 4), F32)
        wcT = K.sb(P, "wcT", (128, 16, 4), F32)
        bcT = K.sb(P, "bcT", (128, 16), F32)
        b_small = Buf("small")
        epsT = K.sb(P, "epsT", (128, 1), F32)
        scr = {"one_f": cF["ones"], "one_b": onesB,
               "sa": K.sb(P, "bsa", (1, 2), F32), "sd": K.sb(P, "bsd", (1, 2), F32), "sp": K.sb(P, "bsp", (1, 2), F32)}
        psF = [K.ps(P, "psF%d" % i, (128, 512), F32) for i in range(6)]
        bF = [Buf("psF%d" % i) for i in range(6)]
        psT = [K.ps(P, "psT%d" % i, (128, 1024), BF16) for i in range(2)]
        bT = [Buf("psT%d" % i) for i in range(2)]
        scr["ps"] = psF[0]
        scr["ps_buf"] = bF[0]
        flushw = K.sb(P, "flushw", (128, 2), BF16)
        nc.vector.memset(flushw[:], 0.0)
        K.pe_flush = flushw[:, 0:1]
        import os as _os
        _stop = _os.environ.get("KSTOP", "")

        def ckpt(tag):
            if _stop == tag:
                K.barrier(scr)
                raise StopBuild()

        dma(sp, cst[:], constd, W=[b_cst])
        dma(sp, cmt[:], cmd, W=[b_cm])
        I(dve, lambda: nc.vector.tensor_copy(out=identB[:], in_=cF["ident"]), R=[b_cst], W=[b_cb])
        I(dve, lambda: nc.vector.tensor_copy(out=onesB[:], in_=cF["ones"]), R=[b_cst], W=[b_cb])
        I(dve, lambda: nc.vector.tensor_copy(out=trigeB[:], in_=cF["trige"]), R=[b_cst], W=[b_cb])
        I(dve, lambda: nc.vector.memset(epsT[:], EPS), W=[b_small])
        dma(sp, gqB[:], g_q.unsqueeze(0).partition_broadcast(128).rearrange("p a d -> p (a d)"), W=[b_small])
        dma(sp, gkB[:], g_k.unsqueeze(0).partition_broadcast(128).rearrange("p a d -> p (a d)"), W=[b_small])
        dma(sp, ghB[:], g_h.rearrange("h d -> (h d)").unsqueeze(0).partition_broadcast(128).rearrange("p a d -> p (a d)"), W=[b_small])
        dma(sp, biB[:], b_i.unsqueeze(0).partition_broadcast(128).rearrange("p a d -> p (a d)"), W=[b_small])
        dma(sp, bfB[:], b_f.unsqueeze(0).partition_broadcast(128).rearrange("p a d -> p (a d)"), W=[b_small])

        ckpt("s1")
        b_win = [Buf("win%d" % i) for i in range(NCG)]
        b_wout = [Buf("wout%d" % i) for i in range(4)]
        b_wff1 = [Buf("wff1%d" % i) for i in range(16)]
        b_wff2 = [[Buf("wff2%d_%d" % (i, j)) for j in range(4)] for i in range(4)]
        cvt_sems = [K.dsem("cvt") for _ in range(3)]
        ncv = [0]

        def convert(dst, src, buf):
            ds = cvt_sems[ncv[0] % 3]
            ncv[0] += 1
            pool.wait((ds, ds.count))
            dma(pool, dst, src, W=[buf], ds=ds)

        A_ORDER = [2, 3, 4, 5, 8, 9, 10, 11, 14, 6, 7, 0, 1, 12, 13]
        for cg in A_ORDER:
            ncol = 512 if cg < 14 else 8
            convert(winS[cg, :, :, 0:ncol],
                    w_in[:, cg * 512:cg * 512 + ncol].rearrange("(kc p) n -> p kc n", p=128), b_win[cg])

        ckpt("s2")
        with ExitStack() as S0:
            cT = K.sb(S0, "cT", (1 + NS, D), F32)
            b_cT = Buf()
            dma(sp, cT[:], cvec, W=[b_cT])
            I(act, lambda: nc.scalar.activation(out=cT[:], in_=cT[:], func=AF.Silu), R=[], W=[b_cT])
            scT = K.sb(S0, "scT", (128, 16, 1 + NS), F32)
            b_scT = Buf()
            nr = 1 + NS
            for c in range(16):
                I(pe, lambda c=c: nc.tensor.transpose(psF[0][:, c * nr:(c + 1) * nr], cT[:, c * 128:(c + 1) * 128], cF["ident"][0:nr, 0:nr]),
                  R=[b_cT, b_cst], W=[bF[0]], sig=(c == 15))
            I(dve, lambda: nc.vector.tensor_copy(out=scT[:].rearrange("p a b -> p (a b)"), in_=psF[0][:, 0:16 * nr]), R=[bF[0]], W=[b_scT])
            ckpt("s3")
            rowsT = K.sb(S0, "rowsT", (128, 128), F32)
            b_rows = Buf()
            dma(sp, rowsT[0:96, :], b_ada.rearrange("(c p) -> c p", p=128), W=[b_rows])
            dma(sp, rowsT[96:112, :], g_norm1.rearrange("(c p) -> c p", p=128), W=[b_rows])
            dma(sp, rowsT[112:128, :], g_norm2.rearrange("(c p) -> c p", p=128), W=[b_rows])
            rows2 = K.sb(S0, "rows2", (128, 128), F32)
            I(dve, lambda: nc.vector.memset(rows2[:], 0.0), W=[b_rows])
            dma(sp, rows2[0:64, :], w_conv.rearrange("j (c p) -> (j c) p", p=128), W=[b_rows])
            dma(sp, rows2[64:80, :], b_conv.rearrange("(c p) -> c p", p=128), W=[b_rows])
            colsT = K.sb(S0, "colsT", (128, 256), F32)
            b_cols = Buf()
            I(pe, lambda: nc.tensor.transpose(psF[1][:, 0:128], rowsT[:], cF["ident"]), R=[b_rows, b_cst], W=[bF[1]], sig=False)
            I(pe, lambda: nc.tensor.transpose(psF[1][:, 128:256], rows2[:], cF["ident"]), R=[b_rows, b_cst], W=[bF[1]])
            I(dve, lambda: nc.vector.tensor_copy(out=colsT[:], in_=psF[1][:, 0:256]), R=[bF[1]], W=[b_cols])
            I(dve, lambda: nc.vector.tensor_copy(out=wcT[:], in_=colsT[:, 128:192].rearrange("p (j c) -> p c j", j=4)), R=[b_cols], W=[b_small])
            I(dve, lambda: nc.vector.tensor_copy(out=bcT[:], in_=colsT[:, 192:208]), R=[b_cols], W=[b_small])
            ckpt("s4")
            wpan = [K.sb(S0, "wpan%d" % i, (128, 16, 512), F32) for i in range(2)]
            b_wpan = [Buf(), Buf()]
            for jg in range(24):
                s = jg % 2
                dma(sp, wpan[s][:], w_ada[:, jg * 512:(jg + 1) * 512].rearrange("(kc p) n -> p kc n", p=128), W=[b_wpan[s]])
                for jj in range(4):
                    j = jg * 4 + jj
                    for kc in range(16):
                        I(pe, lambda j=j, jj=jj, s=s, kc=kc: nc.tensor.matmul(psF[2][:, j * nr:(j + 1) * nr], wpan[s][:, kc, jj * 128:(jj + 1) * 128],
                                                                       scT[:, kc, :], start=(kc == 0), stop=(kc == 15)),
                          R=[b_wpan[s], b_scT], W=[bF[2]], sig=(kc == 15 and jj == 3))
            ckpt("s5")
            I(dve, lambda: nc.vector.tensor_tensor(out=modT[:], in0=psF[2][:, 0:96 * nr].rearrange("p (a b) -> p a b", b=nr),
                                                   in1=colsT[:, 0:96].unsqueeze(2).broadcast_to([128, 96, nr]), op=ALU.add),
              R=[bF[2], b_cols], W=[b_mod])
            ckpt("s6")
            I(dve, lambda: nc.vector.scalar_tensor_tensor(out=A1[:], in0=modT[:, 16:32, :], scalar=1.0,
                                                          in1=colsT[:, 96:112].unsqueeze(2).broadcast_to([128, 16, nr]), op0=ALU.add, op1=ALU.mult),
              R=[b_mod, b_cols], W=[b_mod])
            I(dve, lambda: nc.vector.scalar_tensor_tensor(out=A2[:], in0=modT[:, 64:80, :], scalar=1.0,
                                                          in1=colsT[:, 112:128].unsqueeze(2).broadcast_to([128, 16, nr]), op0=ALU.add, op1=ALU.mult),
              R=[b_mod, b_cols], W=[b_mod])
            K.barrier(scr)

        _dgc = {}

        def row_bcast(stack, dst, srcT, dstbuf):
            if id(stack) not in _dgc:
                _dgc[id(stack)] = (K.sb(stack, "dg", (128, 16, 128), F32), Buf())
            dg, b_dg = _dgc[id(stack)]
            I(dve, lambda: nc.vector.tensor_tensor(out=dg[:], in0=cF["ident"].unsqueeze(1).broadcast_to([128, 16, 128]),
                                                   in1=srcT.unsqueeze(2).broadcast_to([128, 16, 128]), op=ALU.mult),
              R=[b_cst, b_mod], W=[b_dg])
            for q in range(4):
                I(pe, lambda q=q: nc.tensor.matmul(psF[q][:], cF["ones"], dg[:, 4 * q:4 * q + 4, :].rearrange("p a b -> p (a b)"), start=True, stop=True),
                  R=[b_dg, b_cst], W=[bF[q]])
                I(dve, lambda q=q: nc.vector.tensor_copy(out=dst[:, q * 512:(q + 1) * 512], in_=psF[q][:]), R=[bF[q]], W=[dstbuf])

        ckpt("s8")
        for cg in range(4):
            convert(woutS[cg], w_out[:, cg * 512:(cg + 1) * 512].rearrange("(kc p) n -> p kc n", p=128), b_wout[cg])
        for cg in range(16):
            convert(wff1S[cg], w_ff1[:, cg * 512:(cg + 1) * 512].rearrange("(kc p) n -> p kc n", p=128), b_wff1[cg])
        for cg in range(4):
            for kq in range(4):
                convert(wff2S[cg, kq], w_ff2[kq * 2048:(kq + 1) * 2048, cg * 512:(cg + 1) * 512].rearrange("(kc p) n -> p kc n", p=128), b_wff2[cg][kq])

        SCALE_A = 128.0 ** -0.5
        rr = [0]

        def next_bank():
            i = 3 + rr[0] % 3
            rr[0] += 1
            return i

        def mk_mlstm(stack, L):
            M = type("M", (), {})()

            def t(name, shape, dt=F32):
                setattr(M, name, K.sb(stack, "m_" + name, shape, dt))
                setattr(M, "b_" + name, Buf(name))
            for nm in ["fz", "e4", "sp4", "ig", "g", "nb", "cmx", "na", "wi", "wia", "den", "r", "emt", "ssq", "rsd", "fac", "wst", "dsub"]:
                t(nm, (L, 4))
            t("am", (L, 8)); t("dn", (L, 8))
            t("dg", (L, 4, L)); t("Gm", (L, 4, L)); t("ET", (L, 4, L)); t("PT", (L, 4, L), BF16)
            t("cn4", (L, 4, L))
            t("qc", (L, 2, 256)); t("num", (L, 4, 256)); t("kTW", (L, 8, 128), BF16); t("junk", (L, 256))
            t("aLs", (128, 8)); t("dec", (128, 4)); t("dsb", (128, 4))
            I(dve, lambda: nc.vector.tensor_copy(out=M.cn4[:], in_=cF["causnegT"][0:L, 0:L].unsqueeze(1).broadcast_to([L, 4, L])), R=[b_cst], W=[M.b_cn4])
            M.L = L
            return M

        def mk_state(stack):
            St = type("St", (), {})()
            St.C32 = K.sb(stack, "C32", (128, 2, 4, 256), F32); St.b_C32 = Buf()
            St.Cbf = K.sb(stack, "Cbf", (128, 2, 4, 256), BF16); St.b_Cbf = Buf()
            St.n32 = K.sb(stack, "n32", (128, 2, 4), F32); St.b_n32 = Buf()
            St.nbf = K.sb(stack, "nbf", (128, 2, 4), BF16); St.b_nbf = Buf()
            St.mB = K.sb(stack, "mB", (128, 4), F32); St.b_m = Buf()
            return St

        def state_refresh_bf(St):
            I(act, lambda: nc.scalar.copy(out=St.Cbf[:].rearrange("p a h v -> p (a h v)"), in_=St.C32[:].rearrange("p a h v -> p (a h v)")), R=[St.b_C32], W=[St.b_Cbf])
            I(dve, lambda: nc.vector.tensor_copy(out=St.nbf[:].rearrange("p a h -> p (a h)"), in_=St.n32[:].rearrange("p a h -> p (a h)")), R=[St.b_n32], W=[St.b_nbf])

        def mlstm_block(M, St, own, qT, b_qT, kT, b_kT, vB, b_vB, gt, b_gt, sgg, b_sgg, ob, b_ob, dummy, sel):
            L = M.L
            idL = cF["ident"][0:L, 0:L]
            onL = cF["ones"][0:L, 0:L]
            b3 = lambda ap: ap.unsqueeze(2).broadcast_to([L, 4, L])
            m3 = lambda ap: ap.unsqueeze(1).broadcast_to([L, 4, L])
            p3 = lambda bank: psF[bank][0:L, 0:4 * L].rearrange("p (h s) -> p h s", h=4)
            I(dve, lambda: nc.vector.tensor_tensor(out=M.fz[:], in0=gt[:, 4:8], in1=bfB[0:L, :], op=ALU.add), R=[b_gt, b_small], W=[M.b_fz])
            I(act, lambda: nc.scalar.activation(out=M.e4[:], in_=M.fz[:], func=AF.Exp, scale=-1.0), R=[M.b_fz], W=[M.b_e4])
            I(act, lambda: nc.scalar.activation(out=M.sp4[:], in_=M.e4[:], func=AF.Ln, bias=1.0), R=[M.b_e4], W=[M.b_sp4])
            I(dve, lambda: nc.vector.tensor_tensor(out=M.ig[:], in0=gt[:, 0:4], in1=biB[0:L, :], op=ALU.add), R=[b_gt, b_small], W=[M.b_ig])
            if dummy:
                I(dve, lambda: nc.vector.tensor_scalar(out=M.ig[:], in0=M.ig[:], scalar1=cmt[0:L, 1:2], scalar2=None, op0=ALU.add), R=[b_cm], W=[M.b_ig])
                I(dve, lambda: nc.vector.tensor_scalar(out=M.sp4[:], in0=M.sp4[:], scalar1=cmt[0:L, 0:1], scalar2=None, op0=ALU.mult), R=[b_cm], W=[M.b_sp4])
            yield None
            I(pe, lambda: nc.tensor.matmul(psF[0][0:L, 0:4], cF["triinc"][0:L, 0:L], M.sp4[:], start=True, stop=True), R=[M.b_sp4, b_cst], W=[bF[0]])
            I(dve, lambda: nc.vector.tensor_tensor(out=M.g[:], in0=M.ig[:], in1=psF[0][0:L, 0:4], op=ALU.add), R=[M.b_ig], W=[M.b_g, bF[0]])
            I(dve, lambda: nc.vector.tensor_copy(out=M.nb[:], in_=psF[0][0:L, 0:4]), W=[M.b_nb, bF[0]])
            yield None
            I(dve, lambda: nc.vector.tensor_tensor(out=M.dg[:], in0=m3(idL), in1=b3(M.g[:]), op=ALU.mult), R=[M.b_g, b_cst], W=[M.b_dg])
            I(pe, lambda: nc.tensor.matmul(psF[1][0:L, 0:4 * L], onL, M.dg[:].rearrange("p h s -> p (h s)"), start=True, stop=True), R=[M.b_dg, b_cst], W=[bF[1]])
            I(dve, lambda: nc.vector.tensor_tensor(out=M.Gm[:], in0=p3(1), in1=m3(cF["causneg"][0:L, 0:L]), op=ALU.add), R=[b_cst], W=[M.b_Gm, bF[1]])
            I(dve, lambda: nc.vector.tensor_reduce(out=M.cmx[:], in_=M.Gm[:], axis=AX.X, op=ALU.max), R=[M.b_Gm], W=[M.b_cmx])
            I(dve, lambda: nc.vector.tensor_tensor(out=M.am[:, 0:4], in0=M.cmx[:], in1=St.mB[0:L, :], op=ALU.max), R=[M.b_cmx, St.b_m], W=[M.b_am])
            I(dve, lambda: nc.vector.tensor_tensor(out=M.am[:, 4:8], in0=M.am[:, 0:4], in1=M.nb[:], op=ALU.subtract), R=[M.b_nb], W=[M.b_am])
            I(pe, lambda: nc.tensor.matmul(psF[0][:, 8:16], cF[sel][0:L, :], M.am[:], start=True, stop=True), R=[M.b_am, b_cst], W=[bF[0]])
            I(dve, lambda: nc.vector.tensor_copy(out=M.aLs[:], in_=psF[0][:, 8:16]), W=[M.b_aLs, bF[0]])
            yield None
            if own:
                I(dve, lambda: nc.vector.tensor_scalar(out=M.na[:], in0=M.am[:, 0:4], scalar1=-1.0, scalar2=None, op0=ALU.mult), R=[M.b_am], W=[M.b_na])
                I(dve, lambda: nc.vector.tensor_tensor(out=M.dg[:], in0=m3(idL), in1=b3(M.na[:]), op=ALU.mult), R=[M.b_na, b_cst], W=[M.b_dg])
                I(pe, lambda: nc.tensor.matmul(psF[1][0:L, 0:4 * L], onL, M.dg[:].rearrange("p h s -> p (h s)"), start=True, stop=False), R=[M.b_dg, b_cst], W=[bF[1]], sig=False)
                I(pe, lambda: nc.tensor.matmul(psF[1][0:L, 0:4 * L], idL, M.cn4[:].rearrange("p h s -> p (h s)"), start=False, stop=True), R=[M.b_cn4, b_cst], W=[bF[1]])
                for h in range(4):
                    I(act, lambda h=h: nc.scalar.activation(out=M.ET[:, h, :], in_=psF[1][0:L, h * L:(h + 1) * L], func=AF.Exp, bias=M.g[:, h:h + 1], scale=1.0),
                      R=[M.b_g], W=[M.b_ET, bF[1]])
                yield None
                for h in range(4):
                    for half in range(2):
                        I(pe, lambda h=h, half=half: nc.tensor.matmul(psF[2][0:L, h * L:(h + 1) * L], kT(h, half), qT(h, half), start=(half == 0), stop=(half == 1)),
                          R=[b_kT, b_qT], W=[bF[2]], sig=(h == 3 and half == 1))
                I(dve, lambda: nc.vector.scalar_tensor_tensor(out=M.PT[:], in0=p3(2), scalar=1.0 / 16, in1=M.ET[:], op0=ALU.mult, op1=ALU.mult), R=[M.b_ET], W=[M.b_PT, bF[2]])
                yield None
                I(dve, lambda: nc.vector.tensor_tensor(out=M.wia[:], in0=St.mB[0:L, :], in1=M.am[:, 0:4], op=ALU.subtract), R=[St.b_m, M.b_am], W=[M.b_wia])
                I(act, lambda: nc.scalar.activation(out=M.wi[:], in_=M.wia[:], func=AF.Exp), R=[M.b_wia], W=[M.b_wi])
                for h in range(4):
                    I(pe, lambda h=h: nc.tensor.matmul(psF[0][0:L, 16 + h:17 + h], M.PT[:, h, :], onesB[0:L, 0:1], start=True, stop=True), R=[M.b_PT, b_cb], W=[bF[0]], sig=False)
                for h in range(4):
                    for half in range(2):
                        I(pe, lambda h=h, half=half: nc.tensor.matmul(psF[0][0:L, 20 + h:21 + h], qT(h, half), St.nbf[:, half, h:h + 1], start=(half == 0), stop=(half == 1)),
                          R=[b_qT, St.b_nbf], W=[bF[0]], sig=(h == 3 and half == 1))
                I(dve, lambda: nc.vector.tensor_copy(out=M.dn[:], in_=psF[0][0:L, 16:24]), W=[M.b_dn, bF[0]])
                I(dve, lambda: nc.vector.tensor_tensor(out=M.den[:], in0=M.dn[:, 4:8], in1=M.wi[:], op=ALU.mult), R=[M.b_dn, M.b_wi], W=[M.b_den])
                I(dve, lambda: nc.vector.tensor_tensor(out=M.den[:], in0=M.den[:], in1=M.dn[:, 0:4], op=ALU.add), R=[M.b_dn], W=[M.b_den])
                yield None
                I(act, lambda: nc.scalar.activation(out=M.emt[:], in_=M.am[:, 4:8], func=AF.Exp, scale=-1.0), R=[M.b_am], W=[M.b_emt])
                I(dve, lambda: nc.vector.tensor_scalar(out=M.dsub[:], in0=M.den[:], scalar1=-1.0, scalar2=None, op0=ALU.mult), R=[M.b_den], W=[M.b_dsub])
                I(dve, lambda: nc.vector.tensor_tensor(out=M.den[:], in0=M.den[:], in1=M.dsub[:], op=ALU.max), R=[M.b_dsub], W=[M.b_den])
                I(dve, lambda: nc.vector.tensor_tensor(out=M.den[:], in0=M.den[:], in1=M.emt[:], op=ALU.max), R=[M.b_emt], W=[M.b_den])
                I(dve, lambda: nc.vector.reciprocal(out=M.r[:], in_=M.den[:]), R=[M.b_den], W=[M.b_r])
                yield None
                for hp in range(2):
                    for hh in range(2):
                        h = 2 * hp + hh
                        I(pe, lambda h=h, hh=hh: nc.tensor.matmul(psF[1][0:L, hh * 256:(hh + 1) * 256], M.PT[:, h, :], vB[:, h, :], start=True, stop=True),
                          R=[M.b_PT, b_vB], W=[bF[1]], sig=(hh == 1))
                    for hh in range(2):
                        h = 2 * hp + hh
                        for half in range(2):
                            I(pe, lambda h=h, hh=hh, half=half: nc.tensor.matmul(psF[2][0:L, hh * 256:(hh + 1) * 256], qT(h, half), St.Cbf[:, half, h, :], start=(half == 0), stop=(half == 1)),
                              R=[b_qT, St.b_Cbf], W=[bF[2]], sig=(hh == 1 and half == 1))
                    for hh in range(2):
                        h = 2 * hp + hh
                        I(act, lambda h=h, hh=hh: nc.scalar.activation(out=M.qc[:, hh, :], in_=psF[2][0:L, hh * 256:(hh + 1) * 256], func=AF.Copy, scale=M.wi[:, h:h + 1]),
                          R=[M.b_wi], W=[M.b_qc, bF[2]])
                    I(dve, lambda hp=hp: nc.vector.tensor_tensor(out=M.num[:, 2 * hp:2 * hp + 2, :], in0=psF[1][0:L, 0:512].rearrange("p (a v) -> p a v", a=2), in1=M.qc[:], op=ALU.add),
                      R=[M.b_qc], W=[M.b_num, bF[1]])
                yield None
                for h in range(4):
                    I(act, lambda h=h: nc.scalar.activation(out=M.junk[:], in_=M.num[:, h, :], func=AF.Square, scale=M.r[:, h:h + 1], accum_out=M.ssq[:, h:h + 1]),
                      R=[M.b_num, M.b_r], W=[M.b_junk, M.b_ssq])
                I(act, lambda: nc.scalar.activation(out=M.rsd[:], in_=M.ssq[:], func=AF.Ln, scale=1.0 / 256, bias=epsT[0:L, :]), R=[M.b_ssq, b_small], W=[M.b_rsd])
                I(act, lambda: nc.scalar.activation(out=M.rsd[:], in_=M.rsd[:], func=AF.Exp, scale=-0.5), W=[M.b_rsd])
                I(dve, lambda: nc.vector.tensor_tensor(out=M.fac[:], in0=M.r[:], in1=M.rsd[:], op=ALU.mult), R=[M.b_r, M.b_rsd], W=[M.b_fac])
                for h in range(4):
                    I(dve, lambda h=h: nc.vector.scalar_tensor_tensor(out=ob[:, h * 256:(h + 1) * 256], in0=M.num[:, h, :], scalar=M.fac[:, h:h + 1], in1=sgg[:, h * 256:(h + 1) * 256], op0=ALU.mult, op1=ALU.mult),
                      R=[M.b_num, M.b_fac, b_sgg], W=[b_ob])
            yield None
            I(dve, lambda: nc.vector.tensor_tensor(out=M.dsub[:], in0=M.g[:], in1=M.aLs[0:L, 0:4], op=ALU.subtract), R=[M.b_g, M.b_aLs], W=[M.b_dsub])
            I(act, lambda: nc.scalar.activation(out=M.wst[:], in_=M.dsub[:], func=AF.Exp), R=[M.b_dsub], W=[M.b_wst])
            I(dve, lambda: nc.vector.tensor_tensor(out=M.dsb[:], in0=St.mB[:], in1=M.aLs[:, 0:4], op=ALU.subtract), R=[St.b_m, M.b_aLs], W=[M.b_dsb])
            I(act, lambda: nc.scalar.activation(out=M.dec[:], in_=M.dsb[:], func=AF.Exp), R=[M.b_dsb], W=[M.b_dec])
            for h in range(4):
                for half in range(2):
                    idx = 2 * h + half
                    I(pe, lambda h=h, half=half, idx=idx: nc.tensor.transpose(psT[1][0:L, idx * 128:(idx + 1) * 128], kT(h, half), identB[:]),
                      R=[b_kT, b_cb], W=[bT[1]], sig=(idx == 7))
            for h in range(4):
                I(dve, lambda h=h: nc.vector.tensor_scalar(out=M.kTW[:, 2 * h:2 * h + 2, :], in0=psT[1][0:L, 2 * h * 128:(2 * h + 2) * 128].rearrange("p (a d) -> p a d", a=2),
                                                           scalar1=M.wst[:, h:h + 1], scalar2=1.0 / 16, op0=ALU.mult, op1=ALU.mult),
                  R=[M.b_wst], W=[M.b_kTW, bT[1]])
            yield None
            for half in range(2):
                for hp in range(2):
                    bk = 1 + hp
                    for hh in range(2):
                        h = 2 * hp + hh
                        I(pe, lambda h=h, hh=hh, half=half, bk=bk: nc.tensor.matmul(psF[bk][:, hh * 256:(hh + 1) * 256], M.kTW[:, 2 * h + half, :], vB[:, h, :], start=True, stop=True),
                          R=[M.b_kTW, b_vB], W=[bF[bk]], sig=(hh == 1))
                    for hh in range(2):
                        h = 2 * hp + hh
                        I(dve, lambda h=h, hh=hh, half=half, bk=bk: nc.vector.scalar_tensor_tensor(out=St.C32[:, half, h, :], in0=St.C32[:, half, h, :], scalar=M.dec[:, h:h + 1],
                                                                                               in1=psF[bk][:, hh * 256:(hh + 1) * 256], op0=ALU.mult, op1=ALU.add),
                          R=[M.b_dec], W=[St.b_C32, bF[bk]])
            yield None
            for half in range(2):
                for h in range(4):
                    c0 = 24 + half * 4 + h
                    I(pe, lambda h=h, half=half, c0=c0: nc.tensor.matmul(psF[0][:, c0:c0 + 1], M.kTW[:, 2 * h + half, :], onesB[0:L, 0:1], start=True, stop=True),
                      R=[M.b_kTW, b_cb], W=[bF[0]], sig=(half == 1 and h == 3))
            for half in range(2):
                I(dve, lambda half=half: nc.vector.tensor_tensor(out=St.n32[:, half, :], in0=St.n32[:, half, :], in1=M.dec[:], op=ALU.mult), R=[M.b_dec], W=[St.b_n32])
            I(dve, lambda: nc.vector.tensor_tensor(out=St.n32[:].rearrange("p a h -> p (a h)"), in0=St.n32[:].rearrange("p a h -> p (a h)"), in1=psF[0][:, 24:32], op=ALU.add),
              W=[St.b_n32, bF[0]])
            state_refresh_bf(St)
            I(dve, lambda: nc.vector.tensor_copy(out=St.mB[:], in_=M.aLs[:, 4:8]), R=[M.b_aLs], W=[St.b_m])

        def rms_rstd(ssq_ap, out_ap, n, scale, b_in, b_out):
            I(act, lambda: nc.scalar.activation(out=out_ap, in_=ssq_ap, func=AF.Ln, scale=scale, bias=epsT[0:n, :]), R=[b_in, b_small], W=[b_out])
            I(act, lambda: nc.scalar.activation(out=out_ap, in_=out_ap, func=AF.Exp, scale=-0.5), W=[b_out])

        def qknorm(n, bank, gB, out_h, junk, b_junk, ssq4, rs4, b_s4, b_out):
            for h in range(4):
                I(act, lambda h=h: nc.scalar.activation(out=junk[0:n, 0:128], in_=psF[bank][0:n, h * 128:(h + 1) * 128], func=AF.Square, accum_out=ssq4[0:n, h:h + 1]),
                  W=[b_junk, b_s4, bF[bank]])
            rms_rstd(ssq4[0:n, :], rs4[0:n, :], n, 1.0 / 128, b_s4, b_s4)
            for h in range(4):
                I(dve, lambda h=h: nc.vector.scalar_tensor_tensor(out=out_h(h), in0=psF[bank][0:n, h * 128:(h + 1) * 128], scalar=rs4[0:n, h:h + 1], in1=gB[0:n, :], op0=ALU.mult, op1=ALU.mult),
                  R=[b_s4, b_small], W=[b_out, bF[bank]])

        def norm_transpose(n, x_ap, b_x, xn, b_xn, ssx, rsx, b_sx, hT_dst, b_hT, Amod, Bmod, ts, valid=False):
            I(act, lambda: nc.scalar.activation(out=xn[0:n, :], in_=x_ap, func=AF.Square, accum_out=ssx[0:n, :]), R=[b_x], W=[b_xn, b_sx])
            rms_rstd(ssx[0:n, :], rsx[0:n, :], n, 1.0 / D, b_sx, b_sx)
            I(dve, lambda: nc.vector.tensor_scalar(out=xn[0:n, :], in0=x_ap, scalar1=rsx[0:n, 0:1], scalar2=None, op0=ALU.mult), R=[b_x, b_sx], W=[b_xn])
            for rnd in range(2):
                for c in range(8 * rnd, 8 * rnd + 8):
                    I(pe, lambda c=c: nc.tensor.transpose(psT[0][:, (c % 8) * 128:(c % 8) * 128 + n], xn[0:n, c * 128:(c + 1) * 128], identB[0:n, 0:n]),
                      R=[b_xn, b_cb], W=[bT[0]], sig=(c % 8 == 7))
                for c in range(8 * rnd, 8 * rnd + 8):
                    src = psT[0][:, (c % 8) * 128:(c % 8) * 128 + n]
                    if c % 2 == 0:
                        I(act, lambda c=c, src=src: nc.scalar.activation(out=hT_dst(c), in_=src, func=AF.Identity, scale=Amod[:, c, ts:ts + 1], bias=Bmod[:, c, ts:ts + 1]),
                          R=[b_mod], W=[b_hT, bT[0]])
                    else:
                        I(dve, lambda c=c, src=src: nc.vector.tensor_scalar(out=hT_dst(c), in0=src, scalar1=Amod[:, c, ts:ts + 1], scalar2=Bmod[:, c, ts:ts + 1], op0=ALU.mult, op1=ALU.add),
                          R=[b_mod], W=[b_hT, bT[0]])

        def load_w(tile, b_tile, src, b_src):
            return dma(sp, tile[:], src, R=[b_src], W=[b_tile])

        def stage_A():
            with ExitStack() as SA:
                xin = K.sb(SA, "xin", (128, D), F32); b_xin = Buf()
                xnb = K.sb(SA, "xnb", (128, D), BF16); b_xnb = Buf()
                ssx = K.sb(SA, "ssx", (128, 1), F32); rsx = K.sb(SA, "rsx", (128, 1), F32); b_sx = Buf()
                hT = K.sb(SA, "hT", (128, 16, 512), BF16); b_hT = Buf()
                wt = [K.sb(SA, "wt%d" % i, (128, 16, 512), BF16) for i in range(2)]; b_wt = [Buf(), Buf()]
                kst = K.sb(SA, "kst", (128, 4, 128), F32); b_kst = Buf()
                vst = K.sb(SA, "vst", (128, 4, 128), F32); b_vst = Buf()
                knb = K.sb(SA, "knb", (128, 512), BF16); b_knb = Buf()
                junkq = K.sb(SA, "junkq", (128, 128), F32); b_junkq = Buf()
                ssq4 = K.sb(SA, "ssq4", (128, 4), F32); rs4 = K.sb(SA, "rs4", (128, 4), F32); b_s4 = Buf()
                kTst = K.sb(SA, "kTst", (128, 8, 512), BF16); b_kTst = Buf()
                vbst = K.sb(SA, "vbst", (128, 8, 4, 128), BF16); b_vbst = Buf()
                raw = K.sb(SA, "raw", (128, 16, 515), BF16); b_raw = Buf()
                rawl = K.sb(SA, "rawl", (128, 16, 3), F32); b_rawl = Buf()
                cvy = [K.sb(SA, "cvy%d" % i, (128, 512), F32) for i in range(2)]; b_cvy = [Buf(), Buf()]
                qTt2 = [K.sb(SA, "qTt%d" % i, (128, 8, 2, 128), BF16) for i in range(2)]; b_qTt2 = [Buf(), Buf()]
                kTt2 = [K.sb(SA, "kTt%d" % i, (128, 8, 512), BF16) for i in range(2)]; b_kTt2 = [Buf(), Buf()]
                vBt2 = [K.sb(SA, "vBt%d" % i, (128, 4, 4, 256), BF16) for i in range(2)]; b_vBt2 = [Buf(), Buf()]
                gts2 = [K.sb(SA, "gts%d" % i, (128, 4, 8), F32) for i in range(2)]; b_gts2 = [Buf(), Buf()]
                sgg2 = [K.sb(SA, "sgg%d" % i, (128, 2, 1024), BF16) for i in range(2)]; b_sgg2 = [Buf(), Buf()]
                sg32 = K.sb(SA, "sg32", (128, 512), F32); b_sg32 = Buf()
                obt = K.sb(SA, "obt", (128, 1024), BF16); b_obt = Buf()
                qTst = K.sb(SA, "qTst", (128, 8, 256), BF16); b_qTst = Buf()
                oTst = K.sb(SA, "oTst", (128, 8, 256), BF16); b_oTst = Buf()
                M = mk_mlstm(SA, 128)
                St = mk_state(SA)
                for tl, bb in ((St.C32, St.b_C32), (St.n32, St.b_n32), (St.mB, St.b_m)):
                    I(dve, lambda tl=tl: nc.vector.memset(tl[:], 0.0), W=[bb])
                state_refresh_bf(St)
                I(dve, lambda: nc.vector.memset(raw[:, :, 0:3], 0.0), W=[b_raw])

                wsl = [0]

                def genA(G):
                    for j in range(4):
                        blk = 4 * G + j
                        dma(sp, xin[:], xs[blk * 128:(blk + 1) * 128, :], W=[b_xin])
                        yield None
                        norm_transpose(128, xin[:], b_xin, xnb, b_xnb, ssx, rsx, b_sx,
                                       lambda c, j=j: hT[:, c, j * 128:(j + 1) * 128], b_hT, A1, modT, 0)
                        if blk == 0:
                            I(dve, lambda: nc.vector.tensor_scalar(out=hT[:, :, 0:128], in0=hT[:, :, 0:128], scalar1=cmt[:, 0:1], scalar2=None, op0=ALU.mult), R=[b_cm], W=[b_hT])
                    for cg in A_ORDER:
                        s = wsl[0] % 2
                        wsl[0] += 1
                        ncol = 512 if cg < 14 else 8
                        dma(sp, wt[s][:, :, 0:ncol], winS[cg, :, :, 0:ncol], R=[b_win[cg]], W=[b_wt[s]])
                        if cg in (6, 7, 8, 9):
                            yield 'need_raw'
                            for cc in range(4):
                                yield None
                                ci = (cg - 6) * 4 + cc
                                bk = next_bank()
                                for kc in range(16):
                                    I(pe, lambda kc=kc, cc=cc, s=s, bk=bk: nc.tensor.matmul(psF[bk][:], wt[s][:, kc, cc * 128:(cc + 1) * 128], hT[:, kc, :], start=(kc == 0), stop=(kc == 15)),
                                      R=[b_wt[s], b_hT], W=[bF[bk]], sig=(kc == 15))
                                I(act, lambda ci=ci, bk=bk: nc.scalar.copy(out=raw[:, ci, 3:515], in_=psF[bk][:]), W=[b_raw, bF[bk]])
                                if G == cfg.ngrp - 1:
                                    I(dve, lambda ci=ci, bk=bk: nc.vector.tensor_copy(out=rawl[:, ci, :], in_=psF[bk][:, 509:512]), W=[b_rawl, bF[bk]])
                            continue
                        blocks = (1, 3) if cg in (0, 1, 12, 13) else (0, 1, 2, 3)
                        for j in blocks:
                            yield None
                            blk = 4 * G + j
                            bk = next_bank()
                            for kc in range(16):
                                I(pe, lambda kc=kc, j=j, s=s, bk=bk, ncol=ncol: nc.tensor.matmul(psF[bk][:, 0:ncol], hT[:, kc, j * 128:(j + 1) * 128], wt[s][:, kc, 0:ncol], start=(kc == 0), stop=(kc == 15)),
                                  R=[b_wt[s], b_hT], W=[bF[bk]], sig=(kc == 15))
                            if cg in (2, 3):
                                h0 = (cg - 2) * 4
                                qknorm(128, bk, gkB, lambda h: kst[:, h, :], junkq, b_junkq, ssq4, rs4, b_s4, b_kst)
                                I(act, lambda: nc.scalar.copy(out=knb[:], in_=kst[:].rearrange("p h d -> p (h d)")), R=[b_kst], W=[b_knb])
                                for h in range(4):
                                    I(pe, lambda h=h: nc.tensor.transpose(psT[0][:, h * 128:(h + 1) * 128], knb[:, h * 128:(h + 1) * 128], identB[:]), R=[b_knb, b_cb], W=[bT[0]], sig=(h == 3))
                                I(dve, lambda h0=h0, j=j: nc.vector.tensor_copy(out=kTst[:, h0:h0 + 4, j * 128:(j + 1) * 128], in_=psT[0][:, 0:512].rearrange("p (h t) -> p h t", h=4)),
                                  W=[b_kTst, bT[0]])
                                dma(act, k_o.rearrange("h t d -> t h d")[blk * 128:(blk + 1) * 128, h0:h0 + 4, :], kst[:], R=[b_kst])
                            elif cg in (4, 5):
                                h0 = (cg - 4) * 4
                                I(act, lambda bk=bk: nc.scalar.copy(out=vst[:].rearrange("p h d -> p (h d)"), in_=psF[bk][:]), W=[b_vst, bF[bk]])
                                I(dve, lambda h0=h0, j=j: nc.vector.tensor_copy(out=vbst[:, h0:h0 + 4, j, :], in_=vst[:]), R=[b_vst], W=[b_vbst])
                                dma(act, v_o.rearrange("h t d -> t h d")[blk * 128:(blk + 1) * 128, h0:h0 + 4, :], vst[:], R=[b_vst])
                            elif cg in (10, 11):
                                h0 = (cg - 10) * 2
                                I(act, lambda h0=h0, j=j, bk=bk: nc.scalar.copy(out=vBt2[G % 2][:, j, h0:h0 + 2, :].rearrange("p h v -> p (h v)"), in_=psF[bk][:]), W=[b_vBt2[G % 2], bF[bk]])
                            elif cg == 14:
                                I(dve, lambda j=j, bk=bk: nc.vector.tensor_copy(out=gts2[G % 2][:, j, :], in_=psF[bk][:, 0:8]), W=[b_gts2[G % 2], bF[bk]])
                            elif cg in (0, 1):
                                h0 = cg * 4
                                jo = j // 2
                                qknorm(128, bk, gqB, lambda h: knb[:, h * 128:(h + 1) * 128], junkq, b_junkq, ssq4, rs4, b_s4, b_knb)
                                for h in range(4):
                                    I(pe, lambda h=h: nc.tensor.transpose(psT[0][:, h * 128:(h + 1) * 128], knb[:, h * 128:(h + 1) * 128], identB[:]), R=[b_knb, b_cb], W=[bT[0]], sig=(h == 3))
                                I(dve, lambda h0=h0, jo=jo: nc.vector.tensor_copy(out=qTst[:, h0:h0 + 4, jo * 128:(jo + 1) * 128], in_=psT[0][:, 0:512].rearrange("p (h t) -> p h t", h=4)),
                                  W=[b_qTst, bT[0]])
                            elif cg in (12, 13):
                                jo = j // 2
                                c0 = (cg - 12) * 512
                                I(act, lambda bk=bk: nc.scalar.activation(out=sg32[:], in_=psF[bk][:], func=AF.Sigmoid), W=[b_sg32, bF[bk]])
                                I(dve, lambda jo=jo, c0=c0: nc.vector.tensor_tensor(out=sgg2[G % 2][:, jo, c0:c0 + 512], in0=sg32[:], in1=ghB[:, c0:c0 + 512], op=ALU.mult), R=[b_sg32, b_small], W=[b_sgg2[G % 2]])
                    dma(act, KTs.rearrange("h d t -> d h t")[:, :, G * 512:(G + 1) * 512], kTst[:], R=[b_kTst])
                    dma(act, Vss.rearrange("h p b d -> p h b d")[:, :, 4 * G:4 * G + 4, :], vbst[:], R=[b_vbst])
                    dma(act, QTs.rearrange("h d t -> d h t")[:, :, G * 256:(G + 1) * 256], qTst[:], R=[b_qTst])

                def genB(G):
                    for ci in range(16):
                        yield None
                        yb = cvy[ci % 2]; b_y = b_cvy[ci % 2]
                        I(dve, lambda ci=ci, yb=yb: nc.vector.tensor_scalar(out=yb[:], in0=raw[:, ci, 0:512], scalar1=wcT[:, ci, 0:1], scalar2=None, op0=ALU.mult), R=[b_raw, b_small], W=[b_y])
                        for jt in range(1, 4):
                            I(dve, lambda ci=ci, yb=yb, jt=jt: nc.vector.scalar_tensor_tensor(out=yb[:], in0=raw[:, ci, jt:jt + 512], scalar=wcT[:, ci, jt:jt + 1], in1=yb[:], op0=ALU.mult, op1=ALU.add),
                              R=[b_raw, b_small], W=[b_y])
                        if ci < 8:
                            I(act, lambda ci=ci, yb=yb: nc.scalar.activation(out=qTt2[G % 2][:, ci, :, :], in_=yb[:].rearrange("p (j t) -> p j t", j=4)[:, 1::2, :], func=AF.Silu, bias=bcT[:, ci:ci + 1]),
                              R=[b_y, b_small], W=[b_qTt2[G % 2]])
                        else:
                            I(act, lambda ci=ci, yb=yb: nc.scalar.activation(out=kTt2[G % 2][:, ci - 8, :], in_=yb[:], func=AF.Silu, bias=bcT[:, ci:ci + 1]),
                              R=[b_y, b_small], W=[b_kTt2[G % 2]])
                    I(dve, lambda: nc.vector.tensor_copy(out=raw[:, :, 0:3], in_=raw[:, :, 512:515]), W=[b_raw])
                    yield 'conv_done'
                    for j in range(4):
                        blk = 4 * G + j
                        own = (j % 2 == 1)
                        jo = j // 2
                        for _ in mlstm_block(M, St, own,
                                    lambda h, half, j=j: qTt2[G % 2][:, 2 * h + half, j // 2, :], b_qTt2[G % 2],
                                    lambda h, half, j=j: kTt2[G % 2][:, 2 * h + half, j * 128:(j + 1) * 128], b_kTt2[G % 2],
                                    vBt2[G % 2][:, j, :, :], b_vBt2[G % 2], gts2[G % 2][:, j, :], b_gts2[G % 2],
                                    sgg2[G % 2][:, jo, :], b_sgg2[G % 2], obt, b_obt, dummy=(blk == 0), sel="sel127"):
                            yield None
                        yield None
                        if own:
                            for c in range(8):
                                I(pe, lambda c=c: nc.tensor.transpose(psT[1][:, c * 128:(c + 1) * 128], obt[:, c * 128:(c + 1) * 128], identB[:]), R=[b_obt, b_cb], W=[bT[1]], sig=(c == 7))
                            I(act, lambda jo=jo: nc.scalar.copy(out=oTst[:, :, jo * 128:(jo + 1) * 128], in_=psT[1][:, :].rearrange("p (c t) -> p c t", c=8)), W=[b_oTst, bT[1]])
                    dma(act, OTs.rearrange("c d t -> d c t")[:, 8:16, G * 256:(G + 1) * 256], oTst[:], R=[b_oTst])

                def drive(gA, gB):
                    a_done = gA is None
                    b_done = gB is None
                    conv_done = b_done
                    while not (a_done and b_done):
                        if not a_done:
                            try:
                                tag = next(gA)
                            except StopIteration:
                                a_done = True
                                tag = None
                            if tag == 'need_raw':
                                while not conv_done and not b_done:
                                    try:
                                        tb_ = next(gB)
                                    except StopIteration:
                                        b_done = True
                                        break
                                    if tb_ == 'conv_done':
                                        conv_done = True
                        if not b_done:
                            try:
                                tb_ = next(gB)
                            except StopIteration:
                                b_done = True
                                tb_ = None
                            if tb_ == 'conv_done':
                                conv_done = True

                drive(genA(0), None)
                for G in range(cfg.ngrp):
                    drive(genA(G + 1) if G + 1 < cfg.ngrp else None, genB(G))
                for a_ in range(2):
                    dma(act, C_o.rearrange("h (a p) v -> a p h v", p=128)[a_], St.C32[:, a_, :, :], R=[St.b_C32])
                with nc.allow_non_contiguous_dma(reason="tiny state vectors"):
                    for a_ in range(2):
                        dma(act, n_o.rearrange("h (a p) -> a p h", p=128)[a_], St.n32[:, a_, :], R=[St.b_n32])
                    dma(act, m_o, St.mB[0:1, :], R=[St.b_m])
                    for r_ in range(3):
                        dma(act, conv_o[r_].rearrange("(c p) -> p c", p=128), rawl[:, :, r_], R=[b_rawl])
                K.barrier(scr)

        def mk_attn(stack, NQ):
            At = type("At", (), {})()

            def t(name, shape, dt, nslot):
                setattr(At, name, [K.sb(stack, "a_%s%d" % (name, i), shape, dt) for i in range(nslot)])
                setattr(At, "b_" + name, [Buf() for _ in range(nslot)])
            t("E", (128, NQ), F32, 3); t("SP", (128, NQ), BF16, 3); t("X", (128, NQ), F32, 2); t("A", (128, NQ), BF16, 3)
            t("ACC", (128, NQ), F32, 2); t("ACCb", (128, NQ), BF16, 3)
            At.zero = K.sb(stack, "a_zero", (128, max(NQ, 128)), BF16); At.b_zero = Buf()
            I(dve, lambda: nc.vector.memset(At.zero[:], 0.0), W=[At.b_zero])
            At.NQ = NQ
            return At

        def attn_head(At, Kslice, b_K, Vslice, b_V, Qt, b_Q, kb_list, zb, tb, ob, extraR=()):
            NQ = At.NQ
            nk = len(kb_list)
            I(pe, lambda: nc.tensor.matmul(psF[ob][:, 0:NQ], At.zero[:, 0:128], At.zero[:, 0:NQ], start=True, stop=False), R=[At.b_zero], W=[bF[ob]])
            for s_ in range(3):
                I(dve, lambda: nc.vector.memset(At.ACCb[s_][:], 0.0), W=[At.b_ACCb[s_]])

            def rng(i):
                kb, nkeys, q0, diag, kbias = kb_list[i]
                return kb, nkeys, q0, diag, kbias

            def stage1a(i):
                kb, nkeys, q0, diag, kbias = rng(i)
                E, SP = At.E[i % 3], At.SP[i % 3]
                bE, bSP = At.b_E[i % 3], At.b_SP[i % 3]
                I(pe, lambda: nc.tensor.matmul(psF[zb][0:nkeys, q0:NQ], Kslice(kb), Qt[:, q0:NQ], start=True, stop=True), R=[b_K, b_Q] + list(extraR), W=[bF[zb]])
                if kbias:
                    I(act, lambda: nc.scalar.activation(out=E[0:nkeys, q0:NQ], in_=psF[zb][0:nkeys, q0:NQ], func=AF.Exp, scale=SCALE_A, bias=cmt[0:nkeys, 1:2]), R=[b_cm], W=[bE, bF[zb]])
                else:
                    I(act, lambda: nc.scalar.activation(out=E[0:nkeys, q0:NQ], in_=psF[zb][0:nkeys, q0:NQ], func=AF.Exp, scale=SCALE_A), W=[bE, bF[zb]])
                if diag:
                    dq = min(128, NQ - q0)
                    I(dve, lambda: nc.vector.tensor_tensor(out=E[0:nkeys, q0:q0 + dq], in0=E[0:nkeys, q0:q0 + dq], in1=cF["stri01"][0:nkeys, 0:dq], op=ALU.mult), R=[b_cst], W=[bE])

            def stage1b(i):
                kb, nkeys, q0, diag, kbias = rng(i)
                E, SP = At.E[i % 3], At.SP[i % 3]
                bE, bSP = At.b_E[i % 3], At.b_SP[i % 3]
                I(act, lambda: nc.scalar.activation(out=SP[0:nkeys, q0:NQ], in_=E[0:nkeys, q0:NQ], func=AF.Ln, bias=1.0), R=[bE], W=[bSP])
                if i < nk - 1:
                    I(dve, lambda: nc.vector.tensor_tensor(out=At.ACCb[(i + 1) % 3][0:nkeys, q0:NQ], in0=At.ACCb[i % 3][0:nkeys, q0:NQ], in1=SP[0:nkeys, q0:NQ], op=ALU.add),
                      R=[bSP, At.b_ACCb[i % 3]], W=[At.b_ACCb[(i + 1) % 3]])

            def stage2(i):
                kb, nkeys, q0, diag, kbias = rng(i)
                E, SP, X, A = At.E[i % 3], At.SP[i % 3], At.X[i % 2], At.A[i % 3]
                bE, bSP, bX, bA = At.b_E[i % 3], At.b_SP[i % 3], At.b_X[i % 2], At.b_A[i % 3]
                I(pe, lambda: nc.tensor.matmul(psF[tb][0:nkeys, q0:NQ], trigeB[0:nkeys, 0:nkeys], SP[0:nkeys, q0:NQ], start=True, stop=(i == 0)), R=[bSP, b_cb], W=[bF[tb]], sig=(i == 0))
                if i > 0:
                    I(pe, lambda: nc.tensor.matmul(psF[tb][0:nkeys, q0:NQ], onesB[:, 0:nkeys], At.ACCb[i % 3][:, q0:NQ], start=False, stop=True), R=[At.b_ACCb[i % 3], b_cb], W=[bF[tb]])
                I(act, lambda: nc.scalar.activation(out=X[0:nkeys, q0:NQ], in_=psF[tb][0:nkeys, q0:NQ], func=AF.Exp, scale=-1.0), W=[bX, bF[tb]])
                I(dve, lambda: nc.vector.tensor_tensor(out=A[0:nkeys, q0:NQ], in0=X[0:nkeys, q0:NQ], in1=E[0:nkeys, q0:NQ], op=ALU.mult), R=[bX, bE], W=[bA])

            def stage3(i):
                kb, nkeys, q0, diag, kbias = rng(i)
                A, bA = At.A[i % 3], At.b_A[i % 3]
                I(pe, lambda: nc.tensor.matmul(psF[ob][:, q0:NQ], Vslice(kb), A[0:nkeys, q0:NQ], start=False, stop=(i == nk - 1)), R=[b_V, bA] + list(extraR), W=[bF[ob]])

            stage1a(0)
            stage1b(0)
            for i in range(nk):
                if i + 1 < nk:
                    stage1a(i + 1)
                if i >= 2:
                    stage3(i - 2)
                stage2(i)
                if i + 1 < nk:
                    stage1b(i + 1)
            if nk >= 2:
                stage3(nk - 2)
            stage3(nk - 1)

        def stage_B():
            with ExitStack() as SB:
                NQ = 512
                At = mk_attn(SB, NQ)
                Kt = [K.sb(SB, "Kt%d" % i, (128, T), BF16) for i in range(2)]; b_Kt = [Buf(), Buf()]
                Vt = [K.sb(SB, "Vt%d" % i, (128, NB, 128), BF16) for i in range(2)]; b_Vt = [Buf(), Buf()]
                Qt = [K.sb(SB, "Qt%d" % i, (128, NQ), BF16) for i in range(2)]; b_Qt = [Buf(), Buf()]
                oTa = [K.sb(SB, "oTa%d" % i, (128, NQ), BF16) for i in range(2)]; b_oTa = [Buf(), Buf()]
                it = 0
                for g in range(cfg.nown // 4):
                    nkb = 8 * g + 8
                    kb_list = []
                    for kb in reversed(range(nkb)):
                        r = kb - 8 * g
                        if r < 1:
                            q0, diag = 0, False
                        elif r % 2 == 1:
                            q0, diag = (r // 2) * 128, True
                        else:
                            q0, diag = (r // 2) * 128, False
                        kb_list.append((kb, 128, q0, diag, kb == 0))
                    for h in range(8):
                        s = it % 2
                        it += 1
                        dma(sp, Kt[s][:, 0:nkb * 128], KTs[h, :, 0:nkb * 128], W=[b_Kt[s]])
                        dma(sp, Vt[s][:, 0:nkb, :], Vss[h, :, 0:nkb, :], W=[b_Vt[s]])
                        dma(sp, Qt[s][:], QTs[h, :, g * NQ:(g + 1) * NQ], W=[b_Qt[s]])
                        zb, tb, ob = (0, 1, 2) if s == 0 else (3, 4, 5)
                        attn_head(At, lambda kb, s=s: Kt[s][:, kb * 128:(kb + 1) * 128], b_Kt[s],
                                  lambda kb, s=s: Vt[s][:, kb, :], b_Vt[s], Qt[s], b_Qt[s], kb_list, zb, tb, ob)
                        I(act, lambda s=s, ob=ob: nc.scalar.copy(out=oTa[s][:], in_=psF[ob][:, 0:NQ]), W=[b_oTa[s], bF[ob]])
                        dma(sp, OTs[h, :, g * NQ:(g + 1) * NQ], oTa[s][:], R=[b_oTa[s]])
                K.barrier(scr)

        def ffn_tail(stack, ntok_blocks, n, x1, b_x1, h2T, b_h2T, gt2Bt, b_gt2, y_dst, ts, tmp):
            NT = ntok_blocks * n if ntok_blocks > 1 else n
            actT, b_actT, wt, b_wt, yst, b_yst = tmp
            wi_ = [0]
            for cg in range(16):
                s = wi_[0] % 2; wi_[0] += 1
                dma(sp, wt[s][:], wff1S[cg], R=[b_wff1[cg]], W=[b_wt[s]])
                for cc in range(4):
                    hc = cg * 4 + cc
                    bk = next_bank()
                    for kc in range(16):
                        I(pe, lambda kc=kc, cc=cc, s=s, bk=bk: nc.tensor.matmul(psF[bk][:, 0:NT], wt[s][:, kc, cc * 128:(cc + 1) * 128], h2T[:, kc, 0:NT], start=(kc == 0), stop=(kc == 15)),
                          R=[b_wt[s], b_h2T], W=[bF[bk]], sig=(kc == 15))
                    I(act, lambda hc=hc, bk=bk: nc.scalar.activation(out=actT[:, hc, 0:NT], in_=psF[bk][:, 0:NT], func=AF.Relu), W=[b_actT, bF[bk]])
                    I(pool, lambda hc=hc: nc.gpsimd.tensor_tensor(out=actT[:, hc, 0:NT], in0=actT[:, hc, 0:NT], in1=actT[:, hc, 0:NT], op=ALU.mult), W=[b_actT])
            banks = [0, 1, 2, 3]
            for cg in range(4):
                for kq in range(4):
                    s = wi_[0] % 2; wi_[0] += 1
                    dma(sp, wt[s][:], wff2S[cg, kq], R=[b_wff2[cg][kq]], W=[b_wt[s]])
                    for j in range(ntok_blocks):
                        for kc in range(16):
                            hc = kq * 16 + kc
                            I(pe, lambda kc=kc, hc=hc, j=j, s=s: nc.tensor.matmul(psF[banks[j]][0:n, :], actT[:, hc, j * n:(j + 1) * n], wt[s][:, kc, :], start=(hc == 0), stop=(hc == 63)),
                              R=[b_wt[s], b_actT], W=[bF[banks[j]]], sig=(kc == 15))
                for j in range(ntok_blocks):
                    I(dve, lambda j=j, cg=cg: nc.vector.tensor_tensor(out=yst[0:n, :], in0=psF[banks[j]][0:n, :], in1=gt2Bt(j)[0:n, cg * 512:(cg + 1) * 512], op=ALU.mult),
                      R=[b_gt2], W=[b_yst, bF[banks[j]]])
                    I(dve, lambda j=j, cg=cg: nc.vector.tensor_tensor(out=x1[0:n, j, cg * 512:(cg + 1) * 512], in0=x1[0:n, j, cg * 512:(cg + 1) * 512], in1=yst[0:n, :], op=ALU.add),
                      R=[b_yst], W=[b_x1])
            for j in range(ntok_blocks):
                dma(act, y_dst(j), x1[0:n, j, :], R=[b_x1])

        def out_proj(ntok_blocks, n, oT_j, b_oT, x1, b_x1, gt1Bt, b_gt1, wt, b_wt, yst, b_yst):
            for cg in range(4):
                s = cg % 2
                dma(sp, wt[s][:], woutS[cg], R=[b_wout[cg]], W=[b_wt[s]])
                for j in range(ntok_blocks):
                    bk = next_bank()
                    for kc in range(16):
                        I(pe, lambda kc=kc, j=j, s=s, bk=bk: nc.tensor.matmul(psF[bk][0:n, :], oT_j(j, kc), wt[s][:, kc, :], start=(kc == 0), stop=(kc == 15)),
                          R=[b_oT, b_wt[s]], W=[bF[bk]], sig=(kc == 15))
                    I(dve, lambda cg=cg, bk=bk, j=j: nc.vector.tensor_tensor(out=yst[0:n, :], in0=psF[bk][0:n, :], in1=gt1Bt(j)[0:n, cg * 512:(cg + 1) * 512], op=ALU.mult),
                      R=[b_gt1], W=[b_yst, bF[bk]])
                    I(dve, lambda cg=cg, j=j: nc.vector.tensor_tensor(out=x1[0:n, j, cg * 512:(cg + 1) * 512], in0=x1[0:n, j, cg * 512:(cg + 1) * 512], in1=yst[0:n, :], op=ALU.add),
                      R=[b_yst], W=[b_x1])

        def stage_C():
            with ExitStack() as SC:
                gt1B = K.sb(SC, "gt1B", (128, D), F32); gt2B = K.sb(SC, "gt2B", (128, D), F32); b_gtB = Buf()
                with ExitStack() as S1:
                    row_bcast(S1, gt1B, modT[:, 32:48, 0], b_gtB)
                    row_bcast(S1, gt2B, modT[:, 80:96, 0], b_gtB)
                    K.barrier(scr)
                oT = K.sb(SC, "oT", (128, 16, 512), BF16); b_oT = Buf()
                x1 = K.sb(SC, "x1", (128, 4, D), F32); b_x1 = Buf()
                xnb = K.sb(SC, "xnbC", (128, D), BF16); b_xnb = Buf()
                ssx = K.sb(SC, "ssxC", (128, 1), F32); rsx = K.sb(SC, "rsxC", (128, 1), F32); b_sx = Buf()
                h2T = K.sb(SC, "h2T", (128, 16, 512), BF16); b_h2T = Buf()
                actT = K.sb(SC, "actT", (128, 64, 512), BF16); b_actT = Buf()
                wt = [K.sb(SC, "wtC%d" % i, (128, 16, 512), BF16) for i in range(2)]; b_wt = [Buf(), Buf()]
                yst = K.sb(SC, "yst", (128, 512), F32); b_yst = Buf()
                for g in range(cfg.nown // 4):
                    dma(sp, oT[:], OTs.rearrange("c d t -> d c t")[:, :, g * 512:(g + 1) * 512], W=[b_oT])
                    for j in range(4):
                        blk = 2 * (4 * g + j) + 1
                        dma(sp, x1[:, j, :], xs[blk * 128:(blk + 1) * 128, :], W=[b_x1])
                    out_proj(4, 128, lambda j, kc: oT[:, kc, j * 128:(j + 1) * 128], b_oT, x1, b_x1, lambda j: gt1B, b_gtB, wt, b_wt, yst, b_yst)
                    for j in range(4):
                        norm_transpose(128, x1[:, j, :], b_x1, xnb, b_xnb, ssx, rsx, b_sx,
                                       lambda c, j=j: h2T[:, c, j * 128:(j + 1) * 128], b_h2T, A2, modT[:, 48:64, :], 0)
                    ffn_tail(SC, 4, 128, x1, b_x1, h2T, b_h2T, lambda j: gt2B, b_gtB,
                             lambda j, g=g: y_o[(4 * g + j) * 128:(4 * g + j + 1) * 128, :], 0, (actT, b_actT, wt, b_wt, yst, b_yst))
                K.barrier(scr)

        def stage_S():
            n = TS
            nkb = PAST // 128
            with ExitStack() as SS:
                wt = [K.sb(SS, "wtS%d" % i, (128, 16, 512), BF16) for i in range(2)]; b_wt = [Buf(), Buf()]
                xinS = K.sb(SS, "xinS", (n, NS, D), F32); b_xinS = Buf()
                xnb = K.sb(SS, "xnbS", (n, D), BF16); b_xnb = Buf()
                ssx = K.sb(SS, "ssxS", (n, 1), F32); rsx = K.sb(SS, "rsxS", (n, 1), F32); b_sx = Buf()
                hTS = K.sb(SS, "hTS", (128, NS, 16, n), BF16); b_hTS = Buf()
                kstS = K.sb(SS, "kstS", (n, 4, 128), F32); b_kstS = Buf()
                vstS = K.sb(SS, "vstS", (n, 4, 128), F32); b_vstS = Buf()
                knbS = K.sb(SS, "knbS", (n, 512), BF16); b_knbS = Buf()
                junkq = K.sb(SS, "junkqS", (n, 128), F32); b_junkq = Buf()
                ssq4 = K.sb(SS, "ssq4S", (n, 4), F32); rs4 = K.sb(SS, "rs4S", (n, 4), F32); b_s4 = Buf()
                kTn = K.sb(SS, "kTn", (128, NS, 8, n), BF16); b_kTn = Buf()
                qTn = K.sb(SS, "qTn", (128, NS, 8, n), BF16); b_qTn = Buf()
                vnw = K.sb(SS, "vnw", (n, NS, 8, 128), BF16); b_vnw = Buf()
                rawS = K.sb(SS, "rawS", (128, NS, 16, 3 + n), BF16); b_rawS = Buf()
                rawH = K.sb(SS, "rawH", (128, NS, 16, 3), F32); b_rawH = Buf()
                rawlS = K.sb(SS, "rawlS", (128, NS, 16, 3), F32); b_rawlS = Buf()
                cvy = K.sb(SS, "cvyS", (128, n), F32); b_cvy = Buf()
                qTtS = K.sb(SS, "qTtS", (128, NS, 8, n), BF16); b_qTtS = Buf()
                kTtS = K.sb(SS, "kTtS", (128, NS, 8, n), BF16); b_kTtS = Buf()
                vBS = K.sb(SS, "vBS", (n, NS, 4, 256), BF16); b_vBS = Buf()
                gtsS = K.sb(SS, "gtsS", (n, NS, 8), F32); b_gtsS = Buf()
                sg32 = K.sb(SS, "sg32S", (n, 512), F32); b_sg32 = Buf()
                sggS = K.sb(SS, "sggS", (n, NS, 1024), BF16); b_sggS = Buf()
                obtS = K.sb(SS, "obtS", (n, 1024), BF16); b_obtS = Buf()
                oTS = K.sb(SS, "oTS", (128, NS, 16, n), BF16); b_oTS = Buf()
                M16 = mk_mlstm(SS, n)
                St = mk_state(SS)
                dma(sp, xinS[:], xsamp.rearrange("(s t) f -> t s f", t=n), W=[b_xinS])
                with nc.allow_non_contiguous_dma(reason="conv state transposed load (tiny)"):
                    for sb_ in range(NS):
                        for r_ in range(3):
                            dma(sp, rawH[:, sb_, :, r_], sconv[sb_, r_].rearrange("(c p) -> p c", p=128), W=[b_rawH])
                I(dve, lambda: nc.vector.tensor_copy(out=rawS[:, :, :, 0:3], in_=rawH[:]), R=[b_rawH], W=[b_rawS])
                for sb_ in range(NS):
                    norm_transpose(n, xinS[:, sb_, :], b_xinS, xnb, b_xnb, ssx, rsx, b_sx,
                                   lambda c, sb_=sb_: hTS[:, sb_, c, :], b_hTS, A1, modT, 1 + sb_)
                ckpt('S1')
                wsl = [0]
                for cg in A_ORDER:
                    s = wsl[0] % 2
                    wsl[0] += 1
                    ncol = 512 if cg < 14 else 8
                    dma(sp, wt[s][:, :, 0:ncol], winS[cg, :, :, 0:ncol], R=[b_win[cg]], W=[b_wt[s]])
                    for sb_ in range(NS):
                        if cg in (6, 7, 8, 9):
                            for cc in range(4):
                                ci = (cg - 6) * 4 + cc
                                bk = next_bank()
                                for kc in range(16):
                                    I(pe, lambda: nc.tensor.matmul(psF[bk][:, 0:n], wt[s][:, kc, cc * 128:(cc + 1) * 128], hTS[:, sb_, kc, :], start=(kc == 0), stop=(kc == 15)),
                                      R=[b_wt[s], b_hTS], W=[bF[bk]], sig=(kc == 15))
                                I(act, lambda: nc.scalar.copy(out=rawS[:, sb_, ci, 3:3 + n], in_=psF[bk][:, 0:n]), W=[b_rawS, bF[bk]])
                                I(dve, lambda: nc.vector.tensor_copy(out=rawlS[:, sb_, ci, :], in_=psF[bk][:, n - 3:n]), W=[b_rawlS, bF[bk]])
                            continue
                        bk = next_bank()
                        for kc in range(16):
                            I(pe, lambda: nc.tensor.matmul(psF[bk][0:n, 0:ncol], hTS[:, sb_, kc, :], wt[s][:, kc, 0:ncol], start=(kc == 0), stop=(kc == 15)),
                              R=[b_wt[s], b_hTS], W=[bF[bk]], sig=(kc == 15))
                        if cg in (2, 3):
                            h0 = (cg - 2) * 4
                            qknorm(n, bk, gkB, lambda h: kstS[:, h, :], junkq, b_junkq, ssq4, rs4, b_s4, b_kstS)
                            I(act, lambda: nc.scalar.copy(out=knbS[:], in_=kstS[:].rearrange("p h d -> p (h d)")), R=[b_kstS], W=[b_knbS])
                            for h in range(4):
                                I(pe, lambda: nc.tensor.transpose(psT[1][:, h * 128:h * 128 + n], knbS[:, h * 128:(h + 1) * 128], identB[0:n, 0:n]), R=[b_knbS, b_cb], W=[bT[1]], sig=(h == 3))
                            I(dve, lambda: nc.vector.tensor_copy(out=kTn[:, sb_, h0:h0 + 4, :], in_=psT[1][:, 0:512].rearrange("p (h t) -> p h t", h=4)[:, :, 0:n]), W=[b_kTn, bT[1]])
                            dma(act, ks_o[sb_].rearrange("h t d -> t h d")[:, h0:h0 + 4, :], kstS[:], R=[b_kstS])
                        elif cg in (4, 5):
                            h0 = (cg - 4) * 4
                            I(act, lambda: nc.scalar.copy(out=vstS[:].rearrange("p h d -> p (h d)"), in_=psF[bk][0:n, :]), W=[b_vstS, bF[bk]])
                            I(dve, lambda: nc.vector.tensor_copy(out=vnw[:, sb_, h0:h0 + 4, :], in_=vstS[:]), R=[b_vstS], W=[b_vnw])
                            dma(act, vs_o[sb_].rearrange("h t d -> t h d")[:, h0:h0 + 4, :], vstS[:], R=[b_vstS])
                        elif cg in (10, 11):
                            h0 = (cg - 10) * 2
                            I(act, lambda: nc.scalar.copy(out=vBS[:, sb_, h0:h0 + 2, :].rearrange("p h v -> p (h v)"), in_=psF[bk][0:n, :]), W=[b_vBS, bF[bk]])
                        elif cg == 14:
                            I(dve, lambda: nc.vector.tensor_copy(out=gtsS[:, sb_, :], in_=psF[bk][0:n, 0:8]), W=[b_gtsS, bF[bk]])
                        elif cg in (0, 1):
                            h0 = cg * 4
                            qknorm(n, bk, gqB, lambda h: knbS[:, h * 128:(h + 1) * 128], junkq, b_junkq, ssq4, rs4, b_s4, b_knbS)
                            for h in range(4):
                                I(pe, lambda: nc.tensor.transpose(psT[1][:, h * 128:h * 128 + n], knbS[:, h * 128:(h + 1) * 128], identB[0:n, 0:n]), R=[b_knbS, b_cb], W=[bT[1]], sig=(h == 3))
                            I(dve, lambda: nc.vector.tensor_copy(out=qTn[:, sb_, h0:h0 + 4, :], in_=psT[1][:, 0:512].rearrange("p (h t) -> p h t", h=4)[:, :, 0:n]), W=[b_qTn, bT[1]])
                        elif cg in (12, 13):
                            c0 = (cg - 12) * 512
                            I(act, lambda: nc.scalar.activation(out=sg32[:], in_=psF[bk][0:n, :], func=AF.Sigmoid), W=[b_sg32, bF[bk]])
                            I(dve, lambda: nc.vector.tensor_tensor(out=sggS[:, sb_, c0:c0 + 512], in0=sg32[:], in1=ghB[0:n, c0:c0 + 512], op=ALU.mult), R=[b_sg32, b_small], W=[b_sggS])
                ckpt('S2')
                with nc.allow_non_contiguous_dma(reason="conv state transposed store (tiny)"):
                    for sb_ in range(NS):
                        for r_ in range(3):
                            dma(act, convs_o[sb_, r_].rearrange("(c p) -> p c", p=128), rawlS[:, sb_, :, r_], R=[b_rawlS])
                for sb_ in range(NS):
                    for ci in range(16):
                        I(dve, lambda: nc.vector.tensor_scalar(out=cvy[:], in0=rawS[:, sb_, ci, 0:n], scalar1=wcT[:, ci, 0:1], scalar2=None, op0=ALU.mult), R=[b_rawS, b_small], W=[b_cvy])
                        for jt in range(1, 4):
                            I(dve, lambda: nc.vector.scalar_tensor_tensor(out=cvy[:], in0=rawS[:, sb_, ci, jt:jt + n], scalar=wcT[:, ci, jt:jt + 1], in1=cvy[:], op0=ALU.mult, op1=ALU.add),
                              R=[b_rawS, b_small], W=[b_cvy])
                        dst = qTtS[:, sb_, ci, :] if ci < 8 else kTtS[:, sb_, ci - 8, :]
                        I(act, lambda: nc.scalar.activation(out=dst, in_=cvy[:], func=AF.Silu, bias=bcT[:, ci:ci + 1]), R=[b_cvy, b_small], W=[b_qTtS if ci < 8 else b_kTtS])
                ckpt('S3')
                for sb_ in range(NS):
                    for a_ in range(2):
                        dma(sp, St.C32[:, a_, :, :], sC[sb_].rearrange("h (a p) v -> a p h v", p=128)[a_], W=[St.b_C32])
                    with nc.allow_non_contiguous_dma(reason="tiny state vectors"):
                        for a_ in range(2):
                            dma(sp, St.n32[:, a_, :], sn[sb_].rearrange("h (a p) -> a p h", p=128)[a_], W=[St.b_n32])
                        dma(sp, St.mB[:], sm[sb_:sb_ + 1, :].partition_broadcast(128).rearrange("p a d -> p (a d)"), W=[St.b_m])
                    state_refresh_bf(St)
                    for _ in mlstm_block(M16, St, True,
                                lambda h, half: qTtS[:, sb_, 2 * h + half, :], b_qTtS,
                                lambda h, half: kTtS[:, sb_, 2 * h + half, :], b_kTtS,
                                vBS[:, sb_, :, :], b_vBS, gtsS[:, sb_, :], b_gtsS,
                                sggS[:, sb_, :], b_sggS, obtS, b_obtS, dummy=False, sel="sel15"):
                        pass
                    for c in range(8):
                        I(pe, lambda: nc.tensor.transpose(psT[1][:, c * 128:c * 128 + n], obtS[:, c * 128:(c + 1) * 128], identB[0:n, 0:n]), R=[b_obtS, b_cb], W=[bT[1]], sig=(c == 7))
                    I(act, lambda: nc.scalar.copy(out=oTS[:, sb_, 8:16, :], in_=psT[1][:, :].rearrange("p (c t) -> p c t", c=8)[:, :, 0:n]), W=[b_oTS, bT[1]])
                    for a_ in range(2):
                        dma(act, Cs_o[sb_].rearrange("h (a p) v -> a p h v", p=128)[a_], St.C32[:, a_, :, :], R=[St.b_C32])
                    with nc.allow_non_contiguous_dma(reason="tiny state vectors"):
                        for a_ in range(2):
                            dma(act, ns_o[sb_].rearrange("h (a p) -> a p h", p=128)[a_], St.n32[:, a_, :], R=[St.b_n32])
                        dma(act, ms_o[sb_:sb_ + 1, :], St.mB[0:1, :], R=[St.b_m])
                ckpt('S4')
                with ExitStack() as SAT:
                    At = mk_attn(SAT, n)
                    Kc = [K.sb(SAT, "Kc%d" % i, (128, nkb, 128), BF16) for i in range(2)]; b_Kc = [Buf(), Buf()]
                    Vc = [K.sb(SAT, "Vc%d" % i, (128, nkb, 128), BF16) for i in range(2)]; b_Vc = [Buf(), Buf()]
                    KcT = [K.sb(SAT, "KcT%d" % i, (128, nkb * 128), BF16) for i in range(2)]; b_KcT = [Buf(), Buf()]
                    it = 0
                    for sb_ in range(NS):
                        for h in range(8):
                            s = it % 2
                            it += 1
                            for b0 in range(0, nkb, 16):
                                b1 = min(nkb, b0 + 16)
                                dma(pool, Kc[s][:, b0:b1, :], ck[sb_, h].rearrange("(b p) d -> p b d", p=128)[:, b0:b1, :], W=[b_Kc[s]])
                                dma(pool, Vc[s][:, b0:b1, :], cv[sb_, h].rearrange("(b p) d -> p b d", p=128)[:, b0:b1, :], W=[b_Vc[s]])
                            for kb in range(nkb):
                                I(pe, lambda: nc.tensor.transpose(psT[s][:, (kb % 8) * 128:(kb % 8 + 1) * 128], Kc[s][:, kb, :], identB[:]), R=[b_Kc[s], b_cb], W=[bT[s]], sig=(kb % 8 == 7 or kb == nkb - 1))
                                if kb % 8 == 7 or kb == nkb - 1:
                                    k0 = (kb // 8) * 8
                                    w = (kb - k0 + 1) * 128
                                    I(dve, lambda: nc.vector.tensor_copy(out=KcT[s][:, k0 * 128:k0 * 128 + w], in_=psT[s][:, 0:w]), W=[b_KcT[s], bT[s]])
                            kb_list = [("new", n, 0, True, False)] + [(kb, 128, 0, False, False) for kb in reversed(range(nkb))]
                            zb, tb, ob = (0, 1, 2) if s == 0 else (3, 4, 5)
                            Ksl = lambda kb: kTn[:, sb_, h, :] if kb == "new" else KcT[s][:, kb * 128:(kb + 1) * 128]
                            Vsl = lambda kb: vnw[:, sb_, h, :] if kb == "new" else Vc[s][:, kb, :]
                            attn_head(At, Ksl, b_KcT[s], Vsl, b_Vc[s], qTn[:, sb_, h, :], b_qTn, kb_list, zb, tb, ob, extraR=[b_kTn, b_vnw])
                            I(act, lambda: nc.scalar.copy(out=oTS[:, sb_, h, :], in_=psF[ob][:, 0:n]), W=[b_oTS, bF[ob]])
                    K.barrier(scr)
                ckpt('S5')
                gtS = [[K.sb(SS, "gtS%d_%d" % (i, j), (128, D), F32) for j in range(NS)] for i in range(2)]; b_gtS = Buf()
                with ExitStack() as S1:
                    for sb_ in range(NS):
                        row_bcast(S1, gtS[0][sb_], modT[:, 32:48, 1 + sb_], b_gtS)
                        row_bcast(S1, gtS[1][sb_], modT[:, 80:96, 1 + sb_], b_gtS)
                    K.barrier(scr)
                ckpt('S6')
                yst = K.sb(SS, "ystS", (n, 512), F32); b_yst = Buf()
                h2TS = K.sb(SS, "h2TS", (128, 16, NS * n), BF16); b_h2TS = Buf()
                actT = K.sb(SS, "actTS", (128, 64, NS * n), BF16); b_actT = Buf()
                out_proj(NS, n, lambda j, kc: oTS[:, j, kc, :], b_oTS, xinS, b_xinS, lambda j: gtS[0][j], b_gtS, wt, b_wt, yst, b_yst)
                ckpt('S7')
                for sb_ in range(NS):
                    norm_transpose(n, xinS[:, sb_, :], b_xinS, xnb, b_xnb, ssx, rsx, b_sx,
                                   lambda c, sb_=sb_: h2TS[:, c, sb_ * n:(sb_ + 1) * n], b_h2TS, A2, modT[:, 48:64, :], 1 + sb_)
                ckpt('S8')
                ffn_tail(SS, NS, n, xinS, b_xinS, h2TS, b_h2TS, lambda j: gtS[1][j], b_gtS,
                         lambda j: ys_o[j * n:(j + 1) * n, :], 0, (actT, b_actT, wt, b_wt, yst, b_yst))
                K.barrier(scr)

        ckpt("setup")
        stage_A()
        ckpt("A")
        stage_B()
        ckpt("B")
        stage_C()
        ckpt("C")
        stage_S()


_PROG_CACHE = {}


def _core_inputs(c, inp, cfg, consts):
    b, p = c // 2, c % 2
    f = lambda a: np.ascontiguousarray(np.asarray(a, dtype=np.float32))
    x = np.asarray(inp["x_prompt"][b], dtype=np.float32)
    if p == 1:
        xs = x
    else:
        xs = np.concatenate([np.zeros((128, D), np.float32), x[:cfg.T - 128]], axis=0)
    cm = np.zeros((128, 2), np.float32)
    if p == 1:
        cm[:, 0] = 1.0
    else:
        cm[:, 1] = NEG
    ns = cfg.ns
    sl = slice(ns * c, ns * c + ns)
    m = {
        "xs": f(xs), "cm": cm,
        "cvec": f(np.concatenate([np.asarray(inp["c_prompt"])[b:b + 1], np.asarray(inp["c_sample"])[sl]], axis=0)),
        "xsamp": f(np.asarray(inp["x_sample"])[sl].reshape(ns * cfg.ts, D)),
        "ck": f(np.asarray(inp["cache_k"])[0, sl]), "cv": f(np.asarray(inp["cache_v"])[0, sl]),
        "sC": f(np.asarray(inp["state_C"])[0, sl]), "sn": f(np.asarray(inp["state_n"])[0, sl]),
        "sm": f(np.asarray(inp["state_m"])[0, sl]), "sconv": f(np.asarray(inp["state_conv"])[0, sl]),
        "consts": consts,
    }
    for k in ("w_ada", "b_ada", "g_norm1", "w_in", "g_q", "g_k", "w_conv", "b_conv", "b_i", "b_f", "g_h",
              "w_out", "g_norm2", "w_ff1", "w_ff2"):
        m[k] = f(np.asarray(inp[k])[0])
    return m


def run_cores(inp, cores):
    B, SEQ, _ = inp["x_prompt"].shape
    DEC_B, TS, _ = inp["x_sample"].shape
    PAST = inp["cache_k"].shape[3]
    cfg = Cfg(nblk=SEQ // 128, past=PAST, ts=TS, ns=DEC_B // 8)
    key = (cfg.nblk, cfg.past, cfg.ts, cfg.ns)
    if key not in _PROG_CACHE:
        _PROG_CACHE[key] = build_program(cfg)
    nc = _PROG_CACHE[key]
    consts = make_consts()
    in_maps = [_core_inputs(c, inp, cfg, consts) for c in cores]
    res = run_bass_kernel_spmd(nc, in_maps, core_ids=list(range(len(cores))))
    return cfg, res.results


def kernel(**inp):
    B, SEQ, _ = inp["x_prompt"].shape
    DEC_B, TS, _ = inp["x_sample"].shape
    cfg, res = run_cores(inp, list(range(8)))
    ns = cfg.ns
    y_prompt = np.zeros((B, SEQ, D), np.float32)
    k_prompt = np.zeros((1, B, 8, SEQ, 128), np.float32)
    v_prompt = np.zeros((1, B, 8, SEQ, 128), np.float32)
    C_prompt = np.zeros((1, B, 4, 256, 256), np.float32)
    n_prompt = np.zeros((1, B, 4, 256), np.float32)
    m_prompt = np.zeros((1, B, 4), np.float32)
    conv_prompt = np.zeros((1, B, 3, 2048), np.float32)
    y_sample = np.zeros((DEC_B, TS, D), np.float32)
    k_sample = np.zeros((1, DEC_B, 8, TS, 128), np.float32)
    v_sample = np.zeros((1, DEC_B, 8, TS, 128), np.float32)
    C_sample = np.zeros((1, DEC_B, 4, 256, 256), np.float32)
    n_sample = np.zeros((1, DEC_B, 4, 256), np.float32)
    m_sample = np.zeros((1, DEC_B, 4), np.float32)
    conv_sample = np.zeros((1, DEC_B, 3, 2048), np.float32)
    for c in range(8):
        b, p = c // 2, c % 2
        r = res[c]
        yb = y_prompt[b].reshape(cfg.nblk, 128, D)
        yo = r["y"].reshape(cfg.nown, 128, D)
        if p == 1:
            yb[1::2] = yo
            k_prompt[0, b] = r["ko"]
            v_prompt[0, b] = r["vo"]
            C_prompt[0, b] = r["Co"]
            n_prompt[0, b] = r["no"]
            m_prompt[0, b] = r["mo"][0]
            conv_prompt[0, b] = r["convo"]
        else:
            yb[0::2] = yo
        sl = slice(ns * c, ns * c + ns)
        y_sample[sl] = r["ys"].reshape(ns, TS, D)
        k_sample[0, sl] = r["kso"]
        v_sample[0, sl] = r["vso"]
        C_sample[0, sl] = r["Cso"]
        n_sample[0, sl] = r["nso"]
        m_sample[0, sl] = r["mso"]
        conv_sample[0, sl] = r["convso"]
    return (y_prompt, y_sample, k_prompt, v_prompt, C_prompt, n_prompt, m_prompt, conv_prompt,
            k_sample, v_sample, C_sample, n_sample, m_sample, conv_sample)
```

```python
from contextlib import ExitStack

import numpy as np

import concourse.bass as bass
import concourse.mybir as mybir
from concourse.bass_utils import run_bass_kernel_spmd

F32 = mybir.dt.float32
BF16 = mybir.dt.bfloat16
AF = mybir.ActivationFunctionType
ALU = mybir.AluOpType
AX = mybir.AxisListType

D = 2048
DIN = 7176
DFF = 8192
EPS = 1e-6
NEG = -30000.0
NCG = 15


class StopBuild(Exception):
    pass


class Cfg:
    def __init__(self, nblk=64, past=4096, ts=16, ns=2):
        self.nblk = nblk
        self.past = past
        self.ts = ts
        self.ns = ns
        self.T = nblk * 128
        self.nown = nblk // 2
        self.Town = self.nown * 128
        self.ngrp = nblk // 4


class Eng:
    def __init__(self, k, e, name):
        self.k = k
        self.e = e
        self.name = name
        self.sem = k.es.enter_context(k.nc.semaphore("s_" + name))
        self.count = 0
        self.seen = {}

    def wait(self, tok):
        if tok is None:
            return
        src, v = tok
        if src is self and self.name == "pe":
            return
        if self.seen.get(id(src), 0) >= v:
            return
        self.e.wait_ge(src.sem, v)
        self.seen[id(src)] = v

    def signal(self, ins):
        self.count += 1
        ins.then_inc(self.sem, 1)
        return (self, self.count)


class DmaSem:
    def __init__(self, k, name):
        self.sem = k.es.enter_context(k.nc.semaphore("d_" + name))
        self.count = 0


class Buf:
    def __init__(self, name=""):
        self.name = name
        self.w = None
        self.r = {}

    def note_read(self, tok):
        self.r[id(tok[0])] = tok

    def note_write(self, tok):
        self.w = tok
        self.r = {}


class Builder:
    def __init__(self, nc, es):
        self.nc = nc
        self.es = es
        self.pe = Eng(self, nc.tensor, "pe")
        self.act = Eng(self, nc.scalar, "act")
        self.dve = Eng(self, nc.vector, "dve")
        self.pool = Eng(self, nc.gpsimd, "pool")
        self.sp = Eng(self, nc.sync, "sp")
        self.pe_flush = None
        self.dsems = []
        self.ring = []
        self.ring_i = 0
        self.nt = 0

    def sb(self, stack, name, shape, dt):
        self.nt += 1
        return stack.enter_context(self.nc.sbuf_tensor("%s_%d" % (name, self.nt), list(shape), dt))

    def ps(self, stack, name, shape, dt):
        self.nt += 1
        return stack.enter_context(self.nc.psum_tensor("%s_%d" % (name, self.nt), list(shape), dt))

    def dsem(self, name):
        d = DmaSem(self, name + str(len(self.dsems)))
        self.dsems.append(d)
        return d

    def I(self, eng, fn, R=(), W=(), sig=True):
        for b in R:
            eng.wait(b.w)
        for b in W:
            eng.wait(b.w)
            for t in list(b.r.values()):
                eng.wait(t)
        ins = fn()
        if not hasattr(eng, "pend"):
            eng.pend = ([], [])
        if sig:
            tok = eng.signal(ins)
            if eng.name == "pe" and self.pe_flush is not None:
                self.nc.tensor.ldweights(self.pe_flush)
            for b in list(R) + eng.pend[0]:
                b.note_read(tok)
            for b in list(W) + eng.pend[1]:
                b.note_write(tok)
            eng.pend = ([], [])
            return tok
        eng.pend[0].extend(R)
        eng.pend[1].extend(W)
        return None

    def dma(self, q, out, in_, R=(), W=(), ds=None, **kw):
        for b in R:
            q.wait(b.w)
        for b in W:
            q.wait(b.w)
            for t in list(b.r.values()):
                q.wait(t)
        if ds is None:
            if len(self.ring) < 24:
                ds = self.dsem("r")
                self.ring.append(ds)
            else:
                ds = self.ring[self.ring_i % len(self.ring)]
                self.ring_i += 1
                q.wait((ds, ds.count))
        ins = q.e.dma_start(out=out, in_=in_, **kw)
        ins.then_inc(ds.sem, 16)
        ds.count += 16
        tok = (ds, ds.count)
        for b in R:
            b.note_read(tok)
        for b in W:
            b.note_write(tok)
        return tok

    def barrier(self, sc):
        nc = self.nc
        for d in self.dsems:
            if d.count:
                self.sp.wait((d, d.count))
        t_pe = self.I(self.pe, lambda: nc.tensor.matmul(sc["ps"][0:1, 0:2], sc["one_b"][0:1, 0:1], sc["one_b"][0:1, 0:2], start=True, stop=True), W=[sc["ps_buf"]])
        t_act = self.act.signal(nc.scalar.copy(out=sc["sa"][0:1, 0:1], in_=sc["one_f"][0:1, 0:1]))
        t_dve = self.dve.signal(nc.vector.tensor_copy(out=sc["sd"][0:1, 0:1], in_=sc["one_f"][0:1, 0:1]))
        t_pool = self.pool.signal(nc.gpsimd.tensor_copy(out=sc["sp"][0:1, 0:1], in_=sc["one_f"][0:1, 0:1]))
        for t in (t_pe, t_act, t_dve, t_pool):
            self.sp.wait(t)
        self.sp.count += 1
        nc.sync.sem_inc(self.sp.sem, 1)
        t_sp = (self.sp, self.sp.count)
        for e in (self.pe, self.act, self.dve, self.pool):
            e.wait(t_sp)
        return t_sp


CONST_NAMES = ["ident", "ones", "triinc", "causneg", "sel127", "sel15", "stri01", "trige", "causnegT"]


def make_consts():
    i = np.arange(128)
    c = {}
    c["ident"] = np.eye(128, dtype=np.float32)
    c["ones"] = np.ones((128, 128), np.float32)
    c["triinc"] = (i[:, None] <= i[None, :]).astype(np.float32)
    c["causneg"] = np.where(i[None, :] <= i[:, None], 0.0, NEG).astype(np.float32)
    s = np.zeros((128, 128), np.float32); s[127, :] = 1.0
    c["sel127"] = s
    s = np.zeros((128, 128), np.float32); s[15, :] = 1.0
    c["sel15"] = s
    c["stri01"] = (i[:, None] < i[None, :]).astype(np.float32)
    c["trige"] = (i[:, None] >= i[None, :]).astype(np.float32)
    c["causnegT"] = np.ascontiguousarray(c["causneg"].T)
    return np.concatenate([c[n] for n in CONST_NAMES], axis=1)


def build_program(cfg):
    nc = bass.Bass("TRN2", target_bir_lowering=False)
    try:
        _build_body(nc, cfg)
    except StopBuild:
        pass
    return nc


def _build_body(nc, cfg):
    T, Town, NB, NS, TS, PAST = cfg.T, cfg.Town, cfg.nblk, cfg.ns, cfg.ts, cfg.past
    NTOKS = NS * TS

    def din(name, shape):
        return nc.dram_tensor(name, list(shape), F32, kind="ExternalInput").ap()

    def dout(name, shape):
        return nc.dram_tensor(name, list(shape), F32, kind="ExternalOutput").ap()

    def dscr(name, shape, dt=BF16):
        return nc.dram_tensor(name, list(shape), dt).ap()

    xs = din("xs", (T, D))
    cmd = din("cm", (128, 2))
    cvec = din("cvec", (1 + NS, D))
    xsamp = din("xsamp", (NTOKS, D))
    ck = din("ck", (NS, 8, PAST, 128))
    cv = din("cv", (NS, 8, PAST, 128))
    sC = din("sC", (NS, 4, 256, 256))
    sn = din("sn", (NS, 4, 256))
    sm = din("sm", (NS, 4))
    sconv = din("sconv", (NS, 3, 2048))
    w_ada = din("w_ada", (D, 6 * D))
    b_ada = din("b_ada", (6 * D,))
    g_norm1 = din("g_norm1", (D,))
    w_in = din("w_in", (D, DIN))
    g_q = din("g_q", (128,))
    g_k = din("g_k", (128,))
    w_conv = din("w_conv", (4, 2048))
    b_conv = din("b_conv", (2048,))
    b_i = din("b_i", (4,))
    b_f = din("b_f", (4,))
    g_h = din("g_h", (4, 256))
    w_out = din("w_out", (D, D))
    g_norm2 = din("g_norm2", (D,))
    w_ff1 = din("w_ff1", (D, DFF))
    w_ff2 = din("w_ff2", (DFF, D))
    constd = din("consts", (128, 128 * len(CONST_NAMES)))

    y_o = dout("y", (Town, D))
    k_o = dout("ko", (8, T, 128))
    v_o = dout("vo", (8, T, 128))
    C_o = dout("Co", (4, 256, 256))
    n_o = dout("no", (4, 256))
    m_o = dout("mo", (1, 4))
    conv_o = dout("convo", (3, 2048))
    ys_o = dout("ys", (NTOKS, D))
    ks_o = dout("kso", (NS, 8, TS, 128))
    vs_o = dout("vso", (NS, 8, TS, 128))
    Cs_o = dout("Cso", (NS, 4, 256, 256))
    ns_o = dout("nso", (NS, 4, 256))
    ms_o = dout("mso", (NS, 4))
    convs_o = dout("convso", (NS, 3, 2048))

    winS = dscr("winS", (NCG, 128, 16, 512))
    woutS = dscr("woutS", (4, 128, 16, 512))
    wff1S = dscr("wff1S", (16, 128, 16, 512))
    wff2S = dscr("wff2S", (4, 4, 128, 16, 512))
    KTs = dscr("KTs", (8, 128, T))
    Vss = dscr("Vss", (8, 128, NB, 128))
    QTs = dscr("QTs", (8, 128, Town))
    OTs = dscr("OTs", (16, 128, Town))

    with ExitStack() as es:
        K = Builder(nc, es)
        I, dma = K.I, K.dma
        pe, act, dve, pool, sp = K.pe, K.act, K.dve, K.pool, K.sp
        P = es

        cst = K.sb(P, "cst", (128, 128 * len(CONST_NAMES)), F32)
        b_cst = Buf("cst")
        cF = {n: cst[:, i * 128:(i + 1) * 128] for i, n in enumerate(CONST_NAMES)}
        identB = K.sb(P, "identB", (128, 128), BF16)
        onesB = K.sb(P, "onesB", (128, 128), BF16)
        trigeB = K.sb(P, "trigeB", (128, 128), BF16)
        b_cb = Buf("cstb")
        cmt = K.sb(P, "cmt", (128, 2), F32)
        b_cm = Buf("cm")
        modT = K.sb(P, "modT", (128, 96, 1 + NS), F32)
        A1 = K.sb(P, "A1", (128, 16, 1 + NS), F32)
        A2 = K.sb(P, "A2", (128, 16, 1 + NS), F32)
        b_mod = Buf("mod")
        gqB = K.sb(P, "gqB", (128, 128), F32)
        gkB = K.sb(P, "gkB", (128, 128), F32)
        ghB = K.sb(P, "ghB", (128, 1024), F32)
        biB = K.sb(P, "biB", (128, 4), F32)
        bfB = K.sb(P, "bfB", (128, 4), F32)
        wcT = K.sb(P, "wcT", (128, 16, 4), F32)
        bcT = K.sb(P, "bcT", (128, 16), F32)
        b_small = Buf("small")
        epsT = K.sb(P, "epsT", (128, 1), F32)
        scr = {"one_f": cF["ones"], "one_b": onesB,
               "sa": K.sb(P, "bsa", (1, 2), F32), "sd": K.sb(P, "bsd", (1, 2), F32), "sp": K.sb(P, "bsp", (1, 2), F32)}
        psF = [K.ps(P, "psF%d" % i, (128, 512), F32) for i in range(6)]
        bF = [Buf("psF%d" % i) for i in range(6)]
        psT = [K.ps(P, "psT%d" % i, (128, 1024), BF16) for i in range(2)]
        bT = [Buf("psT%d" % i) for i in range(2)]
        scr["ps"] = psF[0]
        scr["ps_buf"] = bF[0]
        flushw = K.sb(P, "flushw", (128, 2), BF16)
        nc.vector.memset(flushw[:], 0.0)
        K.pe_flush = flushw[:, 0:1]
        import os as _os
        _stop = _os.environ.get("KSTOP", "")

        def ckpt(tag):
            if _stop == tag:
                K.barrier(scr)
                raise StopBuild()

        dma(sp, cst[:], constd, W=[b_cst])
        dma(sp, cmt[:], cmd, W=[b_cm])
        I(dve, lambda: nc.vector.tensor_copy(out=identB[:], in_=cF["ident"]), R=[b_cst], W=[b_cb])
        I(dve, lambda: nc.vector.tensor_copy(out=onesB[:], in_=cF["ones"]), R=[b_cst], W=[b_cb])
        I(dve, lambda: nc.vector.tensor_copy(out=trigeB[:], in_=cF["trige"]), R=[b_cst], W=[b_cb])
        I(dve, lambda: nc.vector.memset(epsT[:], EPS), W=[b_small])
        dma(sp, gqB[:], g_q.unsqueeze(0).partition_broadcast(128).rearrange("p a d -> p (a d)"), W=[b_small])
        dma(sp, gkB[:], g_k.unsqueeze(0).partition_broadcast(128).rearrange("p a d -> p (a d)"), W=[b_small])
        dma(sp, ghB[:], g_h.rearrange("h d -> (h d)").unsqueeze(0).partition_broadcast(128).rearrange("p a d -> p (a d)"), W=[b_small])
        dma(sp, biB[:], b_i.unsqueeze(0).partition_broadcast(128).rearrange("p a d -> p (a d)"), W=[b_small])
        dma(sp, bfB[:], b_f.unsqueeze(0).partition_broadcast(128).rearrange("p a d -> p (a d)"), W=[b_small])

        ckpt("s1")
        b_win = [Buf("win%d" % i) for i in range(NCG)]
        b_wout = [Buf("wout%d" % i) for i in range(4)]
        b_wff1 = [Buf("wff1%d" % i) for i in range(16)]
        b_wff2 = [[Buf("wff2%d_%d" % (i, j)) for j in range(4)] for i in range(4)]
        cvt_sems = [K.dsem("cvt") for _ in range(3)]
        ncv = [0]

        def convert(dst, src, buf):
            ds = cvt_sems[ncv[0] % 3]
            ncv[0] += 1
            pool.wait((ds, ds.count))
            dma(pool, dst, src, W=[buf], ds=ds)

        A_ORDER = [2, 3, 4, 5, 8, 9, 10, 11, 14, 6, 7, 0, 1, 12, 13]
        for cg in A_ORDER:
            ncol = 512 if cg < 14 else 8
            convert(winS[cg, :, :, 0:ncol],
                    w_in[:, cg * 512:cg * 512 + ncol].rearrange("(kc p) n -> p kc n", p=128), b_win[cg])

        ckpt("s2")
        with ExitStack() as S0:
            cT = K.sb(S0, "cT", (1 + NS, D), F32)
            b_cT = Buf()
            dma(sp, cT[:], cvec, W=[b_cT])
            I(act, lambda: nc.scalar.activation(out=cT[:], in_=cT[:], func=AF.Silu), R=[], W=[b_cT])
            scT = K.sb(S0, "scT", (128, 16, 1 + NS), F32)
            b_scT = Buf()
            nr = 1 + NS
            for c in range(16):
                I(pe, lambda c=c: nc.tensor.transpose(psF[0][:, c * nr:(c + 1) * nr], cT[:, c * 128:(c + 1) * 128], cF["ident"][0:nr, 0:nr]),
                  R=[b_cT, b_cst], W=[bF[0]], sig=(c == 15))
            I(dve, lambda: nc.vector.tensor_copy(out=scT[:].rearrange("p a b -> p (a b)"), in_=psF[0][:, 0:16 * nr]), R=[bF[0]], W=[b_scT])
            ckpt("s3")
            rowsT = K.sb(S0, "rowsT", (128, 128), F32)
            b_rows = Buf()
            dma(sp, rowsT[0:96, :], b_ada.rearrange("(c p) -> c p", p=128), W=[b_rows])
            dma(sp, rowsT[96:112, :], g_norm1.rearrange("(c p) -> c p", p=128), W=[b_rows])
            dma(sp, rowsT[112:128, :], g_norm2.rearrange("(c p) -> c p", p=128), W=[b_rows])
            rows2 = K.sb(S0, "rows2", (128, 128), F32)
            I(dve, lambda: nc.vector.memset(rows2[:], 0.0), W=[b_rows])
            dma(sp, rows2[0:64, :], w_conv.rearrange("j (c p) -> (j c) p", p=128), W=[b_rows])
            dma(sp, rows2[64:80, :], b_conv.rearrange("(c p) -> c p", p=128), W=[b_rows])
            colsT = K.sb(S0, "colsT", (128, 256), F32)
            b_cols = Buf()
            I(pe, lambda: nc.tensor.transpose(psF[1][:, 0:128], rowsT[:], cF["ident"]), R=[b_rows, b_cst], W=[bF[1]], sig=False)
            I(pe, lambda: nc.tensor.transpose(psF[1][:, 128:256], rows2[:], cF["ident"]), R=[b_rows, b_cst], W=[bF[1]])
            I(dve, lambda: nc.vector.tensor_copy(out=colsT[:], in_=psF[1][:, 0:256]), R=[bF[1]], W=[b_cols])
            I(dve, lambda: nc.vector.tensor_copy(out=wcT[:], in_=colsT[:, 128:192].rearrange("p (j c) -> p c j", j=4)), R=[b_cols], W=[b_small])
            I(dve, lambda: nc.vector.tensor_copy(out=bcT[:], in_=colsT[:, 192:208]), R=[b_cols], W=[b_small])
            ckpt("s4")
            wpan = [K.sb(S0, "wpan%d" % i, (128, 16, 512), F32) for i in range(2)]
            b_wpan = [Buf(), Buf()]
            for jg in range(24):
                s = jg % 2
                dma(sp, wpan[s][:], w_ada[:, jg * 512:(jg + 1) * 512].rearrange("(kc p) n -> p kc n", p=128), W=[b_wpan[s]])
                for jj in range(4):
                    j = jg * 4 + jj
                    for kc in range(16):
                        I(pe, lambda j=j, jj=jj, s=s, kc=kc: nc.tensor.matmul(psF[2][:, j * nr:(j + 1) * nr], wpan[s][:, kc, jj * 128:(jj + 1) * 128],
                                                                       scT[:, kc, :], start=(kc == 0), stop=(kc == 15)),
                          R=[b_wpan[s], b_scT], W=[bF[2]], sig=(kc == 15 and jj == 3))
            ckpt("s5")
            I(dve, lambda: nc.vector.tensor_tensor(out=modT[:], in0=psF[2][:, 0:96 * nr].rearrange("p (a b) -> p a b", b=nr),
                                                   in1=colsT[:, 0:96].unsqueeze(2).broadcast_to([128, 96, nr]), op=ALU.add),
              R=[bF[2], b_cols], W=[b_mod])
            ckpt("s6")
            I(dve, lambda: nc.vector.scalar_tensor_tensor(out=A1[:], in0=modT[:, 16:32, :], scalar=1.0,
                                                          in1=colsT[:, 96:112].unsqueeze(2).broadcast_to([128, 16, nr]), op0=ALU.add, op1=ALU.mult),
              R=[b_mod, b_cols], W=[b_mod])
            I(dve, lambda: nc.vector.scalar_tensor_tensor(out=A2[:], in0=modT[:, 64:80, :], scalar=1.0,
                                                          in1=colsT[:, 112:128].unsqueeze(2).broadcast_to([128, 16, nr]), op0=ALU.add, op1=ALU.mult),
              R=[b_mod, b_cols], W=[b_mod])
            K.barrier(scr)

        _dgc = {}

        def row_bcast(stack, dst, srcT, dstbuf):
            if id(stack) not in _dgc:
                _dgc[id(stack)] = (K.sb(stack, "dg", (128, 16, 128), F32), Buf())
            dg, b_dg = _dgc[id(stack)]
            I(dve, lambda: nc.vector.tensor_tensor(out=dg[:], in0=cF["ident"].unsqueeze(1).broadcast_to([128, 16, 128]),
                                                   in1=srcT.unsqueeze(2).broadcast_to([128, 16, 128]), op=ALU.mult),
              R=[b_cst, b_mod], W=[b_dg])
            for q in range(4):
                I(pe, lambda q=q: nc.tensor.matmul(psF[q][:], cF["ones"], dg[:, 4 * q:4 * q + 4, :].rearrange("p a b -> p (a b)"), start=True, stop=True),
                  R=[b_dg, b_cst], W=[bF[q]])
                I(dve, lambda q=q: nc.vector.tensor_copy(out=dst[:, q * 512:(q + 1) * 512], in_=psF[q][:]), R=[bF[q]], W=[dstbuf])

        ckpt("s8")
        for cg in range(4):
            convert(woutS[cg], w_out[:, cg * 512:(cg + 1) * 512].rearrange("(kc p) n -> p kc n", p=128), b_wout[cg])
        for cg in range(16):
            convert(wff1S[cg], w_ff1[:, cg * 512:(cg + 1) * 512].rearrange("(kc p) n -> p kc n", p=128), b_wff1[cg])
        for cg in range(4):
            for kq in range(4):
                convert(wff2S[cg, kq], w_ff2[kq * 2048:(kq + 1) * 2048, cg * 512:(cg + 1) * 512].rearrange("(kc p) n -> p kc n", p=128), b_wff2[cg][kq])

        SCALE_A = 128.0 ** -0.5
        rr = [0]

        def next_bank():
            i = 3 + rr[0] % 3
            rr[0] += 1
            return i

        def mk_mlstm(stack, L):
            M = type("M", (), {})()

            def t(name, shape, dt=F32):
                setattr(M, name, K.sb(stack, "m_" + name, shape, dt))
                setattr(M, "b_" + name, Buf(name))
            for nm in ["fz", "e4", "sp4", "ig", "g", "nb", "cmx", "na", "wi", "wia", "den", "r", "emt", "ssq", "rsd", "fac", "wst", "dsub"]:
                t(nm, (L, 4))
            t("am", (L, 8)); t("dn", (L, 8))
            t("dg", (L, 4, L)); t("Gm", (L, 4, L)); t("ET", (L, 4, L)); t("PT", (L, 4, L), BF16)
            t("cn4", (L, 4, L))
            t("qc", (L, 2, 256)); t("num", (L, 4, 256)); t("kTW", (L, 8, 128), BF16); t("junk", (L, 256))
            t("aLs", (128, 8)); t("dec", (128, 4)); t("dsb", (128, 4))
            I(dve, lambda: nc.vector.tensor_copy(out=M.cn4[:], in_=cF["causnegT"][0:L, 0:L].unsqueeze(1).broadcast_to([L, 4, L])), R=[b_cst], W=[M.b_cn4])
            M.L = L
            return M

        def mk_state(stack):
            St = type("St", (), {})()
            St.C32 = K.sb(stack, "C32", (128, 2, 4, 256), F32); St.b_C32 = Buf()
            St.Cbf = K.sb(stack, "Cbf", (128, 2, 4, 256), BF16); St.b_Cbf = Buf()
            St.n32 = K.sb(stack, "n32", (128, 2, 4), F32); St.b_n32 = Buf()
            St.nbf = K.sb(stack, "nbf", (128, 2, 4), BF16); St.b_nbf = Buf()
            St.mB = K.sb(stack, "mB", (128, 4), F32); St.b_m = Buf()
            return St

        def state_refresh_bf(St):
            I(act, lambda: nc.scalar.copy(out=St.Cbf[:].rearrange("p a h v -> p (a h v)"), in_=St.C32[:].rearrange("p a h v -> p (a h v)")), R=[St.b_C32], W=[St.b_Cbf])
            I(dve, lambda: nc.vector.tensor_copy(out=St.nbf[:].rearrange("p a h -> p (a h)"), in_=St.n32[:].rearrange("p a h -> p (a h)")), R=[St.b_n32], W=[St.b_nbf])

        def mlstm_block(M, St, own, qT, b_qT, kT, b_kT, vB, b_vB, gt, b_gt, sgg, b_sgg, ob, b_ob, dummy, sel):
            L = M.L
            idL = cF["ident"][0:L, 0:L]
            onL = cF["ones"][0:L, 0:L]
            b3 = lambda ap: ap.unsqueeze(2).broadcast_to([L, 4, L])
            m3 = lambda ap: ap.unsqueeze(1).broadcast_to([L, 4, L])
            p3 = lambda bank: psF[bank][0:L, 0:4 * L].rearrange("p (h s) -> p h s", h=4)
            I(dve, lambda: nc.vector.tensor_tensor(out=M.fz[:], in0=gt[:, 4:8], in1=bfB[0:L, :], op=ALU.add), R=[b_gt, b_small], W=[M.b_fz])
            I(act, lambda: nc.scalar.activation(out=M.e4[:], in_=M.fz[:], func=AF.Exp, scale=-1.0), R=[M.b_fz], W=[M.b_e4])
            I(act, lambda: nc.scalar.activation(out=M.sp4[:], in_=M.e4[:], func=AF.Ln, bias=1.0), R=[M.b_e4], W=[M.b_sp4])
            I(dve, lambda: nc.vector.tensor_tensor(out=M.ig[:], in0=gt[:, 0:4], in1=biB[0:L, :], op=ALU.add), R=[b_gt, b_small], W=[M.b_ig])
            if dummy:
                I(dve, lambda: nc.vector.tensor_scalar(out=M.ig[:], in0=M.ig[:], scalar1=cmt[0:L, 1:2], scalar2=None, op0=ALU.add), R=[b_cm], W=[M.b_ig])
                I(dve, lambda: nc.vector.tensor_scalar(out=M.sp4[:], in0=M.sp4[:], scalar1=cmt[0:L, 0:1], scalar2=None, op0=ALU.mult), R=[b_cm], W=[M.b_sp4])
            yield None
            I(pe, lambda: nc.tensor.matmul(psF[0][0:L, 0:4], cF["triinc"][0:L, 0:L], M.sp4[:], start=True, stop=True), R=[M.b_sp4, b_cst], W=[bF[0]])
            I(dve, lambda: nc.vector.tensor_tensor(out=M.g[:], in0=M.ig[:], in1=psF[0][0:L, 0:4], op=ALU.add), R=[M.b_ig], W=[M.b_g, bF[0]])
            I(dve, lambda: nc.vector.tensor_copy(out=M.nb[:], in_=psF[0][0:L, 0:4]), W=[M.b_nb, bF[0]])
            yield None
            I(dve, lambda: nc.vector.tensor_tensor(out=M.dg[:], in0=m3(idL), in1=b3(M.g[:]), op=ALU.mult), R=[M.b_g, b_cst], W=[M.b_dg])
            I(pe, lambda: nc.tensor.matmul(psF[1][0:L, 0:4 * L], onL, M.dg[:].rearrange("p h s -> p (h s)"), start=True, stop=True), R=[M.b_dg, b_cst], W=[bF[1]])
            I(dve, lambda: nc.vector.tensor_tensor(out=M.Gm[:], in0=p3(1), in1=m3(cF["causneg"][0:L, 0:L]), op=ALU.add), R=[b_cst], W=[M.b_Gm, bF[1]])
            I(dve, lambda: nc.vector.tensor_reduce(out=M.cmx[:], in_=M.Gm[:], axis=AX.X, op=ALU.max), R=[M.b_Gm], W=[M.b_cmx])
            I(dve, lambda: nc.vector.tensor_tensor(out=M.am[:, 0:4], in0=M.cmx[:], in1=St.mB[0:L, :], op=ALU.max), R=[M.b_cmx, St.b_m], W=[M.b_am])
            I(dve, lambda: nc.vector.tensor_tensor(out=M.am[:, 4:8], in0=M.am[:, 0:4], in1=M.nb[:], op=ALU.subtract), R=[M.b_nb], W=[M.b_am])
            I(pe, lambda: nc.tensor.matmul(psF[0][:, 8:16], cF[sel][0:L, :], M.am[:], start=True, stop=True), R=[M.b_am, b_cst], W=[bF[0]])
            I(dve, lambda: nc.vector.tensor_copy(out=M.aLs[:], in_=psF[0][:, 8:16]), W=[M.b_aLs, bF[0]])
            yield None
            if own:
                I(dve, lambda: nc.vector.tensor_scalar(out=M.na[:], in0=M.am[:, 0:4], scalar1=-1.0, scalar2=None, op0=ALU.mult), R=[M.b_am], W=[M.b_na])
                I(dve, lambda: nc.vector.tensor_tensor(out=M.dg[:], in0=m3(idL), in1=b3(M.na[:]), op=ALU.mult), R=[M.b_na, b_cst], W=[M.b_dg])
                I(pe, lambda: nc.tensor.matmul(psF[1][0:L, 0:4 * L], onL, M.dg[:].rearrange("p h s -> p (h s)"), start=True, stop=False), R=[M.b_dg, b_cst], W=[bF[1]], sig=False)
                I(pe, lambda: nc.tensor.matmul(psF[1][0:L, 0:4 * L], idL, M.cn4[:].rearrange("p h s -> p (h s)"), start=False, stop=True), R=[M.b_cn4, b_cst], W=[bF[1]])
                for h in range(4):
                    I(act, lambda h=h: nc.scalar.activation(out=M.ET[:, h, :], in_=psF[1][0:L, h * L:(h + 1) * L], func=AF.Exp, bias=M.g[:, h:h + 1], scale=1.0),
                      R=[M.b_g], W=[M.b_ET, bF[1]])
                yield None
                for h in range(4):
                    for half in range(2):
                        I(pe, lambda h=h, half=half: nc.tensor.matmul(psF[2][0:L, h * L:(h + 1) * L], kT(h, half), qT(h, half), start=(half == 0), stop=(half == 1)),
                          R=[b_kT, b_qT], W=[bF[2]], sig=(h == 3 and half == 1))
                I(dve, lambda: nc.vector.scalar_tensor_tensor(out=M.PT[:], in0=p3(2), scalar=1.0 / 16, in1=M.ET[:], op0=ALU.mult, op1=ALU.mult), R=[M.b_ET], W=[M.b_PT, bF[2]])
                yield None
                I(dve, lambda: nc.vector.tensor_tensor(out=M.wia[:], in0=St.mB[0:L, :], in1=M.am[:, 0:4], op=ALU.subtract), R=[St.b_m, M.b_am], W=[M.b_wia])
                I(act, lambda: nc.scalar.activation(out=M.wi[:], in_=M.wia[:], func=AF.Exp), R=[M.b_wia], W=[M.b_wi])
                for h in range(4):
                    I(pe, lambda h=h: nc.tensor.matmul(psF[0][0:L, 16 + h:17 + h], M.PT[:, h, :], onesB[0:L, 0:1], start=True, stop=True), R=[M.b_PT, b_cb], W=[bF[0]], sig=False)
                for h in range(4):
                    for half in range(2):
                        I(pe, lambda h=h, half=half: nc.tensor.matmul(psF[0][0:L, 20 + h:21 + h], qT(h, half), St.nbf[:, half, h:h + 1], start=(half == 0), stop=(half == 1)),
                          R=[b_qT, St.b_nbf], W=[bF[0]], sig=(h == 3 and half == 1))
                I(dve, lambda: nc.vector.tensor_copy(out=M.dn[:], in_=psF[0][0:L, 16:24]), W=[M.b_dn, bF[0]])
                I(dve, lambda: nc.vector.tensor_tensor(out=M.den[:], in0=M.dn[:, 4:8], in1=M.wi[:], op=ALU.mult), R=[M.b_dn, M.b_wi], W=[M.b_den])
                I(dve, lambda: nc.vector.tensor_tensor(out=M.den[:], in0=M.den[:], in1=M.dn[:, 0:4], op=ALU.add), R=[M.b_dn], W=[M.b_den])
                yield None
                I(act, lambda: nc.scalar.activation(out=M.emt[:], in_=M.am[:, 4:8], func=AF.Exp, scale=-1.0), R=[M.b_am], W=[M.b_emt])
                I(dve, lambda: nc.vector.tensor_scalar(out=M.dsub[:], in0=M.den[:], scalar1=-1.0, scalar2=None, op0=ALU.mult), R=[M.b_den], W=[M.b_dsub])
                I(dve, lambda: nc.vector.tensor_tensor(out=M.den[:], in0=M.den[:], in1=M.dsub[:], op=ALU.max), R=[M.b_dsub], W=[M.b_den])
                I(dve, lambda: nc.vector.tensor_tensor(out=M.den[:], in0=M.den[:], in1=M.emt[:], op=ALU.max), R=[M.b_emt], W=[M.b_den])
                I(dve, lambda: nc.vector.reciprocal(out=M.r[:], in_=M.den[:]), R=[M.b_den], W=[M.b_r])
                yield None
                for hp in range(2):
                    for hh in range(2):
                        h = 2 * hp + hh
                        I(pe, lambda h=h, hh=hh: nc.tensor.matmul(psF[1][0:L, hh * 256:(hh + 1) * 256], M.PT[:, h, :], vB[:, h, :], start=True, stop=True),
                          R=[M.b_PT, b_vB], W=[bF[1]], sig=(hh == 1))
                    for hh in range(2):
                        h = 2 * hp + hh
                        for half in range(2):
                            I(pe, lambda h=h, hh=hh, half=half: nc.tensor.matmul(psF[2][0:L, hh * 256:(hh + 1) * 256], qT(h, half), St.Cbf[:, half, h, :], start=(half == 0), stop=(half == 1)),
                              R=[b_qT, St.b_Cbf], W=[bF[2]], sig=(hh == 1 and half == 1))
                    for hh in range(2):
                        h = 2 * hp + hh
                        I(act, lambda h=h, hh=hh: nc.scalar.activation(out=M.qc[:, hh, :], in_=psF[2][0:L, hh * 256:(hh + 1) * 256], func=AF.Copy, scale=M.wi[:, h:h + 1]),
                          R=[M.b_wi], W=[M.b_qc, bF[2]])
                    I(dve, lambda hp=hp: nc.vector.tensor_tensor(out=M.num[:, 2 * hp:2 * hp + 2, :], in0=psF[1][0:L, 0:512].rearrange("p (a v) -> p a v", a=2), in1=M.qc[:], op=ALU.add),
                      R=[M.b_qc], W=[M.b_num, bF[1]])
                yield None
                for h in range(4):
                    I(act, lambda h=h: nc.scalar.activation(out=M.junk[:], in_=M.num[:, h, :], func=AF.Square, scale=M.r[:, h:h + 1], accum_out=M.ssq[:, h:h + 1]),
                      R=[M.b_num, M.b_r], W=[M.b_junk, M.b_ssq])
                I(act, lambda: nc.scalar.activation(out=M.rsd[:], in_=M.ssq[:], func=AF.Ln, scale=1.0 / 256, bias=epsT[0:L, :]), R=[M.b_ssq, b_small], W=[M.b_rsd])
                I(act, lambda: nc.scalar.activation(out=M.rsd[:], in_=M.rsd[:], func=AF.Exp, scale=-0.5), W=[M.b_rsd])
                I(dve, lambda: nc.vector.tensor_tensor(out=M.fac[:], in0=M.r[:], in1=M.rsd[:], op=ALU.mult), R=[M.b_r, M.b_rsd], W=[M.b_fac])
                for h in range(4):
                    I(dve, lambda h=h: nc.vector.scalar_tensor_tensor(out=ob[:, h * 256:(h + 1) * 256], in0=M.num[:, h, :], scalar=M.fac[:, h:h + 1], in1=sgg[:, h * 256:(h + 1) * 256], op0=ALU.mult, op1=ALU.mult),
                      R=[M.b_num, M.b_fac, b_sgg], W=[b_ob])
            yield None
            I(dve, lambda: nc.vector.tensor_tensor(out=M.dsub[:], in0=M.g[:], in1=M.aLs[0:L, 0:4], op=ALU.subtract), R=[M.b_g, M.b_aLs], W=[M.b_dsub])
            I(act, lambda: nc.scalar.activation(out=M.wst[:], in_=M.dsub[:], func=AF.Exp), R=[M.b_dsub], W=[M.b_wst])
            I(dve, lambda: nc.vector.tensor_tensor(out=M.dsb[:], in0=St.mB[:], in1=M.aLs[:, 0:4], op=ALU.subtract), R=[St.b_m, M.b_aLs], W=[M.b_dsb])
            I(act, lambda: nc.scalar.activation(out=M.dec[:], in_=M.dsb[:], func=AF.Exp), R=[M.b_dsb], W=[M.b_dec])
            for h in range(4):
                for half in range(2):
                    idx = 2 * h + half
                    I(pe, lambda h=h, half=half, idx=idx: nc.tensor.transpose(psT[1][0:L, idx * 128:(idx + 1) * 128], kT(h, half), identB[:]),
                      R=[b_kT, b_cb], W=[bT[1]], sig=(idx == 7))
            for h in range(4):
                I(dve, lambda h=h: nc.vector.tensor_scalar(out=M.kTW[:, 2 * h:2 * h + 2, :], in0=psT[1][0:L, 2 * h * 128:(2 * h + 2) * 128].rearrange("p (a d) -> p a d", a=2),
                                                           scalar1=M.wst[:, h:h + 1], scalar2=1.0 / 16, op0=ALU.mult, op1=ALU.mult),
                  R=[M.b_wst], W=[M.b_kTW, bT[1]])
            yield None
            for half in range(2):
                for hp in range(2):
                    bk = 1 + hp
                    for hh in range(2):
                        h = 2 * hp + hh
                        I(pe, lambda h=h, hh=hh, half=half, bk=bk: nc.tensor.matmul(psF[bk][:, hh * 256:(hh + 1) * 256], M.kTW[:, 2 * h + half, :], vB[:, h, :], start=True, stop=True),
                          R=[M.b_kTW, b_vB], W=[bF[bk]], sig=(hh == 1))
                    for hh in range(2):
                        h = 2 * hp + hh
                        I(dve, lambda h=h, hh=hh, half=half, bk=bk: nc.vector.scalar_tensor_tensor(out=St.C32[:, half, h, :], in0=St.C32[:, half, h, :], scalar=M.dec[:, h:h + 1],
                                                                                               in1=psF[bk][:, hh * 256:(hh + 1) * 256], op0=ALU.mult, op1=ALU.add),
                          R=[M.b_dec], W=[St.b_C32, bF[bk]])
            yield None
            for half in range(2):
                for h in range(4):
                    c0 = 24 + half * 4 + h
                    I(pe, lambda h=h, half=half, c0=c0: nc.tensor.matmul(psF[0][:, c0:c0 + 1], M.kTW[:, 2 * h + half, :], onesB[0:L, 0:1], start=True, stop=True),
                      R=[M.b_kTW, b_cb], W=[bF[0]], sig=(half == 1 and h == 3))
            for half in range(2):
                I(dve, lambda half=half: nc.vector.tensor_tensor(out=St.n32[:, half, :], in0=St.n32[:, half, :], in1=M.dec[:], op=ALU.mult), R=[M.b_dec], W=[St.b_n32])
            I(dve, lambda: nc.vector.tensor_tensor(out=St.n32[:].rearrange("p a h -> p (a h)"), in0=St.n32[:].rearrange("p a h -> p (a h)"), in1=psF[0][:, 24:32], op=ALU.add),
              W=[St.b_n32, bF[0]])
            state_refresh_bf(St)
            I(dve, lambda: nc.vector.tensor_copy(out=St.mB[:], in_=M.aLs[:, 4:8]), R=[M.b_aLs], W=[St.b_m])

        def rms_rstd(ssq_ap, out_ap, n, scale, b_in, b_out):
            I(act, lambda: nc.scalar.activation(out=out_ap, in_=ssq_ap, func=AF.Ln, scale=scale, bias=epsT[0:n, :]), R=[b_in, b_small], W=[b_out])
            I(act, lambda: nc.scalar.activation(out=out_ap, in_=out_ap, func=AF.Exp, scale=-0.5), W=[b_out])

        def qknorm(n, bank, gB, out_h, junk, b_junk, ssq4, rs4, b_s4, b_out):
            for h in range(4):
                I(act, lambda h=h: nc.scalar.activation(out=junk[0:n, 0:128], in_=psF[bank][0:n, h * 128:(h + 1) * 128], func=AF.Square, accum_out=ssq4[0:n, h:h + 1]),
                  W=[b_junk, b_s4, bF[bank]])
            rms_rstd(ssq4[0:n, :], rs4[0:n, :], n, 1.0 / 128, b_s4, b_s4)
            for h in range(4):
                I(dve, lambda h=h: nc.vector.scalar_tensor_tensor(out=out_h(h), in0=psF[bank][0:n, h * 128:(h + 1) * 128], scalar=rs4[0:n, h:h + 1], in1=gB[0:n, :], op0=ALU.mult, op1=ALU.mult),
                  R=[b_s4, b_small], W=[b_out, bF[bank]])

        def norm_transpose(n, x_ap, b_x, xn, b_xn, ssx, rsx, b_sx, hT_dst, b_hT, Amod, Bmod, ts, valid=False):
            I(act, lambda: nc.scalar.activation(out=xn[0:n, :], in_=x_ap, func=AF.Square, accum_out=ssx[0:n, :]), R=[b_x], W=[b_xn, b_sx])
            rms_rstd(ssx[0:n, :], rsx[0:n, :], n, 1.0 / D, b_sx, b_sx)
            I(dve, lambda: nc.vector.tensor_scalar(out=xn[0:n, :], in0=x_ap, scalar1=rsx[0:n, 0:1], scalar2=None, op0=ALU.mult), R=[b_x, b_sx], W=[b_xn])
            for rnd in range(2):
                for c in range(8 * rnd, 8 * rnd + 8):
                    I(pe, lambda c=c: nc.tensor.transpose(psT[0][:, (c % 8) * 128:(c % 8) * 128 + n], xn[0:n, c * 128:(c + 1) * 128], identB[0:n, 0:n]),
                      R=[b_xn, b_cb], W=[bT[0]], sig=(c % 8 == 7))
                for c in range(8 * rnd, 8 * rnd + 8):
                    src = psT[0][:, (c % 8) * 128:(c % 8) * 128 + n]
                    if c % 2 == 0:
                        I(act, lambda c=c, src=src: nc.scalar.activation(out=hT_dst(c), in_=src, func=AF.Identity, scale=Amod[:, c, ts:ts + 1], bias=Bmod[:, c, ts:ts + 1]),
                          R=[b_mod], W=[b_hT, bT[0]])
                    else:
                        I(dve, lambda c=c, src=src: nc.vector.tensor_scalar(out=hT_dst(c), in0=src, scalar1=Amod[:, c, ts:ts + 1], scalar2=Bmod[:, c, ts:ts + 1], op0=ALU.mult, op1=ALU.add),
                          R=[b_mod], W=[b_hT, bT[0]])

        def load_w(tile, b_tile, src, b_src):
            return dma(sp, tile[:], src, R=[b_src], W=[b_tile])

        def stage_A():
            with ExitStack() as SA:
                xin = K.sb(SA, "xin", (128, D), F32); b_xin = Buf()
                xnb = K.sb(SA, "xnb", (128, D), BF16); b_xnb = Buf()
                ssx = K.sb(SA, "ssx", (128, 1), F32); rsx = K.sb(SA, "rsx", (128, 1), F32); b_sx = Buf()
                hT = K.sb(SA, "hT", (128, 16, 512), BF16); b_hT = Buf()
                wt = [K.sb(SA, "wt%d" % i, (128, 16, 512), BF16) for i in range(2)]; b_wt = [Buf(), Buf()]
                kst = K.sb(SA, "kst", (128, 4, 128), F32); b_kst = Buf()
                vst = K.sb(SA, "vst", (128, 4, 128), F32); b_vst = Buf()
                knb = K.sb(SA, "knb", (128, 512), BF16); b_knb = Buf()
                junkq = K.sb(SA, "junkq", (128, 128), F32); b_junkq = Buf()
                ssq4 = K.sb(SA, "ssq4", (128, 4), F32); rs4 = K.sb(SA, "rs4", (128, 4), F32); b_s4 = Buf()
                kTst = K.sb(SA, "kTst", (128, 8, 512), BF16); b_kTst = Buf()
                vbst = K.sb(SA, "vbst", (128, 8, 4, 128), BF16); b_vbst = Buf()
                raw = K.sb(SA, "raw", (128, 16, 515), BF16); b_raw = Buf()
                rawl = K.sb(SA, "rawl", (128, 16, 3), F32); b_rawl = Buf()
                cvy = [K.sb(SA, "cvy%d" % i, (128, 512), F32) for i in range(2)]; b_cvy = [Buf(), Buf()]
                qTt2 = [K.sb(SA, "qTt%d" % i, (128, 8, 2, 128), BF16) for i in range(2)]; b_qTt2 = [Buf(), Buf()]
                kTt2 = [K.sb(SA, "kTt%d" % i, (128, 8, 512), BF16) for i in range(2)]; b_kTt2 = [Buf(), Buf()]
                vBt2 = [K.sb(SA, "vBt%d" % i, (128, 4, 4, 256), BF16) for i in range(2)]; b_vBt2 = [Buf(), Buf()]
                gts2 = [K.sb(SA, "gts%d" % i, (128, 4, 8), F32) for i in range(2)]; b_gts2 = [Buf(), Buf()]
                sgg2 = [K.sb(SA, "sgg%d" % i, (128, 2, 1024), BF16) for i in range(2)]; b_sgg2 = [Buf(), Buf()]
                sg32 = K.sb(SA, "sg32", (128, 512), F32); b_sg32 = Buf()
                obt = K.sb(SA, "obt", (128, 1024), BF16); b_obt = Buf()
                qTst = K.sb(SA, "qTst", (128, 8, 256), BF16); b_qTst = Buf()
                oTst = K.sb(SA, "oTst", (128, 8, 256), BF16); b_oTst = Buf()
                M = mk_mlstm(SA, 128)
                St = mk_state(SA)
                for tl, bb in ((St.C32, St.b_C32), (St.n32, St.b_n32), (St.mB, St.b_m)):
                    I(dve, lambda tl=tl: nc.vector.memset(tl[:], 0.0), W=[bb])
                state_refresh_bf(St)
                I(dve, lambda: nc.vector.memset(raw[:, :, 0:3], 0.0), W=[b_raw])

                wsl = [0]

                def genA(G):
                    for j in range(4):
                        blk = 4 * G + j
                        dma(sp, xin[:], xs[blk * 128:(blk + 1) * 128, :], W=[b_xin])
                        yield None
                        norm_transpose(128, xin[:], b_xin, xnb, b_xnb, ssx, rsx, b_sx,
                                       lambda c, j=j: hT[:, c, j * 128:(j + 1) * 128], b_hT, A1, modT, 0)
                        if blk == 0:
                            I(dve, lambda: nc.vector.tensor_scalar(out=hT[:, :, 0:128], in0=hT[:, :, 0:128], scalar1=cmt[:, 0:1], scalar2=None, op0=ALU.mult), R=[b_cm], W=[b_hT])
                    for cg in A_ORDER:
                        s = wsl[0] % 2
                        wsl[0] += 1
                        ncol = 512 if cg < 14 else 8
                        dma(sp, wt[s][:, :, 0:ncol], winS[cg, :, :, 0:ncol], R=[b_win[cg]], W=[b_wt[s]])
                        if cg in (6, 7, 8, 9):
                            yield 'need_raw'
                            for cc in range(4):
                                yield None
                                ci = (cg - 6) * 4 + cc
                                bk = next_bank()
                                for kc in range(16):
                                    I(pe, lambda kc=kc, cc=cc, s=s, bk=bk: nc.tensor.matmul(psF[bk][:], wt[s][:, kc, cc * 128:(cc + 1) * 128], hT[:, kc, :], start=(kc == 0), stop=(kc == 15)),
                                      R=[b_wt[s], b_hT], W=[bF[bk]], sig=(kc == 15))
                                I(act, lambda ci=ci, bk=bk: nc.scalar.copy(out=raw[:, ci, 3:515], in_=psF[bk][:]), W=[b_raw, bF[bk]])
                                if G == cfg.ngrp - 1:
                                    I(dve, lambda ci=ci, bk=bk: nc.vector.tensor_copy(out=rawl[:, ci, :], in_=psF[bk][:, 509:512]), W=[b_rawl, bF[bk]])
                            continue
                        blocks = (1, 3) if cg in (0, 1, 12, 13) else (0, 1, 2, 3)
                        for j in blocks:
                            yield None
                            blk = 4 * G + j
                            bk = next_bank()
                            for kc in range(16):
                                I(pe, lambda kc=kc, j=j, s=s, bk=bk, ncol=ncol: nc.tensor.matmul(psF[bk][:, 0:ncol], hT[:, kc, j * 128:(j + 1) * 128], wt[s][:, kc, 0:ncol], start=(kc == 0), stop=(kc == 15)),
                                  R=[b_wt[s], b_hT], W=[bF[bk]], sig=(kc == 15))
                            if cg in (2, 3):
                                h0 = (cg - 2) * 4
                                qknorm(128, bk, gkB, lambda h: kst[:, h, :], junkq, b_junkq, ssq4, rs4, b_s4, b_kst)
                                I(act, lambda: nc.scalar.copy(out=knb[:], in_=kst[:].rearrange("p h d -> p (h d)")), R=[b_kst], W=[b_knb])
                                for h in range(4):
                                    I(pe, lambda h=h: nc.tensor.transpose(psT[0][:, h * 128:(h + 1) * 128], knb[:, h * 128:(h + 1) * 128], identB[:]), R=[b_knb, b_cb], W=[bT[0]], sig=(h == 3))
                                I(dve, lambda h0=h0, j=j: nc.vector.tensor_copy(out=kTst[:, h0:h0 + 4, j * 128:(j + 1) * 128], in_=psT[0][:, 0:512].rearrange("p (h t) -> p h t", h=4)),
                                  W=[b_kTst, bT[0]])
                                dma(act, k_o.rearrange("h t d -> t h d")[blk * 128:(blk + 1) * 128, h0:h0 + 4, :], kst[:], R=[b_kst])
                            elif cg in (4, 5):
                                h0 = (cg - 4) * 4
                                I(act, lambda bk=bk: nc.scalar.copy(out=vst[:].rearrange("p h d -> p (h d)"), in_=psF[bk][:]), W=[b_vst, bF[bk]])
                                I(dve, lambda h0=h0, j=j: nc.vector.tensor_copy(out=vbst[:, h0:h0 + 4, j, :], in_=vst[:]), R=[b_vst], W=[b_vbst])
                                dma(act, v_o.rearrange("h t d -> t h d")[blk * 128:(blk + 1) * 128, h0:h0 + 4, :], vst[:], R=[b_vst])
                            elif cg in (10, 11):
                                h0 = (cg - 10) * 2
                                I(act, lambda h0=h0, j=j, bk=bk: nc.scalar.copy(out=vBt2[G % 2][:, j, h0:h0 + 2, :].rearrange("p h v -> p (h v)"), in_=psF[bk][:]), W=[b_vBt2[G % 2], bF[bk]])
                            elif cg == 14:
                                I(dve, lambda j=j, bk=bk: nc.vector.tensor_copy(out=gts2[G % 2][:, j, :], in_=psF[bk][:, 0:8]), W=[b_gts2[G % 2], bF[bk]])
                            elif cg in (0, 1):
                                h0 = cg * 4
                                jo = j // 2
                                qknorm(128, bk, gqB, lambda h: knb[:, h * 128:(h + 1) * 128], junkq, b_junkq, ssq4, rs4, b_s4, b_knb)
                                for h in range(4):
                                    I(pe, lambda h=h: nc.tensor.transpose(psT[0][:, h * 128:(h + 1) * 128], knb[:, h * 128:(h + 1) * 128], identB[:]), R=[b_knb, b_cb], W=[bT[0]], sig=(h == 3))
                                I(dve, lambda h0=h0, jo=jo: nc.vector.tensor_copy(out=qTst[:, h0:h0 + 4, jo * 128:(jo + 1) * 128], in_=psT[0][:, 0:512].rearrange("p (h t) -> p h t", h=4)),
                                  W=[b_qTst, bT[0]])
                            elif cg in (12, 13):
                                jo = j // 2
                                c0 = (cg - 12) * 512
                                I(act, lambda bk=bk: nc.scalar.activation(out=sg32[:], in_=psF[bk][:], func=AF.Sigmoid), W=[b_sg32, bF[bk]])
                                I(dve, lambda jo=jo, c0=c0: nc.vector.tensor_tensor(out=sgg2[G % 2][:, jo, c0:c0 + 512], in0=sg32[:], in1=ghB[:, c0:c0 + 512], op=ALU.mult), R=[b_sg32, b_small], W=[b_sgg2[G % 2]])
                    dma(act, KTs.rearrange("h d t -> d h t")[:, :, G * 512:(G + 1) * 512], kTst[:], R=[b_kTst])
                    dma(act, Vss.rearrange("h p b d -> p h b d")[:, :, 4 * G:4 * G + 4, :], vbst[:], R=[b_vbst])
                    dma(act, QTs.rearrange("h d t -> d h t")[:, :, G * 256:(G + 1) * 256], qTst[:], R=[b_qTst])

                def genB(G):
                    for ci in range(16):
                        yield None
                        yb = cvy[ci % 2]; b_y = b_cvy[ci % 2]
                        I(dve, lambda ci=ci, yb=yb: nc.vector.tensor_scalar(out=yb[:], in0=raw[:, ci, 0:512], scalar1=wcT[:, ci, 0:1], scalar2=None, op0=ALU.mult), R=[b_raw, b_small], W=[b_y])
                        for jt in range(1, 4):
                            I(dve, lambda ci=ci, yb=yb, jt=jt: nc.vector.scalar_tensor_tensor(out=yb[:], in0=raw[:, ci, jt:jt + 512], scalar=wcT[:, ci, jt:jt + 1], in1=yb[:], op0=ALU.mult, op1=ALU.add),
                              R=[b_raw, b_small], W=[b_y])
                        if ci < 8:
                            I(act, lambda ci=ci, yb=yb: nc.scalar.activation(out=qTt2[G % 2][:, ci, :, :], in_=yb[:].rearrange("p (j t) -> p j t", j=4)[:, 1::2, :], func=AF.Silu, bias=bcT[:, ci:ci + 1]),
                              R=[b_y, b_small], W=[b_qTt2[G % 2]])
                        else:
                            I(act, lambda ci=ci, yb=yb: nc.scalar.activation(out=kTt2[G % 2][:, ci - 8, :], in_=yb[:], func=AF.Silu, bias=bcT[:, ci:ci + 1]),
                              R=[b_y, b_small], W=[b_kTt2[G % 2]])
                    I(dve, lambda: nc.vector.tensor_copy(out=raw[:, :, 0:3], in_=raw[:, :, 512:515]), W=[b_raw])
                    yield 'conv_done'
                    for j in range(4):
                        blk = 4 * G + j
                        own = (j % 2 == 1)
                        jo = j // 2
                        for _ in mlstm_block(M, St, own,
                                    lambda h, half, j=j: qTt2[G % 2][:, 2 * h + half, j // 2, :], b_qTt2[G % 2],
                                    lambda h, half, j=j: kTt2[G % 2][:, 2 * h + half, j * 128:(j + 1) * 128], b_kTt2[G % 2],
                                    vBt2[G % 2][:, j, :, :], b_vBt2[G % 2], gts2[G % 2][:, j, :], b_gts2[G % 2],
                                    sgg2[G % 2][:, jo, :], b_sgg2[G % 2], obt, b_obt, dummy=(blk == 0), sel="sel127"):
                            yield None
                        yield None
                        if own:
                            for c in range(8):
                                I(pe, lambda c=c: nc.tensor.transpose(psT[1][:, c * 128:(c + 1) * 128], obt[:, c * 128:(c + 1) * 128], identB[:]), R=[b_obt, b_cb], W=[bT[1]], sig=(c == 7))
                            I(act, lambda jo=jo: nc.scalar.copy(out=oTst[:, :, jo * 128:(jo + 1) * 128], in_=psT[1][:, :].rearrange("p (c t) -> p c t", c=8)), W=[b_oTst, bT[1]])
                    dma(act, OTs.rearrange("c d t -> d c t")[:, 8:16, G * 256:(G + 1) * 256], oTst[:], R=[b_oTst])

                def drive(gA, gB):
                    a_done = gA is None
                    b_done = gB is None
                    conv_done = b_done
                    while not (a_done and b_done):
                        if not a_done:
                            try:
                                tag = next(gA)
                            except StopIteration:
                                a_done = True
                                tag = None
                            if tag == 'need_raw':
                                while not conv_done and not b_done:
                                    try:
                                        tb_ = next(gB)
                                    except StopIteration:
                                        b_done = True
                                        break
                                    if tb_ == 'conv_done':
                                        conv_done = True
                        if not b_done:
                            try:
                                tb_ = next(gB)
                            except StopIteration:
                                b_done = True
                                tb_ = None
                            if tb_ == 'conv_done':
                                conv_done = True

                drive(genA(0), None)
                for G in range(cfg.ngrp):
                    drive(genA(G + 1) if G + 1 < cfg.ngrp else None, genB(G))
                for a_ in range(2):
                    dma(act, C_o.rearrange("h (a p) v -> a p h v", p=128)[a_], St.C32[:, a_, :, :], R=[St.b_C32])
                with nc.allow_non_contiguous_dma(reason="tiny state vectors"):
                    for a_ in range(2):
                        dma(act, n_o.rearrange("h (a p) -> a p h", p=128)[a_], St.n32[:, a_, :], R=[St.b_n32])
                    dma(act, m_o, St.mB[0:1, :], R=[St.b_m])
                    for r_ in range(3):
                        dma(act, conv_o[r_].rearrange("(c p) -> p c", p=128), rawl[:, :, r_], R=[b_rawl])
                K.barrier(scr)

        def mk_attn(stack, NQ):
            At = type("At", (), {})()

            def t(name, shape, dt, nslot):
                setattr(At, name, [K.sb(stack, "a_%s%d" % (name, i), shape, dt) for i in range(nslot)])
                setattr(At, "b_" + name, [Buf() for _ in range(nslot)])
            t("E", (128, NQ), F32, 4); t("SP", (128, NQ), BF16, 3); t("X", (128, NQ), F32, 2); t("A", (128, NQ), BF16, 3)
            t("ACC", (128, NQ), F32, 2); t("ACCb", (128, NQ), BF16, 3)
            At.zero = K.sb(stack, "a_zero", (128, max(NQ, 128)), BF16); At.b_zero = Buf()
            I(dve, lambda: nc.vector.memset(At.zero[:], 0.0), W=[At.b_zero])
            At.NQ = NQ
            return At

        def attn_head(At, Kslice, b_K, Vslice, b_V, Qt, b_Q, kb_list, zb, tb, ob, extraR=()):
            NQ = At.NQ
            nk = len(kb_list)
            I(pe, lambda: nc.tensor.matmul(psF[ob][:, 0:NQ], At.zero[:, 0:128], At.zero[:, 0:NQ], start=True, stop=False), R=[At.b_zero], W=[bF[ob]])
            for s_ in range(3):
                I(dve, lambda: nc.vector.memset(At.ACCb[s_][:], 0.0), W=[At.b_ACCb[s_]])

            def rng(i):
                kb, nkeys, q0, diag, kbias = kb_list[i]
                return kb, nkeys, q0, diag, kbias

            def stage1a(i):
                kb, nkeys, q0, diag, kbias = rng(i)
                E, SP = At.E[i % 4], At.SP[i % 3]
                bE, bSP = At.b_E[i % 4], At.b_SP[i % 3]
                I(pe, lambda: nc.tensor.matmul(psF[zb][0:nkeys, q0:NQ], Kslice(kb), Qt[:, q0:NQ], start=True, stop=True), R=[b_K, b_Q] + list(extraR), W=[bF[zb]])
                if kbias:
                    I(act, lambda: nc.scalar.activation(out=E[0:nkeys, q0:NQ], in_=psF[zb][0:nkeys, q0:NQ], func=AF.Exp, scale=SCALE_A, bias=cmt[0:nkeys, 1:2]), R=[b_cm], W=[bE, bF[zb]])
                else:
                    I(act, lambda: nc.scalar.activation(out=E[0:nkeys, q0:NQ], in_=psF[zb][0:nkeys, q0:NQ], func=AF.Exp, scale=SCALE_A), W=[bE, bF[zb]])
                if diag:
                    dq = min(128, NQ - q0)
                    I(dve, lambda: nc.vector.tensor_tensor(out=E[0:nkeys, q0:q0 + dq], in0=E[0:nkeys, q0:q0 + dq], in1=cF["stri01"][0:nkeys, 0:dq], op=ALU.mult), R=[b_cst], W=[bE])

            def stage1b(i):
                kb, nkeys, q0, diag, kbias = rng(i)
                E, SP = At.E[i % 4], At.SP[i % 3]
                bE, bSP = At.b_E[i % 4], At.b_SP[i % 3]
                I(act, lambda: nc.scalar.activation(out=SP[0:nkeys, q0:NQ], in_=E[0:nkeys, q0:NQ], func=AF.Ln, bias=1.0), R=[bE], W=[bSP])
                if i < nk - 1:
                    I(dve, lambda: nc.vector.tensor_tensor(out=At.ACCb[(i + 1) % 3][0:nkeys, q0:NQ], in0=At.ACCb[i % 3][0:nkeys, q0:NQ], in1=SP[0:nkeys, q0:NQ], op=ALU.add),
                      R=[bSP, At.b_ACCb[i % 3]], W=[At.b_ACCb[(i + 1) % 3]])

            def stage2(i):
                kb, nkeys, q0, diag, kbias = rng(i)
                E, SP, X, A = At.E[i % 4], At.SP[i % 3], At.X[i % 2], At.A[i % 3]
                bE, bSP, bX, bA = At.b_E[i % 4], At.b_SP[i % 3], At.b_X[i % 2], At.b_A[i % 3]
                I(pe, lambda: nc.tensor.matmul(psF[tb][0:nkeys, q0:NQ], trigeB[0:nkeys, 0:nkeys], SP[0:nkeys, q0:NQ], start=True, stop=(i == 0)), R=[bSP, b_cb], W=[bF[tb]], sig=(i == 0))
                if i > 0:
                    I(pe, lambda: nc.tensor.matmul(psF[tb][0:nkeys, q0:NQ], onesB[:, 0:nkeys], At.ACCb[i % 3][:, q0:NQ], start=False, stop=True), R=[At.b_ACCb[i % 3], b_cb], W=[bF[tb]])
                I(act, lambda: nc.scalar.activation(out=X[0:nkeys, q0:NQ], in_=psF[tb][0:nkeys, q0:NQ], func=AF.Exp, scale=-1.0), W=[bX, bF[tb]])
                I(dve, lambda: nc.vector.tensor_tensor(out=A[0:nkeys, q0:NQ], in0=X[0:nkeys, q0:NQ], in1=E[0:nkeys, q0:NQ], op=ALU.mult), R=[bX, bE], W=[bA])

            def stage3(i):
                kb, nkeys, q0, diag, kbias = rng(i)
                A, bA = At.A[i % 3], At.b_A[i % 3]
                I(pe, lambda: nc.tensor.matmul(psF[ob][:, q0:NQ], Vslice(kb), A[0:nkeys, q0:NQ], start=False, stop=(i == nk - 1)), R=[b_V, bA] + list(extraR), W=[bF[ob]])

            stage1a(0)
            stage1b(0)
            if nk > 1:
                stage1a(1)
            for i in range(nk):
                if i + 1 < nk:
                    stage1b(i + 1)
                if i + 2 < nk:
                    stage1a(i + 2)
                if i >= 2:
                    stage3(i - 2)
                stage2(i)
            if nk >= 2:
                stage3(nk - 2)
            stage3(nk - 1)

        def stage_B():
            with ExitStack() as SB:
                NQ = 512
                At = mk_attn(SB, NQ)
                Kt = [K.sb(SB, "Kt%d" % i, (128, T), BF16) for i in range(2)]; b_Kt = [Buf(), Buf()]
                Vt = [K.sb(SB, "Vt%d" % i, (128, NB, 128), BF16) for i in range(2)]; b_Vt = [Buf(), Buf()]
                Qt = [K.sb(SB, "Qt%d" % i, (128, NQ), BF16) for i in range(2)]; b_Qt = [Buf(), Buf()]
                oTa = [K.sb(SB, "oTa%d" % i, (128, NQ), BF16) for i in range(2)]; b_oTa = [Buf(), Buf()]
                it = 0
                for g in range(cfg.nown // 4):
                    nkb = 8 * g + 8
                    kb_list = []
                    for kb in reversed(range(nkb)):
                        r = kb - 8 * g
                        if r < 1:
                            q0, diag = 0, False
                        elif r % 2 == 1:
                            q0, diag = (r // 2) * 128, True
                        else:
                            q0, diag = (r // 2) * 128, False
                        kb_list.append((kb, 128, q0, diag, kb == 0))
                    for h in range(8):
                        s = it % 2
                        it += 1
                        dma(sp, Kt[s][:, 0:nkb * 128], KTs[h, :, 0:nkb * 128], W=[b_Kt[s]])
                        dma(sp, Vt[s][:, 0:nkb, :], Vss[h, :, 0:nkb, :], W=[b_Vt[s]])
                        dma(sp, Qt[s][:], QTs[h, :, g * NQ:(g + 1) * NQ], W=[b_Qt[s]])
                        zb, tb, ob = (0, 1, 2) if s == 0 else (3, 4, 5)
                        attn_head(At, lambda kb, s=s: Kt[s][:, kb * 128:(kb + 1) * 128], b_Kt[s],
                                  lambda kb, s=s: Vt[s][:, kb, :], b_Vt[s], Qt[s], b_Qt[s], kb_list, zb, tb, ob)
                        I(act, lambda s=s, ob=ob: nc.scalar.copy(out=oTa[s][:], in_=psF[ob][:, 0:NQ]), W=[b_oTa[s], bF[ob]])
                        dma(sp, OTs[h, :, g * NQ:(g + 1) * NQ], oTa[s][:], R=[b_oTa[s]])
                K.barrier(scr)

        def ffn_tail(stack, ntok_blocks, n, x1, b_x1, h2T, b_h2T, gt2Bt, b_gt2, y_dst, ts, tmp):
            NT = ntok_blocks * n if ntok_blocks > 1 else n
            actT, b_actT, wt, b_wt, yst, b_yst = tmp
            wi_ = [0]
            for cg in range(16):
                s = wi_[0] % 2; wi_[0] += 1
                dma(sp, wt[s][:], wff1S[cg], R=[b_wff1[cg]], W=[b_wt[s]])
                for cc in range(4):
                    hc = cg * 4 + cc
                    bk = next_bank()
                    for kc in range(16):
                        I(pe, lambda kc=kc, cc=cc, s=s, bk=bk: nc.tensor.matmul(psF[bk][:, 0:NT], wt[s][:, kc, cc * 128:(cc + 1) * 128], h2T[:, kc, 0:NT], start=(kc == 0), stop=(kc == 15)),
                          R=[b_wt[s], b_h2T], W=[bF[bk]], sig=(kc == 15))
                    I(act, lambda hc=hc, bk=bk: nc.scalar.activation(out=actT[:, hc, 0:NT], in_=psF[bk][:, 0:NT], func=AF.Relu), W=[b_actT, bF[bk]])
                    I(pool, lambda hc=hc: nc.gpsimd.tensor_tensor(out=actT[:, hc, 0:NT], in0=actT[:, hc, 0:NT], in1=actT[:, hc, 0:NT], op=ALU.mult), W=[b_actT])
            banks = [0, 1, 2, 3]
            for cg in range(4):
                for kq in range(4):
                    s = wi_[0] % 2; wi_[0] += 1
                    dma(sp, wt[s][:], wff2S[cg, kq], R=[b_wff2[cg][kq]], W=[b_wt[s]])
                    for j in range(ntok_blocks):
                        for kc in range(16):
                            hc = kq * 16 + kc
                            I(pe, lambda kc=kc, hc=hc, j=j, s=s: nc.tensor.matmul(psF[banks[j]][0:n, :], actT[:, hc, j * n:(j + 1) * n], wt[s][:, kc, :], start=(hc == 0), stop=(hc == 63)),
                              R=[b_wt[s], b_actT], W=[bF[banks[j]]], sig=(kc == 15))
                for j in range(ntok_blocks):
                    I(dve, lambda j=j, cg=cg: nc.vector.tensor_tensor(out=yst[0:n, :], in0=psF[banks[j]][0:n, :], in1=gt2Bt(j)[0:n, cg * 512:(cg + 1) * 512], op=ALU.mult),
                      R=[b_gt2], W=[b_yst, bF[banks[j]]])
                    I(dve, lambda j=j, cg=cg: nc.vector.tensor_tensor(out=x1[0:n, j, cg * 512:(cg + 1) * 512], in0=x1[0:n, j, cg * 512:(cg + 1) * 512], in1=yst[0:n, :], op=ALU.add),
                      R=[b_yst], W=[b_x1])
            for j in range(ntok_blocks):
                dma(act, y_dst(j), x1[0:n, j, :], R=[b_x1])

        def out_proj(ntok_blocks, n, oT_j, b_oT, x1, b_x1, gt1Bt, b_gt1, wt, b_wt, yst, b_yst):
            for cg in range(4):
                s = cg % 2
                dma(sp, wt[s][:], woutS[cg], R=[b_wout[cg]], W=[b_wt[s]])
                for j in range(ntok_blocks):
                    bk = next_bank()
                    for kc in range(16):
                        I(pe, lambda kc=kc, j=j, s=s, bk=bk: nc.tensor.matmul(psF[bk][0:n, :], oT_j(j, kc), wt[s][:, kc, :], start=(kc == 0), stop=(kc == 15)),
                          R=[b_oT, b_wt[s]], W=[bF[bk]], sig=(kc == 15))
                    I(dve, lambda cg=cg, bk=bk, j=j: nc.vector.tensor_tensor(out=yst[0:n, :], in0=psF[bk][0:n, :], in1=gt1Bt(j)[0:n, cg * 512:(cg + 1) * 512], op=ALU.mult),
                      R=[b_gt1], W=[b_yst, bF[bk]])
                    I(dve, lambda cg=cg, j=j: nc.vector.tensor_tensor(out=x1[0:n, j, cg * 512:(cg + 1) * 512], in0=x1[0:n, j, cg * 512:(cg + 1) * 512], in1=yst[0:n, :], op=ALU.add),
                      R=[b_yst], W=[b_x1])

        def stage_C():
            with ExitStack() as SC:
                gt1B = K.sb(SC, "gt1B", (128, D), F32); gt2B = K.sb(SC, "gt2B", (128, D), F32); b_gtB = Buf()
                with ExitStack() as S1:
                    row_bcast(S1, gt1B, modT[:, 32:48, 0], b_gtB)
                    row_bcast(S1, gt2B, modT[:, 80:96, 0], b_gtB)
                    K.barrier(scr)
                oT = K.sb(SC, "oT", (128, 16, 512), BF16); b_oT = Buf()
                x1 = K.sb(SC, "x1", (128, 4, D), F32); b_x1 = Buf()
                xnb = K.sb(SC, "xnbC", (128, D), BF16); b_xnb = Buf()
                ssx = K.sb(SC, "ssxC", (128, 1), F32); rsx = K.sb(SC, "rsxC", (128, 1), F32); b_sx = Buf()
                h2T = K.sb(SC, "h2T", (128, 16, 512), BF16); b_h2T = Buf()
                actT = K.sb(SC, "actT", (128, 64, 512), BF16); b_actT = Buf()
                wt = [K.sb(SC, "wtC%d" % i, (128, 16, 512), BF16) for i in range(2)]; b_wt = [Buf(), Buf()]
                yst = K.sb(SC, "yst", (128, 512), F32); b_yst = Buf()
                for g in range(cfg.nown // 4):
                    dma(sp, oT[:], OTs.rearrange("c d t -> d c t")[:, :, g * 512:(g + 1) * 512], W=[b_oT])
                    for j in range(4):
                        blk = 2 * (4 * g + j) + 1
                        dma(sp, x1[:, j, :], xs[blk * 128:(blk + 1) * 128, :], W=[b_x1])
                    out_proj(4, 128, lambda j, kc: oT[:, kc, j * 128:(j + 1) * 128], b_oT, x1, b_x1, lambda j: gt1B, b_gtB, wt, b_wt, yst, b_yst)
                    for j in range(4):
                        norm_transpose(128, x1[:, j, :], b_x1, xnb, b_xnb, ssx, rsx, b_sx,
                                       lambda c, j=j: h2T[:, c, j * 128:(j + 1) * 128], b_h2T, A2, modT[:, 48:64, :], 0)
                    ffn_tail(SC, 4, 128, x1, b_x1, h2T, b_h2T, lambda j: gt2B, b_gtB,
                             lambda j, g=g: y_o[(4 * g + j) * 128:(4 * g + j + 1) * 128, :], 0, (actT, b_actT, wt, b_wt, yst, b_yst))
                K.barrier(scr)

        def stage_S():
            n = TS
            nkb = PAST // 128
            with ExitStack() as SS:
                wt = [K.sb(SS, "wtS%d" % i, (128, 16, 512), BF16) for i in range(2)]; b_wt = [Buf(), Buf()]
                xinS = K.sb(SS, "xinS", (n, NS, D), F32); b_xinS = Buf()
                xnb = K.sb(SS, "xnbS", (n, D), BF16); b_xnb = Buf()
                ssx = K.sb(SS, "ssxS", (n, 1), F32); rsx = K.sb(SS, "rsxS", (n, 1), F32); b_sx = Buf()
                hTS = K.sb(SS, "hTS", (128, NS, 16, n), BF16); b_hTS = Buf()
                kstS = K.sb(SS, "kstS", (n, 4, 128), F32); b_kstS = Buf()
                vstS = K.sb(SS, "vstS", (n, 4, 128), F32); b_vstS = Buf()
                knbS = K.sb(SS, "knbS", (n, 512), BF16); b_knbS = Buf()
                junkq = K.sb(SS, "junkqS", (n, 128), F32); b_junkq = Buf()
                ssq4 = K.sb(SS, "ssq4S", (n, 4), F32); rs4 = K.sb(SS, "rs4S", (n, 4), F32); b_s4 = Buf()
                kTn = K.sb(SS, "kTn", (128, NS, 8, n), BF16); b_kTn = Buf()
                qTn = K.sb(SS, "qTn", (128, NS, 8, n), BF16); b_qTn = Buf()
                vnw = K.sb(SS, "vnw", (n, NS, 8, 128), BF16); b_vnw = Buf()
                rawS = K.sb(SS, "rawS", (128, NS, 16, 3 + n), BF16); b_rawS = Buf()
                rawH = K.sb(SS, "rawH", (128, NS, 16, 3), F32); b_rawH = Buf()
                rawlS = K.sb(SS, "rawlS", (128, NS, 16, 3), F32); b_rawlS = Buf()
                cvy = K.sb(SS, "cvyS", (128, n), F32); b_cvy = Buf()
                qTtS = K.sb(SS, "qTtS", (128, NS, 8, n), BF16); b_qTtS = Buf()
                kTtS = K.sb(SS, "kTtS", (128, NS, 8, n), BF16); b_kTtS = Buf()
                vBS = K.sb(SS, "vBS", (n, NS, 4, 256), BF16); b_vBS = Buf()
                gtsS = K.sb(SS, "gtsS", (n, NS, 8), F32); b_gtsS = Buf()
                sg32 = K.sb(SS, "sg32S", (n, 512), F32); b_sg32 = Buf()
                sggS = K.sb(SS, "sggS", (n, NS, 1024), BF16); b_sggS = Buf()
                obtS = K.sb(SS, "obtS", (n, 1024), BF16); b_obtS = Buf()
                oTS = K.sb(SS, "oTS", (128, NS, 16, n), BF16); b_oTS = Buf()
                M16 = mk_mlstm(SS, n)
                St = mk_state(SS)
                dma(sp, xinS[:], xsamp.rearrange("(s t) f -> t s f", t=n), W=[b_xinS])
                with nc.allow_non_contiguous_dma(reason="conv state transposed load (tiny)"):
                    for sb_ in range(NS):
                        for r_ in range(3):
                            dma(sp, rawH[:, sb_, :, r_], sconv[sb_, r_].rearrange("(c p) -> p c", p=128), W=[b_rawH])
                I(dve, lambda: nc.vector.tensor_copy(out=rawS[:, :, :, 0:3], in_=rawH[:]), R=[b_rawH], W=[b_rawS])
                for sb_ in range(NS):
                    norm_transpose(n, xinS[:, sb_, :], b_xinS, xnb, b_xnb, ssx, rsx, b_sx,
                                   lambda c, sb_=sb_: hTS[:, sb_, c, :], b_hTS, A1, modT, 1 + sb_)
                ckpt('S1')
                wsl = [0]
                for cg in A_ORDER:
                    s = wsl[0] % 2
                    wsl[0] += 1
                    ncol = 512 if cg < 14 else 8
                    dma(sp, wt[s][:, :, 0:ncol], winS[cg, :, :, 0:ncol], R=[b_win[cg]], W=[b_wt[s]])
                    for sb_ in range(NS):
                        if cg in (6, 7, 8, 9):
                            for cc in range(4):
                                ci = (cg - 6) * 4 + cc
                                bk = next_bank()
                                for kc in range(16):
                                    I(pe, lambda: nc.tensor.matmul(psF[bk][:, 0:n], wt[s][:, kc, cc * 128:(cc + 1) * 128], hTS[:, sb_, kc, :], start=(kc == 0), stop=(kc == 15)),
                                      R=[b_wt[s], b_hTS], W=[bF[bk]], sig=(kc == 15))
                                I(act, lambda: nc.scalar.copy(out=rawS[:, sb_, ci, 3:3 + n], in_=psF[bk][:, 0:n]), W=[b_rawS, bF[bk]])
                                I(dve, lambda: nc.vector.tensor_copy(out=rawlS[:, sb_, ci, :], in_=psF[bk][:, n - 3:n]), W=[b_rawlS, bF[bk]])
                            continue
                        bk = next_bank()
                        for kc in range(16):
                            I(pe, lambda: nc.tensor.matmul(psF[bk][0:n, 0:ncol], hTS[:, sb_, kc, :], wt[s][:, kc, 0:ncol], start=(kc == 0), stop=(kc == 15)),
                              R=[b_wt[s], b_hTS], W=[bF[bk]], sig=(kc == 15))
                        if cg in (2, 3):
                            h0 = (cg - 2) * 4
                            qknorm(n, bk, gkB, lambda h: kstS[:, h, :], junkq, b_junkq, ssq4, rs4, b_s4, b_kstS)
                            I(act, lambda: nc.scalar.copy(out=knbS[:], in_=kstS[:].rearrange("p h d -> p (h d)")), R=[b_kstS], W=[b_knbS])
                            for h in range(4):
                                I(pe, lambda: nc.tensor.transpose(psT[1][:, h * 128:h * 128 + n], knbS[:, h * 128:(h + 1) * 128], identB[0:n, 0:n]), R=[b_knbS, b_cb], W=[bT[1]], sig=(h == 3))
                            I(dve, lambda: nc.vector.tensor_copy(out=kTn[:, sb_, h0:h0 + 4, :], in_=psT[1][:, 0:512].rearrange("p (h t) -> p h t", h=4)[:, :, 0:n]), W=[b_kTn, bT[1]])
                            dma(act, ks_o[sb_].rearrange("h t d -> t h d")[:, h0:h0 + 4, :], kstS[:], R=[b_kstS])
                        elif cg in (4, 5):
                            h0 = (cg - 4) * 4
                            I(act, lambda: nc.scalar.copy(out=vstS[:].rearrange("p h d -> p (h d)"), in_=psF[bk][0:n, :]), W=[b_vstS, bF[bk]])
                            I(dve, lambda: nc.vector.tensor_copy(out=vnw[:, sb_, h0:h0 + 4, :], in_=vstS[:]), R=[b_vstS], W=[b_vnw])
                            dma(act, vs_o[sb_].rearrange("h t d -> t h d")[:, h0:h0 + 4, :], vstS[:], R=[b_vstS])
                        elif cg in (10, 11):
                            h0 = (cg - 10) * 2
                            I(act, lambda: nc.scalar.copy(out=vBS[:, sb_, h0:h0 + 2, :].rearrange("p h v -> p (h v)"), in_=psF[bk][0:n, :]), W=[b_vBS, bF[bk]])
                        elif cg == 14:
                            I(dve, lambda: nc.vector.tensor_copy(out=gtsS[:, sb_, :], in_=psF[bk][0:n, 0:8]), W=[b_gtsS, bF[bk]])
                        elif cg in (0, 1):
                            h0 = cg * 4
                            qknorm(n, bk, gqB, lambda h: knbS[:, h * 128:(h + 1) * 128], junkq, b_junkq, ssq4, rs4, b_s4, b_knbS)
                            for h in range(4):
                                I(pe, lambda: nc.tensor.transpose(psT[1][:, h * 128:h * 128 + n], knbS[:, h * 128:(h + 1) * 128], identB[0:n, 0:n]), R=[b_knbS, b_cb], W=[bT[1]], sig=(h == 3))
                            I(dve, lambda: nc.vector.tensor_copy(out=qTn[:, sb_, h0:h0 + 4, :], in_=psT[1][:, 0:512].rearrange("p (h t) -> p h t", h=4)[:, :, 0:n]), W=[b_qTn, bT[1]])
                        elif cg in (12, 13):
                            c0 = (cg - 12) * 512
                            I(act, lambda: nc.scalar.activation(out=sg32[:], in_=psF[bk][0:n, :], func=AF.Sigmoid), W=[b_sg32, bF[bk]])
                            I(dve, lambda: nc.vector.tensor_tensor(out=sggS[:, sb_, c0:c0 + 512], in0=sg32[:], in1=ghB[0:n, c0:c0 + 512], op=ALU.mult), R=[b_sg32, b_small], W=[b_sggS])
                ckpt('S2')
                with nc.allow_non_contiguous_dma(reason="conv state transposed store (tiny)"):
                    for sb_ in range(NS):
                        for r_ in range(3):
                            dma(act, convs_o[sb_, r_].rearrange("(c p) -> p c", p=128), rawlS[:, sb_, :, r_], R=[b_rawlS])
                for sb_ in range(NS):
                    for ci in range(16):
                        I(dve, lambda: nc.vector.tensor_scalar(out=cvy[:], in0=rawS[:, sb_, ci, 0:n], scalar1=wcT[:, ci, 0:1], scalar2=None, op0=ALU.mult), R=[b_rawS, b_small], W=[b_cvy])
                        for jt in range(1, 4):
                            I(dve, lambda: nc.vector.scalar_tensor_tensor(out=cvy[:], in0=rawS[:, sb_, ci, jt:jt + n], scalar=wcT[:, ci, jt:jt + 1], in1=cvy[:], op0=ALU.mult, op1=ALU.add),
                              R=[b_rawS, b_small], W=[b_cvy])
                        dst = qTtS[:, sb_, ci, :] if ci < 8 else kTtS[:, sb_, ci - 8, :]
                        I(act, lambda: nc.scalar.activation(out=dst, in_=cvy[:], func=AF.Silu, bias=bcT[:, ci:ci + 1]), R=[b_cvy, b_small], W=[b_qTtS if ci < 8 else b_kTtS])
                ckpt('S3')
                for sb_ in range(NS):
                    for a_ in range(2):
                        dma(sp, St.C32[:, a_, :, :], sC[sb_].rearrange("h (a p) v -> a p h v", p=128)[a_], W=[St.b_C32])
                    with nc.allow_non_contiguous_dma(reason="tiny state vectors"):
                        for a_ in range(2):
                            dma(sp, St.n32[:, a_, :], sn[sb_].rearrange("h (a p) -> a p h", p=128)[a_], W=[St.b_n32])
                        dma(sp, St.mB[:], sm[sb_:sb_ + 1, :].partition_broadcast(128).rearrange("p a d -> p (a d)"), W=[St.b_m])
                    state_refresh_bf(St)
                    for _ in mlstm_block(M16, St, True,
                                lambda h, half: qTtS[:, sb_, 2 * h + half, :], b_qTtS,
                                lambda h, half: kTtS[:, sb_, 2 * h + half, :], b_kTtS,
                                vBS[:, sb_, :, :], b_vBS, gtsS[:, sb_, :], b_gtsS,
                                sggS[:, sb_, :], b_sggS, obtS, b_obtS, dummy=False, sel="sel15"):
                        pass
                    for c in range(8):
                        I(pe, lambda: nc.tensor.transpose(psT[1][:, c * 128:c * 128 + n], obtS[:, c * 128:(c + 1) * 128], identB[0:n, 0:n]), R=[b_obtS, b_cb], W=[bT[1]], sig=(c == 7))
                    I(act, lambda: nc.scalar.copy(out=oTS[:, sb_, 8:16, :], in_=psT[1][:, :].rearrange("p (c t) -> p c t", c=8)[:, :, 0:n]), W=[b_oTS, bT[1]])
                    for a_ in range(2):
                        dma(act, Cs_o[sb_].rearrange("h (a p) v -> a p h v", p=128)[a_], St.C32[:, a_, :, :], R=[St.b_C32])
                    with nc.allow_non_contiguous_dma(reason="tiny state vectors"):
                        for a_ in range(2):
                            dma(act, ns_o[sb_].rearrange("h (a p) -> a p h", p=128)[a_], St.n32[:, a_, :], R=[St.b_n32])
                        dma(act, ms_o[sb_:sb_ + 1, :], St.mB[0:1, :], R=[St.b_m])
                ckpt('S4')
                with ExitStack() as SAT:
                    At = mk_attn(SAT, n)
                    Kc = [K.sb(SAT, "Kc%d" % i, (128, nkb, 128), BF16) for i in range(2)]; b_Kc = [Buf(), Buf()]
                    Vc = [K.sb(SAT, "Vc%d" % i, (128, nkb, 128), BF16) for i in range(2)]; b_Vc = [Buf(), Buf()]
                    KcT = [K.sb(SAT, "KcT%d" % i, (128, nkb * 128), BF16) for i in range(2)]; b_KcT = [Buf(), Buf()]
                    it = 0
                    for sb_ in range(NS):
                        for h in range(8):
                            s = it % 2
                            it += 1
                            for b0 in range(0, nkb, 16):
                                b1 = min(nkb, b0 + 16)
                                dma(pool, Kc[s][:, b0:b1, :], ck[sb_, h].rearrange("(b p) d -> p b d", p=128)[:, b0:b1, :], W=[b_Kc[s]])
                                dma(pool, Vc[s][:, b0:b1, :], cv[sb_, h].rearrange("(b p) d -> p b d", p=128)[:, b0:b1, :], W=[b_Vc[s]])
                            for kb in range(nkb):
                                I(pe, lambda: nc.tensor.transpose(psT[s][:, (kb % 8) * 128:(kb % 8 + 1) * 128], Kc[s][:, kb, :], identB[:]), R=[b_Kc[s], b_cb], W=[bT[s]], sig=(kb % 8 == 7 or kb == nkb - 1))
                                if kb % 8 == 7 or kb == nkb - 1:
                                    k0 = (kb // 8) * 8
                                    w = (kb - k0 + 1) * 128
                                    I(dve, lambda: nc.vector.tensor_copy(out=KcT[s][:, k0 * 128:k0 * 128 + w], in_=psT[s][:, 0:w]), W=[b_KcT[s], bT[s]])
                            kb_list = [("new", n, 0, True, False)] + [(kb, 128, 0, False, False) for kb in reversed(range(nkb))]
                            zb, tb, ob = (0, 1, 2) if s == 0 else (3, 4, 5)
                            Ksl = lambda kb: kTn[:, sb_, h, :] if kb == "new" else KcT[s][:, kb * 128:(kb + 1) * 128]
                            Vsl = lambda kb: vnw[:, sb_, h, :] if kb == "new" else Vc[s][:, kb, :]
                            attn_head(At, Ksl, b_KcT[s], Vsl, b_Vc[s], qTn[:, sb_, h, :], b_qTn, kb_list, zb, tb, ob, extraR=[b_kTn, b_vnw])
                            I(act, lambda: nc.scalar.copy(out=oTS[:, sb_, h, :], in_=psF[ob][:, 0:n]), W=[b_oTS, bF[ob]])
                    K.barrier(scr)
                ckpt('S5')
                gtS = [[K.sb(SS, "gtS%d_%d" % (i, j), (128, D), F32) for j in range(NS)] for i in range(2)]; b_gtS = Buf()
                with ExitStack() as S1:
                    for sb_ in range(NS):
                        row_bcast(S1, gtS[0][sb_], modT[:, 32:48, 1 + sb_], b_gtS)
                        row_bcast(S1, gtS[1][sb_], modT[:, 80:96, 1 + sb_], b_gtS)
                    K.barrier(scr)
                ckpt('S6')
                yst = K.sb(SS, "ystS", (n, 512), F32); b_yst = Buf()
                h2TS = K.sb(SS, "h2TS", (128, 16, NS * n), BF16); b_h2TS = Buf()
                actT = K.sb(SS, "actTS", (128, 64, NS * n), BF16); b_actT = Buf()
                out_proj(NS, n, lambda j, kc: oTS[:, j, kc, :], b_oTS, xinS, b_xinS, lambda j: gtS[0][j], b_gtS, wt, b_wt, yst, b_yst)
                ckpt('S7')
                for sb_ in range(NS):
                    norm_transpose(n, xinS[:, sb_, :], b_xinS, xnb, b_xnb, ssx, rsx, b_sx,
                                   lambda c, sb_=sb_: h2TS[:, c, sb_ * n:(sb_ + 1) * n], b_h2TS, A2, modT[:, 48:64, :], 1 + sb_)
                ckpt('S8')
                ffn_tail(SS, NS, n, xinS, b_xinS, h2TS, b_h2TS, lambda j: gtS[1][j], b_gtS,
                         lambda j: ys_o[j * n:(j + 1) * n, :], 0, (actT, b_actT, wt, b_wt, yst, b_yst))
                K.barrier(scr)

        ckpt("setup")
        stage_A()
        ckpt("A")
        stage_B()
        ckpt("B")
        stage_C()
        ckpt("C")
        stage_S()


_PROG_CACHE = {}


def _core_inputs(c, inp, cfg, consts):
    b, p = c // 2, c % 2
    f = lambda a: np.ascontiguousarray(np.asarray(a, dtype=np.float32))
    x = np.asarray(inp["x_prompt"][b], dtype=np.float32)
    if p == 1:
        xs = x
    else:
        xs = np.concatenate([np.zeros((128, D), np.float32), x[:cfg.T - 128]], axis=0)
    cm = np.zeros((128, 2), np.float32)
    if p == 1:
        cm[:, 0] = 1.0
    else:
        cm[:, 1] = NEG
    ns = cfg.ns
    sl = slice(ns * c, ns * c + ns)
    m = {
        "xs": f(xs), "cm": cm,
        "cvec": f(np.concatenate([np.asarray(inp["c_prompt"])[b:b + 1], np.asarray(inp["c_sample"])[sl]], axis=0)),
        "xsamp": f(np.asarray(inp["x_sample"])[sl].reshape(ns * cfg.ts, D)),
        "ck": f(np.asarray(inp["cache_k"])[0, sl]), "cv": f(np.asarray(inp["cache_v"])[0, sl]),
        "sC": f(np.asarray(inp["state_C"])[0, sl]), "sn": f(np.asarray(inp["state_n"])[0, sl]),
        "sm": f(np.asarray(inp["state_m"])[0, sl]), "sconv": f(np.asarray(inp["state_conv"])[0, sl]),
        "consts": consts,
    }
    for k in ("w_ada", "b_ada", "g_norm1", "w_in", "g_q", "g_k", "w_conv", "b_conv", "b_i", "b_f", "g_h",
              "w_out", "g_norm2", "w_ff1", "w_ff2"):
        m[k] = f(np.asarray(inp[k])[0])
    return m


def run_cores(inp, cores):
    B, SEQ, _ = inp["x_prompt"].shape
    DEC_B, TS, _ = inp["x_sample"].shape
    PAST = inp["cache_k"].shape[3]
    cfg = Cfg(nblk=SEQ // 128, past=PAST, ts=TS, ns=DEC_B // 8)
    key = (cfg.nblk, cfg.past, cfg.ts, cfg.ns)
    if key not in _PROG_CACHE:
        _PROG_CACHE[key] = build_program(cfg)
    nc = _PROG_CACHE[key]
    consts = make_consts()
    in_maps = [_core_inputs(c, inp, cfg, consts) for c in cores]
    res = run_bass_kernel_spmd(nc, in_maps, core_ids=list(range(len(cores))))
    return cfg, res.results


def kernel(**inp):
    B, SEQ, _ = inp["x_prompt"].shape
    DEC_B, TS, _ = inp["x_sample"].shape
    cfg, res = run_cores(inp, list(range(8)))
    ns = cfg.ns
    y_prompt = np.zeros((B, SEQ, D), np.float32)
    k_prompt = np.zeros((1, B, 8, SEQ, 128), np.float32)
    v_prompt = np.zeros((1, B, 8, SEQ, 128), np.float32)
    C_prompt = np.zeros((1, B, 4, 256, 256), np.float32)
    n_prompt = np.zeros((1, B, 4, 256), np.float32)
    m_prompt = np.zeros((1, B, 4), np.float32)
    conv_prompt = np.zeros((1, B, 3, 2048), np.float32)
    y_sample = np.zeros((DEC_B, TS, D), np.float32)
    k_sample = np.zeros((1, DEC_B, 8, TS, 128), np.float32)
    v_sample = np.zeros((1, DEC_B, 8, TS, 128), np.float32)
    C_sample = np.zeros((1, DEC_B, 4, 256, 256), np.float32)
    n_sample = np.zeros((1, DEC_B, 4, 256), np.float32)
    m_sample = np.zeros((1, DEC_B, 4), np.float32)
    conv_sample = np.zeros((1, DEC_B, 3, 2048), np.float32)
    for c in range(8):
        b, p = c // 2, c % 2
        r = res[c]
        yb = y_prompt[b].reshape(cfg.nblk, 128, D)
        yo = r["y"].reshape(cfg.nown, 128, D)
        if p == 1:
            yb[1::2] = yo
            k_prompt[0, b] = r["ko"]
            v_prompt[0, b] = r["vo"]
            C_prompt[0, b] = r["Co"]
            n_prompt[0, b] = r["no"]
            m_prompt[0, b] = r["mo"][0]
            conv_prompt[0, b] = r["convo"]
        else:
            yb[0::2] = yo
        sl = slice(ns * c, ns * c + ns)
        y_sample[sl] = r["ys"].reshape(ns, TS, D)
        k_sample[0, sl] = r["kso"]
        v_sample[0, sl] = r["vso"]
        C_sample[0, sl] = r["Cso"]
        n_sample[0, sl] = r["nso"]
        m_sample[0, sl] = r["mso"]
        conv_sample[0, sl] = r["convso"]
    return (y_prompt, y_sample, k_prompt, v_prompt, C_prompt, n_prompt, m_prompt, conv_prompt,
            k_sample, v_sample, C_sample, n_sample, m_sample, conv_sample)
```

```python
from contextlib import ExitStack

import numpy as np

import concourse.bass as bass
import concourse.mybir as mybir
from concourse.bass_utils import run_bass_kernel_spmd

F32 = mybir.dt.float32
BF16 = mybir.dt.bfloat16
AF = mybir.ActivationFunctionType
ALU = mybir.AluOpType
AX = mybir.AxisListType

D = 2048
DIN = 7176
DFF = 8192
EPS = 1e-6
NEG = -30000.0
NCG = 15


class StopBuild(Exception):
    pass


class Cfg:
    def __init__(self, nblk=64, past=4096, ts=16, ns=2):
        self.nblk = nblk
        self.past = past
        self.ts = ts
        self.ns = ns
        self.T = nblk * 128
        self.nown = nblk // 2
        self.Town = self.nown * 128
        self.ngrp = nblk // 4


class Eng:
    def __init__(self, k, e, name):
        self.k = k
        self.e = e
        self.name = name
        self.sem = k.es.enter_context(k.nc.semaphore("s_" + name))
        self.count = 0
        self.seen = {}

    def wait(self, tok):
        if tok is None:
            return
        src, v = tok
        if src is self and self.name == "pe":
            return
        if self.seen.get(id(src), 0) >= v:
            return
        self.e.wait_ge(src.sem, v)
        self.seen[id(src)] = v

    def signal(self, ins):
        self.count += 1
        ins.then_inc(self.sem, 1)
        return (self, self.count)


class DmaSem:
    def __init__(self, k, name):
        self.sem = k.es.enter_context(k.nc.semaphore("d_" + name))
        self.count = 0


class Buf:
    def __init__(self, name=""):
        self.name = name
        self.w = None
        self.r = {}

    def note_read(self, tok):
        self.r[id(tok[0])] = tok

    def note_write(self, tok):
        self.w = tok
        self.r = {}


class Builder:
    def __init__(self, nc, es):
        self.nc = nc
        self.es = es
        self.pe = Eng(self, nc.tensor, "pe")
        self.act = Eng(self, nc.scalar, "act")
        self.dve = Eng(self, nc.vector, "dve")
        self.pool = Eng(self, nc.gpsimd, "pool")
        self.sp = Eng(self, nc.sync, "sp")
        self.pe_flush = None
        self.dsems = []
        self.ring = []
        self.ring_i = 0
        self.nt = 0

    def sb(self, stack, name, shape, dt):
        self.nt += 1
        return stack.enter_context(self.nc.sbuf_tensor("%s_%d" % (name, self.nt), list(shape), dt))

    def ps(self, stack, name, shape, dt):
        self.nt += 1
        return stack.enter_context(self.nc.psum_tensor("%s_%d" % (name, self.nt), list(shape), dt))

    def dsem(self, name):
        d = DmaSem(self, name + str(len(self.dsems)))
        self.dsems.append(d)
        return d

    def I(self, eng, fn, R=(), W=(), sig=True):
        for b in R:
            eng.wait(b.w)
        for b in W:
            eng.wait(b.w)
            for t in list(b.r.values()):
                eng.wait(t)
        ins = fn()
        if not hasattr(eng, "pend"):
            eng.pend = ([], [])
        if sig:
            tok = eng.signal(ins)
            if eng.name == "pe" and self.pe_flush is not None:
                self.nc.tensor.ldweights(self.pe_flush)
            for b in list(R) + eng.pend[0]:
                b.note_read(tok)
            for b in list(W) + eng.pend[1]:
                b.note_write(tok)
            eng.pend = ([], [])
            return tok
        eng.pend[0].extend(R)
        eng.pend[1].extend(W)
        return None

    def dma(self, q, out, in_, R=(), W=(), ds=None, **kw):
        for b in R:
            q.wait(b.w)
        for b in W:
            q.wait(b.w)
            for t in list(b.r.values()):
                q.wait(t)
        if ds is None:
            if len(self.ring) < 24:
                ds = self.dsem("r")
                self.ring.append(ds)
            else:
                ds = self.ring[self.ring_i % len(self.ring)]
                self.ring_i += 1
                q.wait((ds, ds.count))
        ins = q.e.dma_start(out=out, in_=in_, **kw)
        ins.then_inc(ds.sem, 16)
        ds.count += 16
        tok = (ds, ds.count)
        for b in R:
            b.note_read(tok)
        for b in W:
            b.note_write(tok)
        return tok

    def barrier(self, sc):
        nc = self.nc
        for d in self.dsems:
            if d.count:
                self.sp.wait((d, d.count))
        t_pe = self.I(self.pe, lambda: nc.tensor.matmul(sc["ps"][0:1, 0:2], sc["one_b"][0:1, 0:1], sc["one_b"][0:1, 0:2], start=True, stop=True), W=[sc["ps_buf"]])
        t_act = self.act.signal(nc.scalar.copy(out=sc["sa"][0:1, 0:1], in_=sc["one_f"][0:1, 0:1]))
        t_dve = self.dve.signal(nc.vector.tensor_copy(out=sc["sd"][0:1, 0:1], in_=sc["one_f"][0:1, 0:1]))
        t_pool = self.pool.signal(nc.gpsimd.tensor_copy(out=sc["sp"][0:1, 0:1], in_=sc["one_f"][0:1, 0:1]))
        for t in (t_pe, t_act, t_dve, t_pool):
            self.sp.wait(t)
        self.sp.count += 1
        nc.sync.sem_inc(self.sp.sem, 1)
        t_sp = (self.sp, self.sp.count)
        for e in (self.pe, self.act, self.dve, self.pool):
            e.wait(t_sp)
        return t_sp


CONST_NAMES = ["ident", "ones", "triinc", "causneg", "sel127", "sel15", "stri01", "trige", "causnegT"]


def make_consts():
    i = np.arange(128)
    c = {}
    c["ident"] = np.eye(128, dtype=np.float32)
    c["ones"] = np.ones((128, 128), np.float32)
    c["triinc"] = (i[:, None] <= i[None, :]).astype(np.float32)
    c["causneg"] = np.where(i[None, :] <= i[:, None], 0.0, NEG).astype(np.float32)
    s = np.zeros((128, 128), np.float32); s[127, :] = 1.0
    c["sel127"] = s
    s = np.zeros((128, 128), np.float32); s[15, :] = 1.0
    c["sel15"] = s
    c["stri01"] = (i[:, None] < i[None, :]).astype(np.float32)
    c["trige"] = (i[:, None] >= i[None, :]).astype(np.float32)
    c["causnegT"] = np.ascontiguousarray(c["causneg"].T)
    return np.concatenate([c[n] for n in CONST_NAMES], axis=1)


def build_program(cfg):
    nc = bass.Bass("TRN2", target_bir_lowering=False)
    try:
        _build_body(nc, cfg)
    except StopBuild:
        pass
    return nc


def _build_body(nc, cfg):
    T, Town, NB, NS, TS, PAST = cfg.T, cfg.Town, cfg.nblk, cfg.ns, cfg.ts, cfg.past
    NTOKS = NS * TS

    def din(name, shape):
        return nc.dram_tensor(name, list(shape), F32, kind="ExternalInput").ap()

    def dout(name, shape):
        return nc.dram_tensor(name, list(shape), F32, kind="ExternalOutput").ap()

    def dscr(name, shape, dt=BF16):
        return nc.dram_tensor(name, list(shape), dt).ap()

    xs = din("xs", (T, D))
    cmd = din("cm", (128, 2))
    cvec = din("cvec", (1 + NS, D))
    xsamp = din("xsamp", (NTOKS, D))
    ck = din("ck", (NS, 8, PAST, 128))
    cv = din("cv", (NS, 8, PAST, 128))
    sC = din("sC", (NS, 4, 256, 256))
    sn = din("sn", (NS, 4, 256))
    sm = din("sm", (NS, 4))
    sconv = din("sconv", (NS, 3, 2048))
    w_ada = din("w_ada", (D, 6 * D))
    b_ada = din("b_ada", (6 * D,))
    g_norm1 = din("g_norm1", (D,))
    w_in = din("w_in", (D, DIN))
    g_q = din("g_q", (128,))
    g_k = din("g_k", (128,))
    w_conv = din("w_conv", (4, 2048))
    b_conv = din("b_conv", (2048,))
    b_i = din("b_i", (4,))
    b_f = din("b_f", (4,))
    g_h = din("g_h", (4, 256))
    w_out = din("w_out", (D, D))
    g_norm2 = din("g_norm2", (D,))
    w_ff1 = din("w_ff1", (D, DFF))
    w_ff2 = din("w_ff2", (DFF, D))
    constd = din("consts", (128, 128 * len(CONST_NAMES)))

    y_o = dout("y", (Town, D))
    k_o = dout("ko", (8, T, 128))
    v_o = dout("vo", (8, T, 128))
    C_o = dout("Co", (4, 256, 256))
    n_o = dout("no", (4, 256))
    m_o = dout("mo", (1, 4))
    conv_o = dout("convo", (3, 2048))
    ys_o = dout("ys", (NTOKS, D))
    ks_o = dout("kso", (NS, 8, TS, 128))
    vs_o = dout("vso", (NS, 8, TS, 128))
    Cs_o = dout("Cso", (NS, 4, 256, 256))
    ns_o = dout("nso", (NS, 4, 256))
    ms_o = dout("mso", (NS, 4))
    convs_o = dout("convso", (NS, 3, 2048))

    winS = dscr("winS", (NCG, 128, 16, 512))
    woutS = dscr("woutS", (4, 128, 16, 512))
    wff1S = dscr("wff1S", (16, 128, 16, 512))
    wff2S = dscr("wff2S", (4, 4, 128, 16, 512))
    KTs = dscr("KTs", (8, 128, T))
    Vss = dscr("Vss", (8, 128, NB, 128))
    QTs = dscr("QTs", (8, 128, Town))
    OTs = dscr("OTs", (16, 128, Town))

    with ExitStack() as es:
        K = Builder(nc, es)
        I, dma = K.I, K.dma
        pe, act, dve, pool, sp = K.pe, K.act, K.dve, K.pool, K.sp
        P = es

        cst = K.sb(P, "cst", (128, 128 * len(CONST_NAMES)), F32)
        b_cst = Buf("cst")
        cF = {n: cst[:, i * 128:(i + 1) * 128] for i, n in enumerate(CONST_NAMES)}
        identB = K.sb(P, "identB", (128, 128), BF16)
        onesB = K.sb(P, "onesB", (128, 128), BF16)
        trigeB = K.sb(P, "trigeB", (128, 128), BF16)
        b_cb = Buf("cstb")
        cmt = K.sb(P, "cmt", (128, 2), F32)
        b_cm = Buf("cm")
        modT = K.sb(P, "modT", (128, 96, 1 + NS), F32)
        A1 = K.sb(P, "A1", (128, 16, 1 + NS), F32)
        A2 = K.sb(P, "A2", (128, 16, 1 + NS), F32)
        b_mod = Buf("mod")
        gqB = K.sb(P, "gqB", (128, 128), F32)
        gkB = K.sb(P, "gkB", (128, 128), F32)
        ghB = K.sb(P, "ghB", (128, 1024), F32)
        biB = K.sb(P, "biB", (128, 4), F32)
        bfB = K.sb(P, "bfB", (128, 4), F32)
        wcT = K.sb(P, "wcT", (128, 16, 4), F32)
        bcT = K.sb(P, "bcT", (128, 16), F32)
        b_small = Buf("small")
        epsT = K.sb(P, "epsT", (128, 1), F32)
        scr = {"one_f": cF["ones"], "one_b": onesB,
               "sa": K.sb(P, "bsa", (1, 2), F32), "sd": K.sb(P, "bsd", (1, 2), F32), "sp": K.sb(P, "bsp", (1, 2), F32)}
        psF = [K.ps(P, "psF%d" % i, (128, 512), F32) for i in range(6)]
        bF = [Buf("psF%d" % i) for i in range(6)]
        psT = [K.ps(P, "psT%d" % i, (128, 1024), BF16) for i in range(2)]
        bT = [Buf("psT%d" % i) for i in range(2)]
        scr["ps"] = psF[0]
        scr["ps_buf"] = bF[0]
        flushw = K.sb(P, "flushw", (128, 2), BF16)
        nc.vector.memset(flushw[:], 0.0)
        K.pe_flush = flushw[:, 0:1]
        import os as _os
        _stop = _os.environ.get("KSTOP", "")

        def ckpt(tag):
            if _stop == tag:
                K.barrier(scr)
                raise StopBuild()

        dma(sp, cst[:], constd, W=[b_cst])
        dma(sp, cmt[:], cmd, W=[b_cm])
        I(dve, lambda: nc.vector.tensor_copy(out=identB[:], in_=cF["ident"]), R=[b_cst], W=[b_cb])
        I(dve, lambda: nc.vector.tensor_copy(out=onesB[:], in_=cF["ones"]), R=[b_cst], W=[b_cb])
        I(dve, lambda: nc.vector.tensor_copy(out=trigeB[:], in_=cF["trige"]), R=[b_cst], W=[b_cb])
        I(dve, lambda: nc.vector.memset(epsT[:], EPS), W=[b_small])
        dma(sp, gqB[:], g_q.unsqueeze(0).partition_broadcast(128).rearrange("p a d -> p (a d)"), W=[b_small])
        dma(sp, gkB[:], g_k.unsqueeze(0).partition_broadcast(128).rearrange("p a d -> p (a d)"), W=[b_small])
        dma(sp, ghB[:], g_h.rearrange("h d -> (h d)").unsqueeze(0).partition_broadcast(128).rearrange("p a d -> p (a d)"), W=[b_small])
        dma(sp, biB[:], b_i.unsqueeze(0).partition_broadcast(128).rearrange("p a d -> p (a d)"), W=[b_small])
        dma(sp, bfB[:], b_f.unsqueeze(0).partition_broadcast(128).rearrange("p a d -> p (a d)"), W=[b_small])

        ckpt("s1")
        b_win = [Buf("win%d" % i) for i in range(NCG)]
        b_wout = [Buf("wout%d" % i) for i in range(4)]
        b_wff1 = [Buf("wff1%d" % i) for i in range(16)]
        b_wff2 = [[Buf("wff2%d_%d" % (i, j)) for j in range(4)] for i in range(4)]
        cvt_sems = [K.dsem("cvt") for _ in range(3)]
        ncv = [0]

        def convert(dst, src, buf):
            ds = cvt_sems[ncv[0] % 3]
            ncv[0] += 1
            pool.wait((ds, ds.count))
            dma(pool, dst, src, W=[buf], ds=ds)

        A_ORDER = [2, 3, 4, 5, 8, 9, 10, 11, 14, 6, 7, 0, 1, 12, 13]
        for cg in A_ORDER:
            ncol = 512 if cg < 14 else 8
            convert(winS[cg, :, :, 0:ncol],
                    w_in[:, cg * 512:cg * 512 + ncol].rearrange("(kc p) n -> p kc n", p=128), b_win[cg])

        ckpt("s2")
        with ExitStack() as S0:
            cT = K.sb(S0, "cT", (1 + NS, D), F32)
            b_cT = Buf()
            dma(sp, cT[:], cvec, W=[b_cT])
            I(act, lambda: nc.scalar.activation(out=cT[:], in_=cT[:], func=AF.Silu), R=[], W=[b_cT])
            scT = K.sb(S0, "scT", (128, 16, 1 + NS), F32)
            b_scT = Buf()
            nr = 1 + NS
            for c in range(16):
                I(pe, lambda c=c: nc.tensor.transpose(psF[0][:, c * nr:(c + 1) * nr], cT[:, c * 128:(c + 1) * 128], cF["ident"][0:nr, 0:nr]),
                  R=[b_cT, b_cst], W=[bF[0]], sig=(c == 15))
            I(dve, lambda: nc.vector.tensor_copy(out=scT[:].rearrange("p a b -> p (a b)"), in_=psF[0][:, 0:16 * nr]), R=[bF[0]], W=[b_scT])
            ckpt("s3")
            rowsT = K.sb(S0, "rowsT", (128, 128), F32)
            b_rows = Buf()
            dma(sp, rowsT[0:96, :], b_ada.rearrange("(c p) -> c p", p=128), W=[b_rows])
            dma(sp, rowsT[96:112, :], g_norm1.rearrange("(c p) -> c p", p=128), W=[b_rows])
            dma(sp, rowsT[112:128, :], g_norm2.rearrange("(c p) -> c p", p=128), W=[b_rows])
            rows2 = K.sb(S0, "rows2", (128, 128), F32)
            I(dve, lambda: nc.vector.memset(rows2[:], 0.0), W=[b_rows])
            dma(sp, rows2[0:64, :], w_conv.rearrange("j (c p) -> (j c) p", p=128), W=[b_rows])
            dma(sp, rows2[64:80, :], b_conv.rearrange("(c p) -> c p", p=128), W=[b_rows])
            colsT = K.sb(S0, "colsT", (128, 256), F32)
            b_cols = Buf()
            I(pe, lambda: nc.tensor.transpose(psF[1][:, 0:128], rowsT[:], cF["ident"]), R=[b_rows, b_cst], W=[bF[1]], sig=False)
            I(pe, lambda: nc.tensor.transpose(psF[1][:, 128:256], rows2[:], cF["ident"]), R=[b_rows, b_cst], W=[bF[1]])
            I(dve, lambda: nc.vector.tensor_copy(out=colsT[:], in_=psF[1][:, 0:256]), R=[bF[1]], W=[b_cols])
            I(dve, lambda: nc.vector.tensor_copy(out=wcT[:], in_=colsT[:, 128:192].rearrange("p (j c) -> p c j", j=4)), R=[b_cols], W=[b_small])
            I(dve, lambda: nc.vector.tensor_copy(out=bcT[:], in_=colsT[:, 192:208]), R=[b_cols], W=[b_small])
            ckpt("s4")
            wpan = [K.sb(S0, "wpan%d" % i, (128, 16, 512), F32) for i in range(2)]
            b_wpan = [Buf(), Buf()]
            for jg in range(24):
                s = jg % 2
                dma(sp, wpan[s][:], w_ada[:, jg * 512:(jg + 1) * 512].rearrange("(kc p) n -> p kc n", p=128), W=[b_wpan[s]])
                for jj in range(4):
                    j = jg * 4 + jj
                    for kc in range(16):
                        I(pe, lambda j=j, jj=jj, s=s, kc=kc: nc.tensor.matmul(psF[2][:, j * nr:(j + 1) * nr], wpan[s][:, kc, jj * 128:(jj + 1) * 128],
                                                                       scT[:, kc, :], start=(kc == 0), stop=(kc == 15)),
                          R=[b_wpan[s], b_scT], W=[bF[2]], sig=(kc == 15 and jj == 3))
            ckpt("s5")
            I(dve, lambda: nc.vector.tensor_tensor(out=modT[:], in0=psF[2][:, 0:96 * nr].rearrange("p (a b) -> p a b", b=nr),
                                                   in1=colsT[:, 0:96].unsqueeze(2).broadcast_to([128, 96, nr]), op=ALU.add),
              R=[bF[2], b_cols], W=[b_mod])
            ckpt("s6")
            I(dve, lambda: nc.vector.scalar_tensor_tensor(out=A1[:], in0=modT[:, 16:32, :], scalar=1.0,
                                                          in1=colsT[:, 96:112].unsqueeze(2).broadcast_to([128, 16, nr]), op0=ALU.add, op1=ALU.mult),
              R=[b_mod, b_cols], W=[b_mod])
            I(dve, lambda: nc.vector.scalar_tensor_tensor(out=A2[:], in0=modT[:, 64:80, :], scalar=1.0,
                                                          in1=colsT[:, 112:128].unsqueeze(2).broadcast_to([128, 16, nr]), op0=ALU.add, op1=ALU.mult),
              R=[b_mod, b_cols], W=[b_mod])
            K.barrier(scr)

        _dgc = {}

        def row_bcast(stack, dst, srcT, dstbuf):
            if id(stack) not in _dgc:
                _dgc[id(stack)] = (K.sb(stack, "dg", (128, 16, 128), F32), Buf())
            dg, b_dg = _dgc[id(stack)]
            I(dve, lambda: nc.vector.tensor_tensor(out=dg[:], in0=cF["ident"].unsqueeze(1).broadcast_to([128, 16, 128]),
                                                   in1=srcT.unsqueeze(2).broadcast_to([128, 16, 128]), op=ALU.mult),
              R=[b_cst, b_mod], W=[b_dg])
            for q in range(4):
                I(pe, lambda q=q: nc.tensor.matmul(psF[q][:], cF["ones"], dg[:, 4 * q:4 * q + 4, :].rearrange("p a b -> p (a b)"), start=True, stop=True),
                  R=[b_dg, b_cst], W=[bF[q]])
                I(dve, lambda q=q: nc.vector.tensor_copy(out=dst[:, q * 512:(q + 1) * 512], in_=psF[q][:]), R=[bF[q]], W=[dstbuf])

        ckpt("s8")
        for cg in range(4):
            convert(woutS[cg], w_out[:, cg * 512:(cg + 1) * 512].rearrange("(kc p) n -> p kc n", p=128), b_wout[cg])
        for cg in range(16):
            convert(wff1S[cg], w_ff1[:, cg * 512:(cg + 1) * 512].rearrange("(kc p) n -> p kc n", p=128), b_wff1[cg])
        for cg in range(4):
            for kq in range(4):
                convert(wff2S[cg, kq], w_ff2[kq * 2048:(kq + 1) * 2048, cg * 512:(cg + 1) * 512].rearrange("(kc p) n -> p kc n", p=128), b_wff2[cg][kq])

        SCALE_A = 128.0 ** -0.5
        rr = [0]

        def next_bank():
            i = 3 + rr[0] % 3
            rr[0] += 1
            return i

        def mk_mlstm(stack, L):
            M = type("M", (), {})()

            def t(name, shape, dt=F32):
                setattr(M, name, K.sb(stack, "m_" + name, shape, dt))
                setattr(M, "b_" + name, Buf(name))
            for nm in ["fz", "e4", "sp4", "ig", "g", "nb", "cmx", "na", "wi", "wia", "den", "r", "emt", "ssq", "rsd", "fac", "wst", "dsub"]:
                t(nm, (L, 4))
            t("am", (L, 8)); t("dn", (L, 8))
            t("dg", (L, 4, L)); t("Gm", (L, 4, L)); t("ET", (L, 4, L)); t("PT", (L, 4, L), BF16)
            t("cn4", (L, 4, L))
            t("qc", (L, 2, 256)); t("num", (L, 4, 256)); t("kTW", (L, 8, 128), BF16); t("junk", (L, 256))
            t("aLs", (128, 8)); t("dec", (128, 4)); t("dsb", (128, 4))
            I(dve, lambda: nc.vector.tensor_copy(out=M.cn4[:], in_=cF["causnegT"][0:L, 0:L].unsqueeze(1).broadcast_to([L, 4, L])), R=[b_cst], W=[M.b_cn4])
            M.L = L
            return M

        def mk_state(stack):
            St = type("St", (), {})()
            St.C32 = K.sb(stack, "C32", (128, 2, 4, 256), F32); St.b_C32 = Buf()
            St.Cbf = K.sb(stack, "Cbf", (128, 2, 4, 256), BF16); St.b_Cbf = Buf()
            St.n32 = K.sb(stack, "n32", (128, 2, 4), F32); St.b_n32 = Buf()
            St.nbf = K.sb(stack, "nbf", (128, 2, 4), BF16); St.b_nbf = Buf()
            St.mB = K.sb(stack, "mB", (128, 4), F32); St.b_m = Buf()
            return St

        def state_refresh_bf(St):
            I(act, lambda: nc.scalar.copy(out=St.Cbf[:].rearrange("p a h v -> p (a h v)"), in_=St.C32[:].rearrange("p a h v -> p (a h v)")), R=[St.b_C32], W=[St.b_Cbf])
            I(dve, lambda: nc.vector.tensor_copy(out=St.nbf[:].rearrange("p a h -> p (a h)"), in_=St.n32[:].rearrange("p a h -> p (a h)")), R=[St.b_n32], W=[St.b_nbf])

        def mlstm_block(M, St, own, qT, b_qT, kT, b_kT, vB, b_vB, gt, b_gt, sgg, b_sgg, ob, b_ob, dummy, sel):
            L = M.L
            idL = cF["ident"][0:L, 0:L]
            onL = cF["ones"][0:L, 0:L]
            b3 = lambda ap: ap.unsqueeze(2).broadcast_to([L, 4, L])
            m3 = lambda ap: ap.unsqueeze(1).broadcast_to([L, 4, L])
            p3 = lambda bank: psF[bank][0:L, 0:4 * L].rearrange("p (h s) -> p h s", h=4)
            I(dve, lambda: nc.vector.tensor_tensor(out=M.fz[:], in0=gt[:, 4:8], in1=bfB[0:L, :], op=ALU.add), R=[b_gt, b_small], W=[M.b_fz])
            I(act, lambda: nc.scalar.activation(out=M.e4[:], in_=M.fz[:], func=AF.Exp, scale=-1.0), R=[M.b_fz], W=[M.b_e4])
            I(act, lambda: nc.scalar.activation(out=M.sp4[:], in_=M.e4[:], func=AF.Ln, bias=1.0), R=[M.b_e4], W=[M.b_sp4])
            I(dve, lambda: nc.vector.tensor_tensor(out=M.ig[:], in0=gt[:, 0:4], in1=biB[0:L, :], op=ALU.add), R=[b_gt, b_small], W=[M.b_ig])
            if dummy:
                I(dve, lambda: nc.vector.tensor_scalar(out=M.ig[:], in0=M.ig[:], scalar1=cmt[0:L, 1:2], scalar2=None, op0=ALU.add), R=[b_cm], W=[M.b_ig])
                I(dve, lambda: nc.vector.tensor_scalar(out=M.sp4[:], in0=M.sp4[:], scalar1=cmt[0:L, 0:1], scalar2=None, op0=ALU.mult), R=[b_cm], W=[M.b_sp4])
            yield None
            I(pe, lambda: nc.tensor.matmul(psF[0][0:L, 0:4], cF["triinc"][0:L, 0:L], M.sp4[:], start=True, stop=True), R=[M.b_sp4, b_cst], W=[bF[0]])
            I(dve, lambda: nc.vector.tensor_tensor(out=M.g[:], in0=M.ig[:], in1=psF[0][0:L, 0:4], op=ALU.add), R=[M.b_ig], W=[M.b_g, bF[0]])
            I(dve, lambda: nc.vector.tensor_copy(out=M.nb[:], in_=psF[0][0:L, 0:4]), W=[M.b_nb, bF[0]])
            yield None
            I(dve, lambda: nc.vector.tensor_tensor(out=M.dg[:], in0=m3(idL), in1=b3(M.g[:]), op=ALU.mult), R=[M.b_g, b_cst], W=[M.b_dg])
            I(pe, lambda: nc.tensor.matmul(psF[1][0:L, 0:4 * L], onL, M.dg[:].rearrange("p h s -> p (h s)"), start=True, stop=True), R=[M.b_dg, b_cst], W=[bF[1]])
            I(dve, lambda: nc.vector.tensor_tensor(out=M.Gm[:], in0=p3(1), in1=m3(cF["causneg"][0:L, 0:L]), op=ALU.add), R=[b_cst], W=[M.b_Gm, bF[1]])
            I(dve, lambda: nc.vector.tensor_reduce(out=M.cmx[:], in_=M.Gm[:], axis=AX.X, op=ALU.max), R=[M.b_Gm], W=[M.b_cmx])
            I(dve, lambda: nc.vector.tensor_tensor(out=M.am[:, 0:4], in0=M.cmx[:], in1=St.mB[0:L, :], op=ALU.max), R=[M.b_cmx, St.b_m], W=[M.b_am])
            I(dve, lambda: nc.vector.tensor_tensor(out=M.am[:, 4:8], in0=M.am[:, 0:4], in1=M.nb[:], op=ALU.subtract), R=[M.b_nb], W=[M.b_am])
            I(pe, lambda: nc.tensor.matmul(psF[0][:, 8:16], cF[sel][0:L, :], M.am[:], start=True, stop=True), R=[M.b_am, b_cst], W=[bF[0]])
            I(dve, lambda: nc.vector.tensor_copy(out=M.aLs[:], in_=psF[0][:, 8:16]), W=[M.b_aLs, bF[0]])
            yield None
            if own:
                I(dve, lambda: nc.vector.tensor_scalar(out=M.na[:], in0=M.am[:, 0:4], scalar1=-1.0, scalar2=None, op0=ALU.mult), R=[M.b_am], W=[M.b_na])
                I(dve, lambda: nc.vector.tensor_tensor(out=M.dg[:], in0=m3(idL), in1=b3(M.na[:]), op=ALU.mult), R=[M.b_na, b_cst], W=[M.b_dg])
                I(pe, lambda: nc.tensor.matmul(psF[1][0:L, 0:4 * L], onL, M.dg[:].rearrange("p h s -> p (h s)"), start=True, stop=False), R=[M.b_dg, b_cst], W=[bF[1]], sig=False)
                I(pe, lambda: nc.tensor.matmul(psF[1][0:L, 0:4 * L], idL, M.cn4[:].rearrange("p h s -> p (h s)"), start=False, stop=True), R=[M.b_cn4, b_cst], W=[bF[1]])
                for h in range(4):
                    I(act, lambda h=h: nc.scalar.activation(out=M.ET[:, h, :], in_=psF[1][0:L, h * L:(h + 1) * L], func=AF.Exp, bias=M.g[:, h:h + 1], scale=1.0),
                      R=[M.b_g], W=[M.b_ET, bF[1]])
                yield None
                for h in range(4):
                    for half in range(2):
                        I(pe, lambda h=h, half=half: nc.tensor.matmul(psF[2][0:L, h * L:(h + 1) * L], kT(h, half), qT(h, half), start=(half == 0), stop=(half == 1)),
                          R=[b_kT, b_qT], W=[bF[2]], sig=(h == 3 and half == 1))
                I(dve, lambda: nc.vector.scalar_tensor_tensor(out=M.PT[:], in0=p3(2), scalar=1.0 / 16, in1=M.ET[:], op0=ALU.mult, op1=ALU.mult), R=[M.b_ET], W=[M.b_PT, bF[2]])
                yield None
                I(dve, lambda: nc.vector.tensor_tensor(out=M.wia[:], in0=St.mB[0:L, :], in1=M.am[:, 0:4], op=ALU.subtract), R=[St.b_m, M.b_am], W=[M.b_wia])
                I(act, lambda: nc.scalar.activation(out=M.wi[:], in_=M.wia[:], func=AF.Exp), R=[M.b_wia], W=[M.b_wi])
                for h in range(4):
                    I(pe, lambda h=h: nc.tensor.matmul(psF[0][0:L, 16 + h:17 + h], M.PT[:, h, :], onesB[0:L, 0:1], start=True, stop=True), R=[M.b_PT, b_cb], W=[bF[0]], sig=False)
                for h in range(4):
                    for half in range(2):
                        I(pe, lambda h=h, half=half: nc.tensor.matmul(psF[0][0:L, 20 + h:21 + h], qT(h, half), St.nbf[:, half, h:h + 1], start=(half == 0), stop=(half == 1)),
                          R=[b_qT, St.b_nbf], W=[bF[0]], sig=(h == 3 and half == 1))
                I(dve, lambda: nc.vector.tensor_copy(out=M.dn[:], in_=psF[0][0:L, 16:24]), W=[M.b_dn, bF[0]])
                I(dve, lambda: nc.vector.tensor_tensor(out=M.den[:], in0=M.dn[:, 4:8], in1=M.wi[:], op=ALU.mult), R=[M.b_dn, M.b_wi], W=[M.b_den])
                I(dve, lambda: nc.vector.tensor_tensor(out=M.den[:], in0=M.den[:], in1=M.dn[:, 0:4], op=ALU.add), R=[M.b_dn], W=[M.b_den])
                yield None
                I(act, lambda: nc.scalar.activation(out=M.emt[:], in_=M.am[:, 4:8], func=AF.Exp, scale=-1.0), R=[M.b_am], W=[M.b_emt])
                I(dve, lambda: nc.vector.tensor_scalar(out=M.dsub[:], in0=M.den[:], scalar1=-1.0, scalar2=None, op0=ALU.mult), R=[M.b_den], W=[M.b_dsub])
                I(dve, lambda: nc.vector.tensor_tensor(out=M.den[:], in0=M.den[:], in1=M.dsub[:], op=ALU.max), R=[M.b_dsub], W=[M.b_den])
                I(dve, lambda: nc.vector.tensor_tensor(out=M.den[:], in0=M.den[:], in1=M.emt[:], op=ALU.max), R=[M.b_emt], W=[M.b_den])
                I(dve, lambda: nc.vector.reciprocal(out=M.r[:], in_=M.den[:]), R=[M.b_den], W=[M.b_r])
                yield None
                for hp in range(2):
                    for hh in range(2):
                        h = 2 * hp + hh
                        I(pe, lambda h=h, hh=hh: nc.tensor.matmul(psF[1][0:L, hh * 256:(hh + 1) * 256], M.PT[:, h, :], vB[:, h, :], start=True, stop=True),
                          R=[M.b_PT, b_vB], W=[bF[1]], sig=(hh == 1))
                    for hh in range(2):
                        h = 2 * hp + hh
                        for half in range(2):
                            I(pe, lambda h=h, hh=hh, half=half: nc.tensor.matmul(psF[2][0:L, hh * 256:(hh + 1) * 256], qT(h, half), St.Cbf[:, half, h, :], start=(half == 0), stop=(half == 1)),
                              R=[b_qT, St.b_Cbf], W=[bF[2]], sig=(hh == 1 and half == 1))
                    for hh in range(2):
                        h = 2 * hp + hh
                        I(act, lambda h=h, hh=hh: nc.scalar.activation(out=M.qc[:, hh, :], in_=psF[2][0:L, hh * 256:(hh + 1) * 256], func=AF.Copy, scale=M.wi[:, h:h + 1]),
                          R=[M.b_wi], W=[M.b_qc, bF[2]])
                    I(dve, lambda hp=hp: nc.vector.tensor_tensor(out=M.num[:, 2 * hp:2 * hp + 2, :], in0=psF[1][0:L, 0:512].rearrange("p (a v) -> p a v", a=2), in1=M.qc[:], op=ALU.add),
                      R=[M.b_qc], W=[M.b_num, bF[1]])
                yield None
                for h in range(4):
                    I(act, lambda h=h: nc.scalar.activation(out=M.junk[:], in_=M.num[:, h, :], func=AF.Square, scale=M.r[:, h:h + 1], accum_out=M.ssq[:, h:h + 1]),
                      R=[M.b_num, M.b_r], W=[M.b_junk, M.b_ssq])
                I(act, lambda: nc.scalar.activation(out=M.rsd[:], in_=M.ssq[:], func=AF.Ln, scale=1.0 / 256, bias=epsT[0:L, :]), R=[M.b_ssq, b_small], W=[M.b_rsd])
                I(act, lambda: nc.scalar.activation(out=M.rsd[:], in_=M.rsd[:], func=AF.Exp, scale=-0.5), W=[M.b_rsd])
                I(dve, lambda: nc.vector.tensor_tensor(out=M.fac[:], in0=M.r[:], in1=M.rsd[:], op=ALU.mult), R=[M.b_r, M.b_rsd], W=[M.b_fac])
                for h in range(4):
                    I(dve, lambda h=h: nc.vector.scalar_tensor_tensor(out=ob[:, h * 256:(h + 1) * 256], in0=M.num[:, h, :], scalar=M.fac[:, h:h + 1], in1=sgg[:, h * 256:(h + 1) * 256], op0=ALU.mult, op1=ALU.mult),
                      R=[M.b_num, M.b_fac, b_sgg], W=[b_ob])
            yield None
            I(dve, lambda: nc.vector.tensor_tensor(out=M.dsub[:], in0=M.g[:], in1=M.aLs[0:L, 0:4], op=ALU.subtract), R=[M.b_g, M.b_aLs], W=[M.b_dsub])
            I(act, lambda: nc.scalar.activation(out=M.wst[:], in_=M.dsub[:], func=AF.Exp), R=[M.b_dsub], W=[M.b_wst])
            I(dve, lambda: nc.vector.tensor_tensor(out=M.dsb[:], in0=St.mB[:], in1=M.aLs[:, 0:4], op=ALU.subtract), R=[St.b_m, M.b_aLs], W=[M.b_dsb])
            I(act, lambda: nc.scalar.activation(out=M.dec[:], in_=M.dsb[:], func=AF.Exp), R=[M.b_dsb], W=[M.b_dec])
            for h in range(4):
                for half in range(2):
                    idx = 2 * h + half
                    I(pe, lambda h=h, half=half, idx=idx: nc.tensor.transpose(psT[1][0:L, idx * 128:(idx + 1) * 128], kT(h, half), identB[:]),
                      R=[b_kT, b_cb], W=[bT[1]], sig=(idx == 7))
            for h in range(4):
                I(dve, lambda h=h: nc.vector.tensor_scalar(out=M.kTW[:, 2 * h:2 * h + 2, :], in0=psT[1][0:L, 2 * h * 128:(2 * h + 2) * 128].rearrange("p (a d) -> p a d", a=2),
                                                           scalar1=M.wst[:, h:h + 1], scalar2=1.0 / 16, op0=ALU.mult, op1=ALU.mult),
                  R=[M.b_wst], W=[M.b_kTW, bT[1]])
            yield None
            for half in range(2):
                for hp in range(2):
                    bk = 1 + hp
                    for hh in range(2):
                        h = 2 * hp + hh
                        I(pe, lambda h=h, hh=hh, half=half, bk=bk: nc.tensor.matmul(psF[bk][:, hh * 256:(hh + 1) * 256], M.kTW[:, 2 * h + half, :], vB[:, h, :], start=True, stop=True),
                          R=[M.b_kTW, b_vB], W=[bF[bk]], sig=(hh == 1))
                    for hh in range(2):
                        h = 2 * hp + hh
                        I(dve, lambda h=h, hh=hh, half=half, bk=bk: nc.vector.scalar_tensor_tensor(out=St.C32[:, half, h, :], in0=St.C32[:, half, h, :], scalar=M.dec[:, h:h + 1],
                                                                                               in1=psF[bk][:, hh * 256:(hh + 1) * 256], op0=ALU.mult, op1=ALU.add),
                          R=[M.b_dec], W=[St.b_C32, bF[bk]])
            yield None
            for half in range(2):
                for h in range(4):
                    c0 = 24 + half * 4 + h
                    I(pe, lambda h=h, half=half, c0=c0: nc.tensor.matmul(psF[0][:, c0:c0 + 1], M.kTW[:, 2 * h + half, :], onesB[0:L, 0:1], start=True, stop=True),
                      R=[M.b_kTW, b_cb], W=[bF[0]], sig=(half == 1 and h == 3))
            for half in range(2):
                I(dve, lambda half=half: nc.vector.tensor_tensor(out=St.n32[:, half, :], in0=St.n32[:, half, :], in1=M.dec[:], op=ALU.mult), R=[M.b_dec], W=[St.b_n32])
            I(dve, lambda: nc.vector.tensor_tensor(out=St.n32[:].rearrange("p a h -> p (a h)"), in0=St.n32[:].rearrange("p a h -> p (a h)"), in1=psF[0][:, 24:32], op=ALU.add),
              W=[St.b_n32, bF[0]])
            state_refresh_bf(St)
            I(dve, lambda: nc.vector.tensor_copy(out=St.mB[:], in_=M.aLs[:, 4:8]), R=[M.b_aLs], W=[St.b_m])

        def rms_rstd(ssq_ap, out_ap, n, scale, b_in, b_out):
            I(act, lambda: nc.scalar.activation(out=out_ap, in_=ssq_ap, func=AF.Ln, scale=scale, bias=epsT[0:n, :]), R=[b_in, b_small], W=[b_out])
            I(act, lambda: nc.scalar.activation(out=out_ap, in_=out_ap, func=AF.Exp, scale=-0.5), W=[b_out])

        def qknorm(n, bank, gB, out_h, junk, b_junk, ssq4, rs4, b_s4, b_out):
            for h in range(4):
                I(act, lambda h=h: nc.scalar.activation(out=junk[0:n, 0:128], in_=psF[bank][0:n, h * 128:(h + 1) * 128], func=AF.Square, accum_out=ssq4[0:n, h:h + 1]),
                  W=[b_junk, b_s4, bF[bank]])
            rms_rstd(ssq4[0:n, :], rs4[0:n, :], n, 1.0 / 128, b_s4, b_s4)
            for h in range(4):
                I(dve, lambda h=h: nc.vector.scalar_tensor_tensor(out=out_h(h), in0=psF[bank][0:n, h * 128:(h + 1) * 128], scalar=rs4[0:n, h:h + 1], in1=gB[0:n, :], op0=ALU.mult, op1=ALU.mult),
                  R=[b_s4, b_small], W=[b_out, bF[bank]])

        def norm_transpose(n, x_ap, b_x, xn, b_xn, ssx, rsx, b_sx, hT_dst, b_hT, Amod, Bmod, ts, valid=False):
            I(act, lambda: nc.scalar.activation(out=xn[0:n, :], in_=x_ap, func=AF.Square, accum_out=ssx[0:n, :]), R=[b_x], W=[b_xn, b_sx])
            rms_rstd(ssx[0:n, :], rsx[0:n, :], n, 1.0 / D, b_sx, b_sx)
            I(dve, lambda: nc.vector.tensor_scalar(out=xn[0:n, :], in0=x_ap, scalar1=rsx[0:n, 0:1], scalar2=None, op0=ALU.mult), R=[b_x, b_sx], W=[b_xn])
            for rnd in range(2):
                for c in range(8 * rnd, 8 * rnd + 8):
                    I(pe, lambda c=c: nc.tensor.transpose(psT[0][:, (c % 8) * 128:(c % 8) * 128 + n], xn[0:n, c * 128:(c + 1) * 128], identB[0:n, 0:n]),
                      R=[b_xn, b_cb], W=[bT[0]], sig=(c % 8 == 7))
                for c in range(8 * rnd, 8 * rnd + 8):
                    src = psT[0][:, (c % 8) * 128:(c % 8) * 128 + n]
                    if c % 2 == 0:
                        I(act, lambda c=c, src=src: nc.scalar.activation(out=hT_dst(c), in_=src, func=AF.Identity, scale=Amod[:, c, ts:ts + 1], bias=Bmod[:, c, ts:ts + 1]),
                          R=[b_mod], W=[b_hT, bT[0]])
                    else:
                        I(dve, lambda c=c, src=src: nc.vector.tensor_scalar(out=hT_dst(c), in0=src, scalar1=Amod[:, c, ts:ts + 1], scalar2=Bmod[:, c, ts:ts + 1], op0=ALU.mult, op1=ALU.add),
                          R=[b_mod], W=[b_hT, bT[0]])

        def load_w(tile, b_tile, src, b_src):
            return dma(sp, tile[:], src, R=[b_src], W=[b_tile])

        def stage_A():
            with ExitStack() as SA:
                xin = K.sb(SA, "xin", (128, D), F32); b_xin = Buf()
                xnb = K.sb(SA, "xnb", (128, D), BF16); b_xnb = Buf()
                ssx = K.sb(SA, "ssx", (128, 1), F32); rsx = K.sb(SA, "rsx", (128, 1), F32); b_sx = Buf()
                hT = K.sb(SA, "hT", (128, 16, 512), BF16); b_hT = Buf()
                wt = [K.sb(SA, "wt%d" % i, (128, 16, 512), BF16) for i in range(2)]; b_wt = [Buf(), Buf()]
                kst = K.sb(SA, "kst", (128, 4, 128), F32); b_kst = Buf()
                vst = K.sb(SA, "vst", (128, 4, 128), F32); b_vst = Buf()
                knb = K.sb(SA, "knb", (128, 512), BF16); b_knb = Buf()
                junkq = K.sb(SA, "junkq", (128, 128), F32); b_junkq = Buf()
                ssq4 = K.sb(SA, "ssq4", (128, 4), F32); rs4 = K.sb(SA, "rs4", (128, 4), F32); b_s4 = Buf()
                kTst = K.sb(SA, "kTst", (128, 8, 512), BF16); b_kTst = Buf()
                vbst = K.sb(SA, "vbst", (128, 8, 4, 128), BF16); b_vbst = Buf()
                raw = K.sb(SA, "raw", (128, 16, 515), BF16); b_raw = Buf()
                rawl = K.sb(SA, "rawl", (128, 16, 3), F32); b_rawl = Buf()
                cvy = [K.sb(SA, "cvy%d" % i, (128, 512), F32) for i in range(2)]; b_cvy = [Buf(), Buf()]
                qTt2 = [K.sb(SA, "qTt%d" % i, (128, 8, 2, 128), BF16) for i in range(2)]; b_qTt2 = [Buf(), Buf()]
                kTt2 = [K.sb(SA, "kTt%d" % i, (128, 8, 512), BF16) for i in range(2)]; b_kTt2 = [Buf(), Buf()]
                vBt2 = [K.sb(SA, "vBt%d" % i, (128, 4, 4, 256), BF16) for i in range(2)]; b_vBt2 = [Buf(), Buf()]
                gts2 = [K.sb(SA, "gts%d" % i, (128, 4, 8), F32) for i in range(2)]; b_gts2 = [Buf(), Buf()]
                sgg2 = [K.sb(SA, "sgg%d" % i, (128, 2, 1024), BF16) for i in range(2)]; b_sgg2 = [Buf(), Buf()]
                sg32 = K.sb(SA, "sg32", (128, 512), F32); b_sg32 = Buf()
                obt = K.sb(SA, "obt", (128, 1024), BF16); b_obt = Buf()
                qTst = K.sb(SA, "qTst", (128, 8, 256), BF16); b_qTst = Buf()
                oTst = K.sb(SA, "oTst", (128, 8, 256), BF16); b_oTst = Buf()
                M = mk_mlstm(SA, 128)
                St = mk_state(SA)
                for tl, bb in ((St.C32, St.b_C32), (St.n32, St.b_n32), (St.mB, St.b_m)):
                    I(dve, lambda tl=tl: nc.vector.memset(tl[:], 0.0), W=[bb])
                state_refresh_bf(St)
                I(dve, lambda: nc.vector.memset(raw[:, :, 0:3], 0.0), W=[b_raw])

                wsl = [0]

                def genA(G):
                    for j in range(4):
                        blk = 4 * G + j
                        dma(sp, xin[:], xs[blk * 128:(blk + 1) * 128, :], W=[b_xin])
                        yield None
                        norm_transpose(128, xin[:], b_xin, xnb, b_xnb, ssx, rsx, b_sx,
                                       lambda c, j=j: hT[:, c, j * 128:(j + 1) * 128], b_hT, A1, modT, 0)
                        if blk == 0:
                            I(dve, lambda: nc.vector.tensor_scalar(out=hT[:, :, 0:128], in0=hT[:, :, 0:128], scalar1=cmt[:, 0:1], scalar2=None, op0=ALU.mult), R=[b_cm], W=[b_hT])
                    for cg in A_ORDER:
                        s = wsl[0] % 2
                        wsl[0] += 1
                        ncol = 512 if cg < 14 else 8
                        dma(sp, wt[s][:, :, 0:ncol], winS[cg, :, :, 0:ncol], R=[b_win[cg]], W=[b_wt[s]])
                        if cg in (6, 7, 8, 9):
                            yield 'need_raw'
                            for cc in range(4):
                                yield None
                                ci = (cg - 6) * 4 + cc
                                bk = next_bank()
                                for kc in range(16):
                                    I(pe, lambda kc=kc, cc=cc, s=s, bk=bk: nc.tensor.matmul(psF[bk][:], wt[s][:, kc, cc * 128:(cc + 1) * 128], hT[:, kc, :], start=(kc == 0), stop=(kc == 15)),
                                      R=[b_wt[s], b_hT], W=[bF[bk]], sig=(kc == 15))
                                    if kc % 4 == 3 and kc != 15:
                                        yield None
                                I(act, lambda ci=ci, bk=bk: nc.scalar.copy(out=raw[:, ci, 3:515], in_=psF[bk][:]), W=[b_raw, bF[bk]])
                                if G == cfg.ngrp - 1:
                                    I(dve, lambda ci=ci, bk=bk: nc.vector.tensor_copy(out=rawl[:, ci, :], in_=psF[bk][:, 509:512]), W=[b_rawl, bF[bk]])
                            continue
                        blocks = (1, 3) if cg in (0, 1, 12, 13) else (0, 1, 2, 3)
                        for j in blocks:
                            yield None
                            blk = 4 * G + j
                            bk = next_bank()
                            for kc in range(16):
                                I(pe, lambda kc=kc, j=j, s=s, bk=bk, ncol=ncol: nc.tensor.matmul(psF[bk][:, 0:ncol], hT[:, kc, j * 128:(j + 1) * 128], wt[s][:, kc, 0:ncol], start=(kc == 0), stop=(kc == 15)),
                                  R=[b_wt[s], b_hT], W=[bF[bk]], sig=(kc == 15))
                                if kc % 4 == 3 and kc != 15:
                                    yield None
                            if cg in (2, 3):
                                h0 = (cg - 2) * 4
                                qknorm(128, bk, gkB, lambda h: kst[:, h, :], junkq, b_junkq, ssq4, rs4, b_s4, b_kst)
                                I(act, lambda: nc.scalar.copy(out=knb[:], in_=kst[:].rearrange("p h d -> p (h d)")), R=[b_kst], W=[b_knb])
                                for h in range(4):
                                    I(pe, lambda h=h: nc.tensor.transpose(psT[0][:, h * 128:(h + 1) * 128], knb[:, h * 128:(h + 1) * 128], identB[:]), R=[b_knb, b_cb], W=[bT[0]], sig=(h == 3))
                                I(dve, lambda h0=h0, j=j: nc.vector.tensor_copy(out=kTst[:, h0:h0 + 4, j * 128:(j + 1) * 128], in_=psT[0][:, 0:512].rearrange("p (h t) -> p h t", h=4)),
                                  W=[b_kTst, bT[0]])
                                dma(act, k_o.rearrange("h t d -> t h d")[blk * 128:(blk + 1) * 128, h0:h0 + 4, :], kst[:], R=[b_kst])
                            elif cg in (4, 5):
                                h0 = (cg - 4) * 4
                                I(act, lambda bk=bk: nc.scalar.copy(out=vst[:].rearrange("p h d -> p (h d)"), in_=psF[bk][:]), W=[b_vst, bF[bk]])
                                I(dve, lambda h0=h0, j=j: nc.vector.tensor_copy(out=vbst[:, h0:h0 + 4, j, :], in_=vst[:]), R=[b_vst], W=[b_vbst])
                                dma(act, v_o.rearrange("h t d -> t h d")[blk * 128:(blk + 1) * 128, h0:h0 + 4, :], vst[:], R=[b_vst])
                            elif cg in (10, 11):
                                h0 = (cg - 10) * 2
                                I(act, lambda h0=h0, j=j, bk=bk: nc.scalar.copy(out=vBt2[G % 2][:, j, h0:h0 + 2, :].rearrange("p h v -> p (h v)"), in_=psF[bk][:]), W=[b_vBt2[G % 2], bF[bk]])
                            elif cg == 14:
                                I(dve, lambda j=j, bk=bk: nc.vector.tensor_copy(out=gts2[G % 2][:, j, :], in_=psF[bk][:, 0:8]), W=[b_gts2[G % 2], bF[bk]])
                            elif cg in (0, 1):
                                h0 = cg * 4
                                jo = j // 2
                                qknorm(128, bk, gqB, lambda h: knb[:, h * 128:(h + 1) * 128], junkq, b_junkq, ssq4, rs4, b_s4, b_knb)
                                for h in range(4):
                                    I(pe, lambda h=h: nc.tensor.transpose(psT[0][:, h * 128:(h + 1) * 128], knb[:, h * 128:(h + 1) * 128], identB[:]), R=[b_knb, b_cb], W=[bT[0]], sig=(h == 3))
                                I(dve, lambda h0=h0, jo=jo: nc.vector.tensor_copy(out=qTst[:, h0:h0 + 4, jo * 128:(jo + 1) * 128], in_=psT[0][:, 0:512].rearrange("p (h t) -> p h t", h=4)),
                                  W=[b_qTst, bT[0]])
                            elif cg in (12, 13):
                                jo = j // 2
                                c0 = (cg - 12) * 512
                                I(act, lambda bk=bk: nc.scalar.activation(out=sg32[:], in_=psF[bk][:], func=AF.Sigmoid), W=[b_sg32, bF[bk]])
                                I(dve, lambda jo=jo, c0=c0: nc.vector.tensor_tensor(out=sgg2[G % 2][:, jo, c0:c0 + 512], in0=sg32[:], in1=ghB[:, c0:c0 + 512], op=ALU.mult), R=[b_sg32, b_small], W=[b_sgg2[G % 2]])
                    dma(act, KTs.rearrange("h d t -> d h t")[:, :, G * 512:(G + 1) * 512], kTst[:], R=[b_kTst])
                    dma(act, Vss.rearrange("h p b d -> p h b d")[:, :, 4 * G:4 * G + 4, :], vbst[:], R=[b_vbst])
                    dma(act, QTs.rearrange("h d t -> d h t")[:, :, G * 256:(G + 1) * 256], qTst[:], R=[b_qTst])

                def genB(G):
                    for ci in range(16):
                        yield None
                        yb = cvy[ci % 2]; b_y = b_cvy[ci % 2]
                        I(dve, lambda ci=ci, yb=yb: nc.vector.tensor_scalar(out=yb[:], in0=raw[:, ci, 0:512], scalar1=wcT[:, ci, 0:1], scalar2=None, op0=ALU.mult), R=[b_raw, b_small], W=[b_y])
                        for jt in range(1, 4):
                            I(dve, lambda ci=ci, yb=yb, jt=jt: nc.vector.scalar_tensor_tensor(out=yb[:], in0=raw[:, ci, jt:jt + 512], scalar=wcT[:, ci, jt:jt + 1], in1=yb[:], op0=ALU.mult, op1=ALU.add),
                              R=[b_raw, b_small], W=[b_y])
                        if ci < 8:
                            I(act, lambda ci=ci, yb=yb: nc.scalar.activation(out=qTt2[G % 2][:, ci, :, :], in_=yb[:].rearrange("p (j t) -> p j t", j=4)[:, 1::2, :], func=AF.Silu, bias=bcT[:, ci:ci + 1]),
                              R=[b_y, b_small], W=[b_qTt2[G % 2]])
                        else:
                            I(act, lambda ci=ci, yb=yb: nc.scalar.activation(out=kTt2[G % 2][:, ci - 8, :], in_=yb[:], func=AF.Silu, bias=bcT[:, ci:ci + 1]),
                              R=[b_y, b_small], W=[b_kTt2[G % 2]])
                    I(dve, lambda: nc.vector.tensor_copy(out=raw[:, :, 0:3], in_=raw[:, :, 512:515]), W=[b_raw])
                    yield 'conv_done'
                    for j in range(4):
                        blk = 4 * G + j
                        own = (j % 2 == 1)
                        jo = j // 2
                        for _ in mlstm_block(M, St, own,
                                    lambda h, half, j=j: qTt2[G % 2][:, 2 * h + half, j // 2, :], b_qTt2[G % 2],
                                    lambda h, half, j=j: kTt2[G % 2][:, 2 * h + half, j * 128:(j + 1) * 128], b_kTt2[G % 2],
                                    vBt2[G % 2][:, j, :, :], b_vBt2[G % 2], gts2[G % 2][:, j, :], b_gts2[G % 2],
                                    sgg2[G % 2][:, jo, :], b_sgg2[G % 2], obt, b_obt, dummy=(blk == 0), sel="sel127"):
                            yield None
                        yield None
                        if own:
                            for c in range(8):
                                I(pe, lambda c=c: nc.tensor.transpose(psT[1][:, c * 128:(c + 1) * 128], obt[:, c * 128:(c + 1) * 128], identB[:]), R=[b_obt, b_cb], W=[bT[1]], sig=(c == 7))
                            I(act, lambda jo=jo: nc.scalar.copy(out=oTst[:, :, jo * 128:(jo + 1) * 128], in_=psT[1][:, :].rearrange("p (c t) -> p c t", c=8)), W=[b_oTst, bT[1]])
                    dma(act, OTs.rearrange("c d t -> d c t")[:, 8:16, G * 256:(G + 1) * 256], oTst[:], R=[b_oTst])

                def drive(gA, gB):
                    a_done = gA is None
                    b_done = gB is None
                    conv_done = b_done
                    while not (a_done and b_done):
                        for _rep in range(3):
                            if a_done:
                                break
                            try:
                                tag = next(gA)
                            except StopIteration:
                                a_done = True
                                tag = None
                            if tag == 'need_raw':
                                while not conv_done and not b_done:
                                    try:
                                        tb_ = next(gB)
                                    except StopIteration:
                                        b_done = True
                                        break
                                    if tb_ == 'conv_done':
                                        conv_done = True
                        if not b_done:
                            try:
                                tb_ = next(gB)
                            except StopIteration:
                                b_done = True
                                tb_ = None
                            if tb_ == 'conv_done':
                                conv_done = True

                drive(genA(0), None)
                for G in range(cfg.ngrp):
                    drive(genA(G + 1) if G + 1 < cfg.ngrp else None, genB(G))
                for a_ in range(2):
                    dma(act, C_o.rearrange("h (a p) v -> a p h v", p=128)[a_], St.C32[:, a_, :, :], R=[St.b_C32])
                with nc.allow_non_contiguous_dma(reason="tiny state vectors"):
                    for a_ in range(2):
                        dma(act, n_o.rearrange("h (a p) -> a p h", p=128)[a_], St.n32[:, a_, :], R=[St.b_n32])
                    dma(act, m_o, St.mB[0:1, :], R=[St.b_m])
                    for r_ in range(3):
                        dma(act, conv_o[r_].rearrange("(c p) -> p c", p=128), rawl[:, :, r_], R=[b_rawl])
                K.barrier(scr)

        def mk_attn(stack, NQ):
            At = type("At", (), {})()

            def t(name, shape, dt, nslot):
                setattr(At, name, [K.sb(stack, "a_%s%d" % (name, i), shape, dt) for i in range(nslot)])
                setattr(At, "b_" + name, [Buf() for _ in range(nslot)])
            t("E", (128, NQ), F32, 4); t("SP", (128, NQ), BF16, 3); t("X", (128, NQ), F32, 2); t("A", (128, NQ), BF16, 3)
            t("ACC", (128, NQ), F32, 2); t("ACCb", (128, NQ), BF16, 3)
            At.zero = K.sb(stack, "a_zero", (128, max(NQ, 128)), BF16); At.b_zero = Buf()
            I(dve, lambda: nc.vector.memset(At.zero[:], 0.0), W=[At.b_zero])
            At.NQ = NQ
            return At

        def attn_head(At, Kslice, b_K, Vslice, b_V, Qt, b_Q, kb_list, zb, tb, ob, extraR=()):
            NQ = At.NQ
            nk = len(kb_list)
            I(pe, lambda: nc.tensor.matmul(psF[ob][:, 0:NQ], At.zero[:, 0:128], At.zero[:, 0:NQ], start=True, stop=False), R=[At.b_zero], W=[bF[ob]])
            for s_ in range(3):
                I(dve, lambda: nc.vector.memset(At.ACCb[s_][:], 0.0), W=[At.b_ACCb[s_]])

            def rng(i):
                kb, nkeys, q0, diag, kbias = kb_list[i]
                return kb, nkeys, q0, diag, kbias

            def stage1a(i):
                kb, nkeys, q0, diag, kbias = rng(i)
                E, SP = At.E[i % 4], At.SP[i % 3]
                bE, bSP = At.b_E[i % 4], At.b_SP[i % 3]
                I(pe, lambda: nc.tensor.matmul(psF[zb][0:nkeys, q0:NQ], Kslice(kb), Qt[:, q0:NQ], start=True, stop=True), R=[b_K, b_Q] + list(extraR), W=[bF[zb]])
                if kbias:
                    I(act, lambda: nc.scalar.activation(out=E[0:nkeys, q0:NQ], in_=psF[zb][0:nkeys, q0:NQ], func=AF.Exp, scale=SCALE_A, bias=cmt[0:nkeys, 1:2]), R=[b_cm], W=[bE, bF[zb]])
                else:
                    I(act, lambda: nc.scalar.activation(out=E[0:nkeys, q0:NQ], in_=psF[zb][0:nkeys, q0:NQ], func=AF.Exp, scale=SCALE_A), W=[bE, bF[zb]])
                if diag:
                    dq = min(128, NQ - q0)
                    I(dve, lambda: nc.vector.tensor_tensor(out=E[0:nkeys, q0:q0 + dq], in0=E[0:nkeys, q0:q0 + dq], in1=cF["stri01"][0:nkeys, 0:dq], op=ALU.mult), R=[b_cst], W=[bE])

            def stage1b(i):
                kb, nkeys, q0, diag, kbias = rng(i)
                E, SP = At.E[i % 4], At.SP[i % 3]
                bE, bSP = At.b_E[i % 4], At.b_SP[i % 3]
                I(act, lambda: nc.scalar.activation(out=SP[0:nkeys, q0:NQ], in_=E[0:nkeys, q0:NQ], func=AF.Ln, bias=1.0), R=[bE], W=[bSP])
                if i < nk - 1:
                    I(dve, lambda: nc.vector.tensor_tensor(out=At.ACCb[(i + 1) % 3][0:nkeys, q0:NQ], in0=At.ACCb[i % 3][0:nkeys, q0:NQ], in1=SP[0:nkeys, q0:NQ], op=ALU.add),
                      R=[bSP, At.b_ACCb[i % 3]], W=[At.b_ACCb[(i + 1) % 3]])

            def stage2(i):
                kb, nkeys, q0, diag, kbias = rng(i)
                E, SP, X, A = At.E[i % 4], At.SP[i % 3], At.X[i % 2], At.A[i % 3]
                bE, bSP, bX, bA = At.b_E[i % 4], At.b_SP[i % 3], At.b_X[i % 2], At.b_A[i % 3]
                I(pe, lambda: nc.tensor.matmul(psF[tb][0:nkeys, q0:NQ], trigeB[0:nkeys, 0:nkeys], SP[0:nkeys, q0:NQ], start=True, stop=(i == 0)), R=[bSP, b_cb], W=[bF[tb]], sig=(i == 0))
                if i > 0:
                    I(pe, lambda: nc.tensor.matmul(psF[tb][0:nkeys, q0:NQ], onesB[:, 0:nkeys], At.ACCb[i % 3][:, q0:NQ], start=False, stop=True), R=[At.b_ACCb[i % 3], b_cb], W=[bF[tb]])
                I(act, lambda: nc.scalar.activation(out=X[0:nkeys, q0:NQ], in_=psF[tb][0:nkeys, q0:NQ], func=AF.Exp, scale=-1.0), W=[bX, bF[tb]])
                I(dve, lambda: nc.vector.tensor_tensor(out=A[0:nkeys, q0:NQ], in0=X[0:nkeys, q0:NQ], in1=E[0:nkeys, q0:NQ], op=ALU.mult), R=[bX, bE], W=[bA])

            def stage3(i):
                kb, nkeys, q0, diag, kbias = rng(i)
                A, bA = At.A[i % 3], At.b_A[i % 3]
                I(pe, lambda: nc.tensor.matmul(psF[ob][:, q0:NQ], Vslice(kb), A[0:nkeys, q0:NQ], start=False, stop=(i == nk - 1)), R=[b_V, bA] + list(extraR), W=[bF[ob]])

            stage1a(0)
            stage1b(0)
            if nk > 1:
                stage1a(1)
            for i in range(nk):
                if i + 1 < nk:
                    stage1b(i + 1)
                if i + 2 < nk:
                    stage1a(i + 2)
                if i >= 2:
                    stage3(i - 2)
                stage2(i)
            if nk >= 2:
                stage3(nk - 2)
            stage3(nk - 1)

        def stage_B():
            with ExitStack() as SB:
                NQ = 512
                At = mk_attn(SB, NQ)
                Kt = [K.sb(SB, "Kt%d" % i, (128, T), BF16) for i in range(2)]; b_Kt = [Buf(), Buf()]
                Vt = [K.sb(SB, "Vt%d" % i, (128, NB, 128), BF16) for i in range(2)]; b_Vt = [Buf(), Buf()]
                Qt = [K.sb(SB, "Qt%d" % i, (128, NQ), BF16) for i in range(2)]; b_Qt = [Buf(), Buf()]
                oTa = [K.sb(SB, "oTa%d" % i, (128, NQ), BF16) for i in range(2)]; b_oTa = [Buf(), Buf()]
                it = 0
                for g in range(cfg.nown // 4):
                    nkb = 8 * g + 8
                    kb_list = []
                    for kb in reversed(range(nkb)):
                        r = kb - 8 * g
                        if r < 1:
                            q0, diag = 0, False
                        elif r % 2 == 1:
                            q0, diag = (r // 2) * 128, True
                        else:
                            q0, diag = (r // 2) * 128, False
                        kb_list.append((kb, 128, q0, diag, kb == 0))
                    for h in range(8):
                        s = it % 2
                        it += 1
                        dma(sp, Kt[s][:, 0:nkb * 128], KTs[h, :, 0:nkb * 128], W=[b_Kt[s]])
                        dma(sp, Vt[s][:, 0:nkb, :], Vss[h, :, 0:nkb, :], W=[b_Vt[s]])
                        dma(sp, Qt[s][:], QTs[h, :, g * NQ:(g + 1) * NQ], W=[b_Qt[s]])
                        zb, tb, ob = (0, 1, 2) if s == 0 else (3, 4, 5)
                        attn_head(At, lambda kb, s=s: Kt[s][:, kb * 128:(kb + 1) * 128], b_Kt[s],
                                  lambda kb, s=s: Vt[s][:, kb, :], b_Vt[s], Qt[s], b_Qt[s], kb_list, zb, tb, ob)
                        I(act, lambda s=s, ob=ob: nc.scalar.copy(out=oTa[s][:], in_=psF[ob][:, 0:NQ]), W=[b_oTa[s], bF[ob]])
                        dma(sp, OTs[h, :, g * NQ:(g + 1) * NQ], oTa[s][:], R=[b_oTa[s]])
                K.barrier(scr)

        def ffn_tail(stack, ntok_blocks, n, x1, b_x1, h2T, b_h2T, gt2Bt, b_gt2, y_dst, ts, tmp):
            NT = ntok_blocks * n if ntok_blocks > 1 else n
            actT, b_actT, wt, b_wt, yst, b_yst = tmp
            wi_ = [0]
            for cg in range(16):
                s = wi_[0] % 2; wi_[0] += 1
                dma(sp, wt[s][:], wff1S[cg], R=[b_wff1[cg]], W=[b_wt[s]])
                for cc in range(4):
                    hc = cg * 4 + cc
                    bk = next_bank()
                    for kc in range(16):
                        I(pe, lambda kc=kc, cc=cc, s=s, bk=bk: nc.tensor.matmul(psF[bk][:, 0:NT], wt[s][:, kc, cc * 128:(cc + 1) * 128], h2T[:, kc, 0:NT], start=(kc == 0), stop=(kc == 15)),
                          R=[b_wt[s], b_h2T], W=[bF[bk]], sig=(kc == 15))
                    I(act, lambda hc=hc, bk=bk: nc.scalar.activation(out=actT[:, hc, 0:NT], in_=psF[bk][:, 0:NT], func=AF.Relu), W=[b_actT, bF[bk]])
                    I(pool, lambda hc=hc: nc.gpsimd.tensor_tensor(out=actT[:, hc, 0:NT], in0=actT[:, hc, 0:NT], in1=actT[:, hc, 0:NT], op=ALU.mult), W=[b_actT])
            banks = [0, 1, 2, 3]
            for cg in range(4):
                for kq in range(4):
                    s = wi_[0] % 2; wi_[0] += 1
                    dma(sp, wt[s][:], wff2S[cg, kq], R=[b_wff2[cg][kq]], W=[b_wt[s]])
                    for j in range(ntok_blocks):
                        for kc in range(16):
                            hc = kq * 16 + kc
                            I(pe, lambda kc=kc, hc=hc, j=j, s=s: nc.tensor.matmul(psF[banks[j]][0:n, :], actT[:, hc, j * n:(j + 1) * n], wt[s][:, kc, :], start=(hc == 0), stop=(hc == 63)),
                              R=[b_wt[s], b_actT], W=[bF[banks[j]]], sig=(kc == 15))
                for j in range(ntok_blocks):
                    I(dve, lambda j=j, cg=cg: nc.vector.tensor_tensor(out=yst[0:n, :], in0=psF[banks[j]][0:n, :], in1=gt2Bt(j)[0:n, cg * 512:(cg + 1) * 512], op=ALU.mult),
                      R=[b_gt2], W=[b_yst, bF[banks[j]]])
                    I(dve, lambda j=j, cg=cg: nc.vector.tensor_tensor(out=x1[0:n, j, cg * 512:(cg + 1) * 512], in0=x1[0:n, j, cg * 512:(cg + 1) * 512], in1=yst[0:n, :], op=ALU.add),
                      R=[b_yst], W=[b_x1])
            for j in range(ntok_blocks):
                dma(act, y_dst(j), x1[0:n, j, :], R=[b_x1])

        def out_proj(ntok_blocks, n, oT_j, b_oT, x1, b_x1, gt1Bt, b_gt1, wt, b_wt, yst, b_yst):
            for cg in range(4):
                s = cg % 2
                dma(sp, wt[s][:], woutS[cg], R=[b_wout[cg]], W=[b_wt[s]])
                for j in range(ntok_blocks):
                    bk = next_bank()
                    for kc in range(16):
                        I(pe, lambda kc=kc, j=j, s=s, bk=bk: nc.tensor.matmul(psF[bk][0:n, :], oT_j(j, kc), wt[s][:, kc, :], start=(kc == 0), stop=(kc == 15)),
                          R=[b_oT, b_wt[s]], W=[bF[bk]], sig=(kc == 15))
                    I(dve, lambda cg=cg, bk=bk, j=j: nc.vector.tensor_tensor(out=yst[0:n, :], in0=psF[bk][0:n, :], in1=gt1Bt(j)[0:n, cg * 512:(cg + 1) * 512], op=ALU.mult),
                      R=[b_gt1], W=[b_yst, bF[bk]])
                    I(dve, lambda cg=cg, j=j: nc.vector.tensor_tensor(out=x1[0:n, j, cg * 512:(cg + 1) * 512], in0=x1[0:n, j, cg * 512:(cg + 1) * 512], in1=yst[0:n, :], op=ALU.add),
                      R=[b_yst], W=[b_x1])

        def stage_C():
            with ExitStack() as SC:
                gt1B = K.sb(SC, "gt1B", (128, D), F32); gt2B = K.sb(SC, "gt2B", (128, D), F32); b_gtB = Buf()
                with ExitStack() as S1:
                    row_bcast(S1, gt1B, modT[:, 32:48, 0], b_gtB)
                    row_bcast(S1, gt2B, modT[:, 80:96, 0], b_gtB)
                    K.barrier(scr)
                oT = K.sb(SC, "oT", (128, 16, 512), BF16); b_oT = Buf()
                x1 = K.sb(SC, "x1", (128, 4, D), F32); b_x1 = Buf()
                xnb = K.sb(SC, "xnbC", (128, D), BF16); b_xnb = Buf()
                ssx = K.sb(SC, "ssxC", (128, 1), F32); rsx = K.sb(SC, "rsxC", (128, 1), F32); b_sx = Buf()
                h2T = K.sb(SC, "h2T", (128, 16, 512), BF16); b_h2T = Buf()
                actT = K.sb(SC, "actT", (128, 64, 512), BF16); b_actT = Buf()
                wt = [K.sb(SC, "wtC%d" % i, (128, 16, 512), BF16) for i in range(2)]; b_wt = [Buf(), Buf()]
                yst = K.sb(SC, "yst", (128, 512), F32); b_yst = Buf()
                for g in range(cfg.nown // 4):
                    dma(sp, oT[:], OTs.rearrange("c d t -> d c t")[:, :, g * 512:(g + 1) * 512], W=[b_oT])
                    for j in range(4):
                        blk = 2 * (4 * g + j) + 1
                        dma(sp, x1[:, j, :], xs[blk * 128:(blk + 1) * 128, :], W=[b_x1])
                    out_proj(4, 128, lambda j, kc: oT[:, kc, j * 128:(j + 1) * 128], b_oT, x1, b_x1, lambda j: gt1B, b_gtB, wt, b_wt, yst, b_yst)
                    for j in range(4):
                        norm_transpose(128, x1[:, j, :], b_x1, xnb, b_xnb, ssx, rsx, b_sx,
                                       lambda c, j=j: h2T[:, c, j * 128:(j + 1) * 128], b_h2T, A2, modT[:, 48:64, :], 0)
                    ffn_tail(SC, 4, 128, x1, b_x1, h2T, b_h2T, lambda j: gt2B, b_gtB,
                             lambda j, g=g: y_o[(4 * g + j) * 128:(4 * g + j + 1) * 128, :], 0, (actT, b_actT, wt, b_wt, yst, b_yst))
                K.barrier(scr)

        def stage_S():
            n = TS
            nkb = PAST // 128
            with ExitStack() as SS:
                wt = [K.sb(SS, "wtS%d" % i, (128, 16, 512), BF16) for i in range(2)]; b_wt = [Buf(), Buf()]
                xinS = K.sb(SS, "xinS", (n, NS, D), F32); b_xinS = Buf()
                xnb = K.sb(SS, "xnbS", (n, D), BF16); b_xnb = Buf()
                ssx = K.sb(SS, "ssxS", (n, 1), F32); rsx = K.sb(SS, "rsxS", (n, 1), F32); b_sx = Buf()
                hTS = K.sb(SS, "hTS", (128, NS, 16, n), BF16); b_hTS = Buf()
                kstS = K.sb(SS, "kstS", (n, 4, 128), F32); b_kstS = Buf()
                vstS = K.sb(SS, "vstS", (n, 4, 128), F32); b_vstS = Buf()
                knbS = K.sb(SS, "knbS", (n, 512), BF16); b_knbS = Buf()
                junkq = K.sb(SS, "junkqS", (n, 128), F32); b_junkq = Buf()
                ssq4 = K.sb(SS, "ssq4S", (n, 4), F32); rs4 = K.sb(SS, "rs4S", (n, 4), F32); b_s4 = Buf()
                kTn = K.sb(SS, "kTn", (128, NS, 8, n), BF16); b_kTn = Buf()
                qTn = K.sb(SS, "qTn", (128, NS, 8, n), BF16); b_qTn = Buf()
                vnw = K.sb(SS, "vnw", (n, NS, 8, 128), BF16); b_vnw = Buf()
                rawS = K.sb(SS, "rawS", (128, NS, 16, 3 + n), BF16); b_rawS = Buf()
                rawH = K.sb(SS, "rawH", (128, NS, 16, 3), F32); b_rawH = Buf()
                rawlS = K.sb(SS, "rawlS", (128, NS, 16, 3), F32); b_rawlS = Buf()
                cvy = K.sb(SS, "cvyS", (128, n), F32); b_cvy = Buf()
                qTtS = K.sb(SS, "qTtS", (128, NS, 8, n), BF16); b_qTtS = Buf()
                kTtS = K.sb(SS, "kTtS", (128, NS, 8, n), BF16); b_kTtS = Buf()
                vBS = K.sb(SS, "vBS", (n, NS, 4, 256), BF16); b_vBS = Buf()
                gtsS = K.sb(SS, "gtsS", (n, NS, 8), F32); b_gtsS = Buf()
                sg32 = K.sb(SS, "sg32S", (n, 512), F32); b_sg32 = Buf()
                sggS = K.sb(SS, "sggS", (n, NS, 1024), BF16); b_sggS = Buf()
                obtS = K.sb(SS, "obtS", (n, 1024), BF16); b_obtS = Buf()
                oTS = K.sb(SS, "oTS", (128, NS, 16, n), BF16); b_oTS = Buf()
                M16 = mk_mlstm(SS, n)
                St = mk_state(SS)
                dma(sp, xinS[:], xsamp.rearrange("(s t) f -> t s f", t=n), W=[b_xinS])
                with nc.allow_non_contiguous_dma(reason="conv state transposed load (tiny)"):
                    for sb_ in range(NS):
                        for r_ in range(3):
                            dma(sp, rawH[:, sb_, :, r_], sconv[sb_, r_].rearrange("(c p) -> p c", p=128), W=[b_rawH])
                I(dve, lambda: nc.vector.tensor_copy(out=rawS[:, :, :, 0:3], in_=rawH[:]), R=[b_rawH], W=[b_rawS])
                for sb_ in range(NS):
                    norm_transpose(n, xinS[:, sb_, :], b_xinS, xnb, b_xnb, ssx, rsx, b_sx,
                                   lambda c, sb_=sb_: hTS[:, sb_, c, :], b_hTS, A1, modT, 1 + sb_)
                ckpt('S1')
                wsl = [0]
                for cg in A_ORDER:
                    s = wsl[0] % 2
                    wsl[0] += 1
                    ncol = 512 if cg < 14 else 8
                    dma(sp, wt[s][:, :, 0:ncol], winS[cg, :, :, 0:ncol], R=[b_win[cg]], W=[b_wt[s]])
                    for sb_ in range(NS):
                        if cg in (6, 7, 8, 9):
                            for cc in range(4):
                                ci = (cg - 6) * 4 + cc
                                bk = next_bank()
                                for kc in range(16):
                                    I(pe, lambda: nc.tensor.matmul(psF[bk][:, 0:n], wt[s][:, kc, cc * 128:(cc + 1) * 128], hTS[:, sb_, kc, :], start=(kc == 0), stop=(kc == 15)),
                                      R=[b_wt[s], b_hTS], W=[bF[bk]], sig=(kc == 15))
                                I(act, lambda: nc.scalar.copy(out=rawS[:, sb_, ci, 3:3 + n], in_=psF[bk][:, 0:n]), W=[b_rawS, bF[bk]])
                                I(dve, lambda: nc.vector.tensor_copy(out=rawlS[:, sb_, ci, :], in_=psF[bk][:, n - 3:n]), W=[b_rawlS, bF[bk]])
                            continue
                        bk = next_bank()
                        for kc in range(16):
                            I(pe, lambda: nc.tensor.matmul(psF[bk][0:n, 0:ncol], hTS[:, sb_, kc, :], wt[s][:, kc, 0:ncol], start=(kc == 0), stop=(kc == 15)),
                              R=[b_wt[s], b_hTS], W=[bF[bk]], sig=(kc == 15))
                        if cg in (2, 3):
                            h0 = (cg - 2) * 4
                            qknorm(n, bk, gkB, lambda h: kstS[:, h, :], junkq, b_junkq, ssq4, rs4, b_s4, b_kstS)
                            I(act, lambda: nc.scalar.copy(out=knbS[:], in_=kstS[:].rearrange("p h d -> p (h d)")), R=[b_kstS], W=[b_knbS])
                            for h in range(4):
                                I(pe, lambda: nc.tensor.transpose(psT[1][:, h * 128:h * 128 + n], knbS[:, h * 128:(h + 1) * 128], identB[0:n, 0:n]), R=[b_knbS, b_cb], W=[bT[1]], sig=(h == 3))
                            I(dve, lambda: nc.vector.tensor_copy(out=kTn[:, sb_, h0:h0 + 4, :], in_=psT[1][:, 0:512].rearrange("p (h t) -> p h t", h=4)[:, :, 0:n]), W=[b_kTn, bT[1]])
                            dma(act, ks_o[sb_].rearrange("h t d -> t h d")[:, h0:h0 + 4, :], kstS[:], R=[b_kstS])
                        elif cg in (4, 5):
                            h0 = (cg - 4) * 4
                            I(act, lambda: nc.scalar.copy(out=vstS[:].rearrange("p h d -> p (h d)"), in_=psF[bk][0:n, :]), W=[b_vstS, bF[bk]])
                            I(dve, lambda: nc.vector.tensor_copy(out=vnw[:, sb_, h0:h0 + 4, :], in_=vstS[:]), R=[b_vstS], W=[b_vnw])
                            dma(act, vs_o[sb_].rearrange("h t d -> t h d")[:, h0:h0 + 4, :], vstS[:], R=[b_vstS])
                        elif cg in (10, 11):
                            h0 = (cg - 10) * 2
                            I(act, lambda: nc.scalar.copy(out=vBS[:, sb_, h0:h0 + 2, :].rearrange("p h v -> p (h v)"), in_=psF[bk][0:n, :]), W=[b_vBS, bF[bk]])
                        elif cg == 14:
                            I(dve, lambda: nc.vector.tensor_copy(out=gtsS[:, sb_, :], in_=psF[bk][0:n, 0:8]), W=[b_gtsS, bF[bk]])
                        elif cg in (0, 1):
                            h0 = cg * 4
                            qknorm(n, bk, gqB, lambda h: knbS[:, h * 128:(h + 1) * 128], junkq, b_junkq, ssq4, rs4, b_s4, b_knbS)
                            for h in range(4):
                                I(pe, lambda: nc.tensor.transpose(psT[1][:, h * 128:h * 128 + n], knbS[:, h * 128:(h + 1) * 128], identB[0:n, 0:n]), R=[b_knbS, b_cb], W=[bT[1]], sig=(h == 3))
                            I(dve, lambda: nc.vector.tensor_copy(out=qTn[:, sb_, h0:h0 + 4, :], in_=psT[1][:, 0:512].rearrange("p (h t) -> p h t", h=4)[:, :, 0:n]), W=[b_qTn, bT[1]])
                        elif cg in (12, 13):
                            c0 = (cg - 12) * 512
                            I(act, lambda: nc.scalar.activation(out=sg32[:], in_=psF[bk][0:n, :], func=AF.Sigmoid), W=[b_sg32, bF[bk]])
                            I(dve, lambda: nc.vector.tensor_tensor(out=sggS[:, sb_, c0:c0 + 512], in0=sg32[:], in1=ghB[0:n, c0:c0 + 512], op=ALU.mult), R=[b_sg32, b_small], W=[b_sggS])
                ckpt('S2')
                with nc.allow_non_contiguous_dma(reason="conv state transposed store (tiny)"):
                    for sb_ in range(NS):
                        for r_ in range(3):
                            dma(act, convs_o[sb_, r_].rearrange("(c p) -> p c", p=128), rawlS[:, sb_, :, r_], R=[b_rawlS])
                for sb_ in range(NS):
                    for ci in range(16):
                        I(dve, lambda: nc.vector.tensor_scalar(out=cvy[:], in0=rawS[:, sb_, ci, 0:n], scalar1=wcT[:, ci, 0:1], scalar2=None, op0=ALU.mult), R=[b_rawS, b_small], W=[b_cvy])
                        for jt in range(1, 4):
                            I(dve, lambda: nc.vector.scalar_tensor_tensor(out=cvy[:], in0=rawS[:, sb_, ci, jt:jt + n], scalar=wcT[:, ci, jt:jt + 1], in1=cvy[:], op0=ALU.mult, op1=ALU.add),
                              R=[b_rawS, b_small], W=[b_cvy])
                        dst = qTtS[:, sb_, ci, :] if ci < 8 else kTtS[:, sb_, ci - 8, :]
                        I(act, lambda: nc.scalar.activation(out=dst, in_=cvy[:], func=AF.Silu, bias=bcT[:, ci:ci + 1]), R=[b_cvy, b_small], W=[b_qTtS if ci < 8 else b_kTtS])
                ckpt('S3')
                for sb_ in range(NS):
                    for a_ in range(2):
                        dma(sp, St.C32[:, a_, :, :], sC[sb_].rearrange("h (a p) v -> a p h v", p=128)[a_], W=[St.b_C32])
                    with nc.allow_non_contiguous_dma(reason="tiny state vectors"):
                        for a_ in range(2):
                            dma(sp, St.n32[:, a_, :], sn[sb_].rearrange("h (a p) -> a p h", p=128)[a_], W=[St.b_n32])
                        dma(sp, St.mB[:], sm[sb_:sb_ + 1, :].partition_broadcast(128).rearrange("p a d -> p (a d)"), W=[St.b_m])
                    state_refresh_bf(St)
                    for _ in mlstm_block(M16, St, True,
                                lambda h, half: qTtS[:, sb_, 2 * h + half, :], b_qTtS,
                                lambda h, half: kTtS[:, sb_, 2 * h + half, :], b_kTtS,
                                vBS[:, sb_, :, :], b_vBS, gtsS[:, sb_, :], b_gtsS,
                                sggS[:, sb_, :], b_sggS, obtS, b_obtS, dummy=False, sel="sel15"):
                        pass
                    for c in range(8):
                        I(pe, lambda: nc.tensor.transpose(psT[1][:, c * 128:c * 128 + n], obtS[:, c * 128:(c + 1) * 128], identB[0:n, 0:n]), R=[b_obtS, b_cb], W=[bT[1]], sig=(c == 7))
                    I(act, lambda: nc.scalar.copy(out=oTS[:, sb_, 8:16, :], in_=psT[1][:, :].rearrange("p (c t) -> p c t", c=8)[:, :, 0:n]), W=[b_oTS, bT[1]])
                    for a_ in range(2):
                        dma(act, Cs_o[sb_].rearrange("h (a p) v -> a p h v", p=128)[a_], St.C32[:, a_, :, :], R=[St.b_C32])
                    with nc.allow_non_contiguous_dma(reason="tiny state vectors"):
                        for a_ in range(2):
                            dma(act, ns_o[sb_].rearrange("h (a p) -> a p h", p=128)[a_], St.n32[:, a_, :], R=[St.b_n32])
                        dma(act, ms_o[sb_:sb_ + 1, :], St.mB[0:1, :], R=[St.b_m])
                ckpt('S4')
                with ExitStack() as SAT:
                    At = mk_attn(SAT, n)
                    Kc = [K.sb(SAT, "Kc%d" % i, (128, nkb, 128), BF16) for i in range(2)]; b_Kc = [Buf(), Buf()]
                    Vc = [K.sb(SAT, "Vc%d" % i, (128, nkb, 128), BF16) for i in range(2)]; b_Vc = [Buf(), Buf()]
                    KcT = [K.sb(SAT, "KcT%d" % i, (128, nkb * 128), BF16) for i in range(2)]; b_KcT = [Buf(), Buf()]
                    it = 0
                    for sb_ in range(NS):
                        for h in range(8):
                            s = it % 2
                            it += 1
                            for b0 in range(0, nkb, 16):
                                b1 = min(nkb, b0 + 16)
                                dma(pool, Kc[s][:, b0:b1, :], ck[sb_, h].rearrange("(b p) d -> p b d", p=128)[:, b0:b1, :], W=[b_Kc[s]])
                                dma(pool, Vc[s][:, b0:b1, :], cv[sb_, h].rearrange("(b p) d -> p b d", p=128)[:, b0:b1, :], W=[b_Vc[s]])
                            for kb in range(nkb):
                                I(pe, lambda: nc.tensor.transpose(psT[s][:, (kb % 8) * 128:(kb % 8 + 1) * 128], Kc[s][:, kb, :], identB[:]), R=[b_Kc[s], b_cb], W=[bT[s]], sig=(kb % 8 == 7 or kb == nkb - 1))
                                if kb % 8 == 7 or kb == nkb - 1:
                                    k0 = (kb // 8) * 8
                                    w = (kb - k0 + 1) * 128
                                    I(dve, lambda: nc.vector.tensor_copy(out=KcT[s][:, k0 * 128:k0 * 128 + w], in_=psT[s][:, 0:w]), W=[b_KcT[s], bT[s]])
                            kb_list = [("new", n, 0, True, False)] + [(kb, 128, 0, False, False) for kb in reversed(range(nkb))]
                            zb, tb, ob = (0, 1, 2) if s == 0 else (3, 4, 5)
                            Ksl = lambda kb: kTn[:, sb_, h, :] if kb == "new" else KcT[s][:, kb * 128:(kb + 1) * 128]
                            Vsl = lambda kb: vnw[:, sb_, h, :] if kb == "new" else Vc[s][:, kb, :]
                            attn_head(At, Ksl, b_KcT[s], Vsl, b_Vc[s], qTn[:, sb_, h, :], b_qTn, kb_list, zb, tb, ob, extraR=[b_kTn, b_vnw])
                            I(act, lambda: nc.scalar.copy(out=oTS[:, sb_, h, :], in_=psF[ob][:, 0:n]), W=[b_oTS, bF[ob]])
                    K.barrier(scr)
                ckpt('S5')
                gtS = [[K.sb(SS, "gtS%d_%d" % (i, j), (128, D), F32) for j in range(NS)] for i in range(2)]; b_gtS = Buf()
                with ExitStack() as S1:
                    for sb_ in range(NS):
                        row_bcast(S1, gtS[0][sb_], modT[:, 32:48, 1 + sb_], b_gtS)
                        row_bcast(S1, gtS[1][sb_], modT[:, 80:96, 1 + sb_], b_gtS)
                    K.barrier(scr)
                ckpt('S6')
                yst = K.sb(SS, "ystS", (n, 512), F32); b_yst = Buf()
                h2TS = K.sb(SS, "h2TS", (128, 16, NS * n), BF16); b_h2TS = Buf()
                actT = K.sb(SS, "actTS", (128, 64, NS * n), BF16); b_actT = Buf()
                out_proj(NS, n, lambda j, kc: oTS[:, j, kc, :], b_oTS, xinS, b_xinS, lambda j: gtS[0][j], b_gtS, wt, b_wt, yst, b_yst)
                ckpt('S7')
                for sb_ in range(NS):
                    norm_transpose(n, xinS[:, sb_, :], b_xinS, xnb, b_xnb, ssx, rsx, b_sx,
                                   lambda c, sb_=sb_: h2TS[:, c, sb_ * n:(sb_ + 1) * n], b_h2TS, A2, modT[:, 48:64, :], 1 + sb_)
                ckpt('S8')
                ffn_tail(SS, NS, n, xinS, b_xinS, h2TS, b_h2TS, lambda j: gtS[1][j], b_gtS,
                         lambda j: ys_o[j * n:(j + 1) * n, :], 0, (actT, b_actT, wt, b_wt, yst, b_yst))
                K.barrier(scr)

        ckpt("setup")
        stage_A()
        ckpt("A")
        stage_B()
        ckpt("B")
        stage_C()
        ckpt("C")
        stage_S()


_PROG_CACHE = {}


def _core_inputs(c, inp, cfg, consts):
    b, p = c // 2, c % 2
    f = lambda a: np.ascontiguousarray(np.asarray(a, dtype=np.float32))
    x = np.asarray(inp["x_prompt"][b], dtype=np.float32)
    if p == 1:
        xs = x
    else:
        xs = np.concatenate([np.zeros((128, D), np.float32), x[:cfg.T - 128]], axis=0)
    cm = np.zeros((128, 2), np.float32)
    if p == 1:
        cm[:, 0] = 1.0
    else:
        cm[:, 1] = NEG
    ns = cfg.ns
    sl = slice(ns * c, ns * c + ns)
    m = {
        "xs": f(xs), "cm": cm,
        "cvec": f(np.concatenate([np.asarray(inp["c_prompt"])[b:b + 1], np.asarray(inp["c_sample"])[sl]], axis=0)),
        "xsamp": f(np.asarray(inp["x_sample"])[sl].reshape(ns * cfg.ts, D)),
        "ck": f(np.asarray(inp["cache_k"])[0, sl]), "cv": f(np.asarray(inp["cache_v"])[0, sl]),
        "sC": f(np.asarray(inp["state_C"])[0, sl]), "sn": f(np.asarray(inp["state_n"])[0, sl]),
        "sm": f(np.asarray(inp["state_m"])[0, sl]), "sconv": f(np.asarray(inp["state_conv"])[0, sl]),
        "consts": consts,
    }
    for k in ("w_ada", "b_ada", "g_norm1", "w_in", "g_q", "g_k", "w_conv", "b_conv", "b_i", "b_f", "g_h",
              "w_out", "g_norm2", "w_ff1", "w_ff2"):
        m[k] = f(np.asarray(inp[k])[0])
    return m


def run_cores(inp, cores):
    B, SEQ, _ = inp["x_prompt"].shape
    DEC_B, TS, _ = inp["x_sample"].shape
    PAST = inp["cache_k"].shape[3]
    cfg = Cfg(nblk=SEQ // 128, past=PAST, ts=TS, ns=DEC_B // 8)
    key = (cfg.nblk, cfg.past, cfg.ts, cfg.ns)
    if key not in _PROG_CACHE:
        _PROG_CACHE[key] = build_program(cfg)
    nc = _PROG_CACHE[key]
    consts = make_consts()
    in_maps = [_core_inputs(c, inp, cfg, consts) for c in cores]
    res = run_bass_kernel_spmd(nc, in_maps, core_ids=list(range(len(cores))))
    return cfg, res.results


def kernel(**inp):
    B, SEQ, _ = inp["x_prompt"].shape
    DEC_B, TS, _ = inp["x_sample"].shape
    cfg, res = run_cores(inp, list(range(8)))
    ns = cfg.ns
    y_prompt = np.zeros((B, SEQ, D), np.float32)
    k_prompt = np.zeros((1, B, 8, SEQ, 128), np.float32)
    v_prompt = np.zeros((1, B, 8, SEQ, 128), np.float32)
    C_prompt = np.zeros((1, B, 4, 256, 256), np.float32)
    n_prompt = np.zeros((1, B, 4, 256), np.float32)
    m_prompt = np.zeros((1, B, 4), np.float32)
    conv_prompt = np.zeros((1, B, 3, 2048), np.float32)
    y_sample = np.zeros((DEC_B, TS, D), np.float32)
    k_sample = np.zeros((1, DEC_B, 8, TS, 128), np.float32)
    v_sample = np.zeros((1, DEC_B, 8, TS, 128), np.float32)
    C_sample = np.zeros((1, DEC_B, 4, 256, 256), np.float32)
    n_sample = np.zeros((1, DEC_B, 4, 256), np.float32)
    m_sample = np.zeros((1, DEC_B, 4), np.float32)
    conv_sample = np.zeros((1, DEC_B, 3, 2048), np.float32)
    for c in range(8):
        b, p = c // 2, c % 2
        r = res[c]
        yb = y_prompt[b].reshape(cfg.nblk, 128, D)
        yo = r["y"].reshape(cfg.nown, 128, D)
        if p == 1:
            yb[1::2] = yo
            k_prompt[0, b] = r["ko"]
            v_prompt[0, b] = r["vo"]
            C_prompt[0, b] = r["Co"]
            n_prompt[0, b] = r["no"]
            m_prompt[0, b] = r["mo"][0]
            conv_prompt[0, b] = r["convo"]
        else:
            yb[0::2] = yo
        sl = slice(ns * c, ns * c + ns)
        y_sample[sl] = r["ys"].reshape(ns, TS, D)
        k_sample[0, sl] = r["kso"]
        v_sample[0, sl] = r["vso"]
        C_sample[0, sl] = r["Cso"]
        n_sample[0, sl] = r["nso"]
        m_sample[0, sl] = r["mso"]
        conv_sample[0, sl] = r["convso"]
    return (y_prompt, y_sample, k_prompt, v_prompt, C_prompt, n_prompt, m_prompt, conv_prompt,
            k_sample, v_sample, C_sample, n_sample, m_sample, conv_sample)
```
